# Optimizing a Trainium2 kernel written in Bass

```python
import jax
import jax.numpy as jnp
from jax import lax
import numpy as np

D_MODEL = 1024
BATCH = 4
SEQ = 4096
DEPTH = 2

GRID_W = 64
CTX_LEN = 256
N_EVEN = (DEPTH + 1) // 2
N_ODD = DEPTH // 2
EPS = 1e-6
ROPE_BASE = 10000.0
CHUNK = 128
Q_BLOCK = 128

RET_HEADS = 8
RET_DK = 64
RET_DV = 64
RET_WIDTH = RET_HEADS * RET_DV
SSD_HEADS = 8
SSD_HEADDIM = 64
SSD_INNER = SSD_HEADS * SSD_HEADDIM
SSD_GROUPS = 2
SSD_STATE = 128
SSD_CONV = 5
CONV_CH = SSD_INNER + 2 * SSD_GROUPS * SSD_STATE
IN0_SPLITS = [RET_WIDTH, 2 * RET_WIDTH, 3 * RET_WIDTH, 4 * RET_WIDTH, 4 * RET_WIDTH + SSD_INNER, 4 * RET_WIDTH + SSD_INNER + CONV_CH]
IN0_WIDTH = 4 * RET_WIDTH + SSD_INNER + CONV_CH + 2 * SSD_HEADS
MIX0_WIDTH = RET_WIDTH + SSD_INNER
MLA_HEADS = 16
QK_NOPE = 64
QK_ROPE = 32
V_HEAD = 64
KV_RANK = 256
Q_RANK = 768
IN1_WIDTH = Q_RANK + KV_RANK + QK_ROPE
MLA_WIDTH = MLA_HEADS * V_HEAD
_FF_CEIL = -(-8 * D_MODEL // 3)
D_FF = -(-_FF_CEIL // 256) * 256

kernel_name = 'hybrid_retention_ssd_mla_prefix_dit'


def rms_norm(x, w=None):
    xf = x.astype(jnp.float32)
    y = xf * lax.rsqrt(jnp.mean(xf * xf, axis=-1, keepdims=True) + EPS)
    if w is not None:
        y = y * w.astype(jnp.float32)
    return y.astype(x.dtype)


def adaln_params(cond, w_ada, b_ada):
    m = jax.nn.silu(cond) @ w_ada + b_ada
    return jnp.split(m[:, None, :], 6, axis=-1)


def modulate(h, shift, scale):
    return h * (1 + scale) + shift


def axial_rope_tables(n_tok, rot_dim):
    rows = n_tok // GRID_W
    row = jnp.repeat(jnp.arange(rows), GRID_W).astype(jnp.float32)
    col = jnp.tile(jnp.arange(GRID_W), rows).astype(jnp.float32)
    n_freq = rot_dim // 4
    inv_freq = ROPE_BASE ** (-jnp.arange(n_freq, dtype=jnp.float32) / n_freq)
    ang = jnp.concatenate([row[:, None] * inv_freq, col[:, None] * inv_freq], axis=-1)
    return jnp.cos(ang), jnp.sin(ang)


def apply_rope(x, cos, sin):
    half = x.shape[-1] // 2
    x1 = x[..., :half].astype(jnp.float32)
    x2 = x[..., half:].astype(jnp.float32)
    c = cos[:, None, :]
    s = sin[:, None, :]
    return jnp.concatenate([x1 * c - x2 * s, x1 * s + x2 * c], axis=-1).astype(x.dtype)


def bhld(t):
    return jnp.transpose(t, (0, 2, 1, 3))


def depthwise_conv_centred(x, w, b):
    k = w.shape[0]
    y = lax.conv_general_dilated(x, w[:, None, :].astype(x.dtype), window_strides=(1,), padding=[(k // 2, k // 2)],
                                 dimension_numbers=('NWC', 'WIO', 'NWC'), feature_group_count=x.shape[-1])
    return y + b


def chunked_decay_scan(q, k, v, log_a, s0):
    b, h, l, dk = q.shape
    dv = v.shape[-1]
    n = l // CHUNK
    qc = q.astype(jnp.float32).reshape(b, h, n, CHUNK, dk)
    kc = k.astype(jnp.float32).reshape(b, h, n, CHUNK, dk)
    vc = v.astype(jnp.float32).reshape(b, h, n, CHUNK, dv)
    cum = jnp.cumsum(log_a.astype(jnp.float32).reshape(b, h, n, CHUNK), axis=-1)
    idx = jnp.arange(CHUNK)
    lower = idx[:, None] >= idx[None, :]
    decay = jnp.exp(jnp.where(lower, cum[..., :, None] - cum[..., None, :], -jnp.inf))
    scores = jnp.einsum('bhnid,bhnjd->bhnij', qc, kc) * decay
    y_intra = jnp.einsum('bhnij,bhnjv->bhniv', scores, vc)
    to_end = jnp.exp(cum[..., -1:] - cum)
    kv_chunk = jnp.einsum('bhnjd,bhnj,bhnjv->bhndv', kc, to_end, vc)
    total = jnp.exp(cum[..., -1])

    def step(s, inp):
        kv_n, tot_n = inp
        return tot_n[..., None, None] * s + kv_n, s

    s_final, s_before = lax.scan(step, s0, (jnp.moveaxis(kv_chunk, 2, 0), jnp.moveaxis(total, 2, 0)))
    s_before = jnp.moveaxis(s_before, 0, 2)
    y_inter = jnp.einsum('bhnid,bhni,bhndv->bhniv', qc, jnp.exp(cum), s_before)
    return (y_intra + y_inter).reshape(b, h, l, dv), s_final


def bidir_prefix_scan(ctx_in, lat_in):
    qc, kc, vfc, vbc, lafc, labc = ctx_in
    ql, kl, vfl, vbl, lafl, labl = lat_in
    b, h, _, dk = qc.shape
    dv = vfc.shape[-1]
    s0 = jnp.zeros((b, h, dk, dv), jnp.float32)
    flip = lambda t: jnp.flip(t, axis=2)
    yc_f, sc_f = chunked_decay_scan(qc, kc, vfc, lafc, s0)
    yc_b, sc_b = chunked_decay_scan(flip(qc), flip(kc), flip(vbc), flip(labc), s0)
    yl_f, _ = chunked_decay_scan(ql, kl, vfl, lafl, sc_f)
    yl_b, _ = chunked_decay_scan(flip(ql), flip(kl), flip(vbl), flip(labl), sc_b)

    def merge(q, k, vb, yf, yb):
        diag = jnp.einsum('bhld,bhld->bhl', q.astype(jnp.float32), k.astype(jnp.float32))[..., None] * vb.astype(jnp.float32)
        return (yf + flip(yb) - diag).astype(vb.dtype)

    return merge(qc, kc, vbc, yc_f, yc_b), merge(ql, kl, vbl, yl_f, yl_b)


def ret_ssd_project(h, w_in, conv_w, conv_b, dt_bias, a_log, decay_logit, rope):
    b, l, _ = h.shape
    q, k, v, g, z, xbc, dt_raw = jnp.split(h @ w_in, IN0_SPLITS, axis=-1)
    q = q.reshape(b, l, RET_HEADS, RET_DK)
    k = k.reshape(b, l, RET_HEADS, RET_DK) * (RET_DK ** -0.5)
    if rope is not None:
        q = apply_rope(q, *rope)
        k = apply_rope(k, *rope)
    v = v.reshape(b, l, RET_HEADS, RET_DV)
    ret_la = jax.nn.log_sigmoid(decay_logit.astype(jnp.float32))
    la_r = [jnp.broadcast_to(ret_la[d][None, :, None], (b, RET_HEADS, l)) for d in range(2)]
    ret_in = (bhld(q), bhld(k), bhld(v), bhld(v), la_r[0], la_r[1])
    xbc = jax.nn.silu(depthwise_conv_centred(xbc, conv_w, conv_b))
    xs, bm, cm = jnp.split(xbc, [SSD_INNER, SSD_INNER + SSD_GROUPS * SSD_STATE], axis=-1)
    xs = xs.reshape(b, l, SSD_HEADS, SSD_HEADDIM)
    rep = SSD_HEADS // SSD_GROUPS
    bm = jnp.repeat(bm.reshape(b, l, SSD_GROUPS, SSD_STATE), rep, axis=2)
    cm = jnp.repeat(cm.reshape(b, l, SSD_GROUPS, SSD_STATE), rep, axis=2)
    dt = jax.nn.softplus(dt_raw.astype(jnp.float32).reshape(b, l, 2, SSD_HEADS) + dt_bias.astype(jnp.float32))
    log_a = dt * (-jnp.exp(a_log.astype(jnp.float32)))
    v_f = xs * dt[:, :, 0, :, None].astype(xs.dtype)
    v_b = xs * dt[:, :, 1, :, None].astype(xs.dtype)
    ssd_in = (bhld(cm), bhld(bm), bhld(v_f), bhld(v_b),
              jnp.transpose(log_a[:, :, 0], (0, 2, 1)), jnp.transpose(log_a[:, :, 1], (0, 2, 1)))
    return ret_in, ssd_in, g, z, xs


def retention_ssd_mixer(a_ctx, a_lat, w_in, conv_w, conv_b, dt_bias, a_log, d_skip, ssd_norm_w,
                        decay_logit, gn_w, w_out, rope, need_ctx):
    pc = ret_ssd_project(a_ctx, w_in, conv_w, conv_b, dt_bias, a_log, decay_logit, None)
    pl = ret_ssd_project(a_lat, w_in, conv_w, conv_b, dt_bias, a_log, decay_logit, rope)
    ret_c, ret_l = bidir_prefix_scan(pc[0], pl[0])
    ssd_c, ssd_l = bidir_prefix_scan(pc[1], pl[1])

    def finish(proj, y_ret, y_ssd):
        _, _, g, z, xs = proj
        b, l = g.shape[:2]
        yr = bhld(y_ret).astype(jnp.float32)
        mu = jnp.mean(yr, axis=-1, keepdims=True)
        var = jnp.mean(jnp.square(yr - mu), axis=-1, keepdims=True)
        yr = ((yr - mu) * lax.rsqrt(var + EPS)).reshape(b, l, RET_WIDTH) * gn_w.astype(jnp.float32)
        yr = yr.astype(g.dtype) * jax.nn.silu(g)
        ys = bhld(y_ssd) + d_skip[:, None] * xs
        ys = rms_norm(ys.reshape(b, l, SSD_INNER) * jax.nn.silu(z), ssd_norm_w)
        return jnp.concatenate([yr, ys], axis=-1) @ w_out

    out_l = finish(pl, ret_l, ssd_l)
    out_c = finish(pc, ret_c, ssd_c) if need_ctx else None
    return out_c, out_l


def mla_project(h, w_in, q_norm_w, w_uq, kv_norm_w, w_ukv, rope):
    b, l, _ = h.shape
    c_q, c_kv, k_pe = jnp.split(h @ w_in, [Q_RANK, Q_RANK + KV_RANK], axis=-1)
    q = (rms_norm(c_q, q_norm_w) @ w_uq).reshape(b, l, MLA_HEADS, QK_NOPE + QK_ROPE)
    q_nope, q_pe = jnp.split(q, [QK_NOPE], axis=-1)
    kv = (rms_norm(c_kv, kv_norm_w) @ w_ukv).reshape(b, l, MLA_HEADS, QK_NOPE + V_HEAD)
    k_nope, v = jnp.split(kv, [QK_NOPE], axis=-1)
    k_pe = k_pe[:, :, None, :]
    if rope is not None:
        q_pe = apply_rope(q_pe, *rope)
        k_pe = apply_rope(k_pe, *rope)
    q = jnp.concatenate([q_nope, q_pe], axis=-1)
    k = jnp.concatenate([k_nope, jnp.broadcast_to(k_pe, (b, l, MLA_HEADS, QK_ROPE))], axis=-1)
    return q, k, v


def softmax_attention(q, k, v):
    s = jnp.einsum('bqhd,bkhd->bhqk', q, k).astype(jnp.float32) * ((QK_NOPE + QK_ROPE) ** -0.5)
    p = jax.nn.softmax(s, axis=-1)
    return jnp.einsum('bhqk,bkhd->bqhd', p.astype(v.dtype), v)


def mla_mixer(a_ctx, a_lat, w_in, q_norm_w, w_uq, kv_norm_w, w_ukv, w_out, rope, need_ctx):
    q_c, k_c, v_c = mla_project(a_ctx, w_in, q_norm_w, w_uq, kv_norm_w, w_ukv, None)
    q_l, k_l, v_l = mla_project(a_lat, w_in, q_norm_w, w_uq, kv_norm_w, w_ukv, rope)
    k_all = jnp.concatenate([k_c, k_l], axis=1)
    v_all = jnp.concatenate([v_c, v_l], axis=1)
    b, l = a_lat.shape[:2]
    nb = l // Q_BLOCK
    q_blocks = jnp.moveaxis(q_l.reshape(b, nb, Q_BLOCK, MLA_HEADS, QK_NOPE + QK_ROPE), 1, 0)
    o_l = lax.map(lambda qb: softmax_attention(qb, k_all, v_all), q_blocks)
    out_l = jnp.moveaxis(o_l, 0, 1).reshape(b, l, MLA_WIDTH) @ w_out
    out_c = None
    if need_ctx:
        out_c = softmax_attention(q_c, k_c, v_c).reshape(b, a_ctx.shape[1], MLA_WIDTH) @ w_out
    return out_c, out_l


def swiglu(h, w_gate, w_up, w_down):
    return (jax.nn.silu(h @ w_gate) * (h @ w_up)) @ w_down


def setup_inputs(seed: int = 0) -> dict:
    key = jax.random.key(seed)
    ks = iter(jax.random.split(key, 40))
    f32 = jnp.float32

    def normal(shape, scale=1.0):
        return scale * jax.random.normal(next(ks), shape, f32)

    def dense(shape):
        return normal(shape, shape[-2] ** -0.5)

    def gain(shape):
        return 1.0 + normal(shape, 0.02)

    gammas = 1.0 - 2.0 ** (-5.0 - np.arange(RET_HEADS))
    base_logit = jnp.asarray(np.log(gammas / (1.0 - gammas)), dtype=f32)
    dt0 = jnp.exp(jax.random.uniform(next(ks), (N_EVEN, 2, SSD_HEADS), f32, np.log(1e-3), np.log(1e-1)))
    return {
        'x': normal((BATCH, SEQ, D_MODEL)),
        'c': normal((BATCH, D_MODEL)),
        'ctx': normal((BATCH, CTX_LEN, D_MODEL)),
        'c_ctx': normal((D_MODEL,)),
        'w_ada': dense((DEPTH, D_MODEL, 6 * D_MODEL)),
        'b_ada': normal((DEPTH, 6 * D_MODEL), 0.02),
        'w_ffn_gate': dense((DEPTH, D_MODEL, D_FF)),
        'w_ffn_up': dense((DEPTH, D_MODEL, D_FF)),
        'w_ffn_down': dense((DEPTH, D_FF, D_MODEL)),
        'ret_ssd_w_in': dense((N_EVEN, D_MODEL, IN0_WIDTH)),
        'ssd_conv_w': dense((N_EVEN, SSD_CONV, CONV_CH)),
        'ssd_conv_b': normal((N_EVEN, CONV_CH), 0.02),
        'ssd_dt_bias': dt0 + jnp.log(-jnp.expm1(-dt0)),
        'ssd_a_log': jnp.log(jax.random.uniform(next(ks), (N_EVEN, 2, SSD_HEADS), f32, 1.0, 16.0)),
        'ssd_d': gain((N_EVEN, SSD_HEADS)),
        'ssd_norm_w': gain((N_EVEN, SSD_INNER)),
        'ret_decay_logit': base_logit + normal((N_EVEN, 2, RET_HEADS), 0.1),
        'ret_gn_w': gain((N_EVEN, RET_WIDTH)),
        'ret_ssd_w_out': dense((N_EVEN, MIX0_WIDTH, D_MODEL)),
        'mla_w_in': dense((N_ODD, D_MODEL, IN1_WIDTH)),
        'mla_q_norm_w': gain((N_ODD, Q_RANK)),
        'mla_w_uq': dense((N_ODD, Q_RANK, MLA_HEADS * (QK_NOPE + QK_ROPE))),
        'mla_kv_norm_w': gain((N_ODD, KV_RANK)),
        'mla_w_ukv': dense((N_ODD, KV_RANK, MLA_HEADS * (QK_NOPE + V_HEAD))),
        'mla_w_out': dense((N_ODD, MLA_WIDTH, D_MODEL)),
        'final_norm_w': gain((D_MODEL,)),
    }


def reference(x, c, ctx, c_ctx, w_ada, b_ada, w_ffn_gate, w_ffn_up, w_ffn_down,
              ret_ssd_w_in, ssd_conv_w, ssd_conv_b, ssd_dt_bias, ssd_a_log, ssd_d, ssd_norm_w,
              ret_decay_logit, ret_gn_w, ret_ssd_w_out,
              mla_w_in, mla_q_norm_w, mla_w_uq, mla_kv_norm_w, mla_w_ukv, mla_w_out, final_norm_w):
    lat_len = x.shape[1]
    rope_ret = axial_rope_tables(lat_len, RET_DK)
    rope_mla = axial_rope_tables(lat_len, QK_ROPE)
    h_ctx, h_lat = ctx, x
    for i in range(DEPTH):
        last = i == DEPTH - 1
        j = i // 2
        m_lat = adaln_params(c, w_ada[i], b_ada[i])
        m_ctx = adaln_params(c_ctx[None], w_ada[i], b_ada[i])
        a_ctx = modulate(rms_norm(h_ctx), m_ctx[0], m_ctx[1])
        a_lat = modulate(rms_norm(h_lat), m_lat[0], m_lat[1])
        if i % 2 == 0:
            o_ctx, o_lat = retention_ssd_mixer(a_ctx, a_lat, ret_ssd_w_in[j], ssd_conv_w[j], ssd_conv_b[j],
                                               ssd_dt_bias[j], ssd_a_log[j], ssd_d[j], ssd_norm_w[j],
                                               ret_decay_logit[j], ret_gn_w[j], ret_ssd_w_out[j], rope_ret, not last)
        else:
            o_ctx, o_lat = mla_mixer(a_ctx, a_lat, mla_w_in[j], mla_q_norm_w[j], mla_w_uq[j], mla_kv_norm_w[j],
                                     mla_w_ukv[j], mla_w_out[j], rope_mla, not last)
        h_lat = h_lat + m_lat[2] * o_lat
        h_lat = h_lat + m_lat[5] * swiglu(modulate(rms_norm(h_lat), m_lat[3], m_lat[4]),
                                          w_ffn_gate[i], w_ffn_up[i], w_ffn_down[i])
        if not last:
            h_ctx = h_ctx + m_ctx[2] * o_ctx
            h_ctx = h_ctx + m_ctx[5] * swiglu(modulate(rms_norm(h_ctx), m_ctx[3], m_ctx[4]),
                                              w_ffn_gate[i], w_ffn_up[i], w_ffn_down[i])
    return rms_norm(h_lat, final_norm_w)
```

```python
import numpy as np
from contextlib import ExitStack
import concourse.bass as bass
import concourse.mybir as mybir
from concourse.bass_utils import run_bass_kernel_spmd

F32 = mybir.dt.float32
BF16 = mybir.dt.bfloat16
AF = mybir.ActivationFunctionType
ALU = mybir.AluOpType
AX = mybir.AxisListType

D = 1024
B = 4
L = 4096
CTX = 256
DFF = 2816
EPS = 1e-6
NCORES = 8
IN0 = 3600
IN1 = 1056


class T:
    def __init__(self, ap, sem=None, name=""):
        self.ap = ap
        self.sem = sem
        self.w = []
        self.r = []
        self.name = name

    def __getitem__(self, idx):
        return self.ap[idx]


class KB:
    def __init__(self, nc, es):
        self.nc = nc
        self.es = es
        self.eng = dict(pe=nc.tensor, act=nc.scalar, dve=nc.vector, pool=nc.gpsimd, sp=nc.sync)
        self.sems = {}
        self.count = {}
        self.seen = {e: {} for e in self.eng}
        for e in ("pe", "act", "dve", "pool"):
            self._newsem("e_" + e)
        self._uid = 0
        self.outs = []

    def _newsem(self, name):
        self.sems[name] = self.nc.alloc_semaphore(name=name)
        self.count[name] = 0
        return name

    def uid(self, p):
        self._uid += 1
        return "%s_%d" % (p, self._uid)

    def wrap(self, ap, name="", dma=False):
        sem = self._newsem(self.uid("d_" + name)) if dma else None
        return T(ap, sem, name)

    def sb(self, name, shape, dt, dma=False):
        t = self.es.enter_context(self.nc.sbuf_tensor(name, list(shape), dt))
        return self.wrap(t, name, dma)

    def ps(self, name, shape, dt=F32):
        t = self.es.enter_context(self.nc.psum_tensor(name, list(shape), dt))
        return self.wrap(t, name)

    def dram_in(self, name, shape, dt=F32):
        return self.nc.dram_tensor(name, list(shape), dt, kind="ExternalInput").ap()

    def dram_out(self, name, shape, dt=F32):
        ap = self.nc.dram_tensor(name, list(shape), dt, kind="ExternalOutput").ap()
        t = self.wrap(ap, name)
        self.outs.append(t)
        return t

    def _wait(self, e, tok):
        sem, val = tok
        if self.seen[e].get(sem, 0) >= val:
            return
        self.eng[e].wait_ge(self.sems[sem], val)
        self.seen[e][sem] = val

    limit = None
    nlim = 0

    def op(self, e, fn, reads=(), writes=(), dma=None):
        if self.limit is not None:
            self.nlim += 1
            if self.nlim > self.limit:
                return None
        for b in reads:
            for tok in b.w:
                self._wait(e, tok)
        for b in writes:
            for tok in b.w:
                self._wait(e, tok)
            for tok in b.r:
                self._wait(e, tok)
        ins = fn()
        if dma is not None:
            if not isinstance(ins, (list, tuple)):
                ins = [ins]
            sem = dma.sem
            for i in ins:
                i.then_inc(self.sems[sem], 16)
                self.count[sem] += 16
        else:
            if isinstance(ins, (list, tuple)):
                ins = ins[-1]
            sem = "e_" + e
            ins.then_inc(self.sems[sem], 1)
            self.count[sem] += 1
        tok = (sem, self.count[sem])
        for b in reads:
            b.r.append(tok)
        for b in writes:
            b.w = [tok]
            b.r = []
        return tok

    def load(self, dst, dst_ap, src_ap, q="sp"):
        eng = self.eng[q]
        return self.op(q, lambda: eng.dma_start(out=dst_ap, in_=src_ap), writes=[dst], dma=dst)

    def store(self, dst_t, dst_ap, src, src_ap, q="sp"):
        eng = self.eng[q]
        return self.op(q, lambda: eng.dma_start(out=dst_ap, in_=src_ap), reads=[src], writes=[dst_t], dma=src)

    def barrier(self):
        for e in self.eng:
            for sem, cnt in self.count.items():
                if cnt > 0:
                    self._wait(e, (sem, cnt))

    def finish(self):
        for b in self.outs:
            for tok in b.w + b.r:
                self._wait("sp", tok)


N_LAUNCH = [0]


def launch(build, in_maps):
    nc = bass.Bass("TRN2", target_bir_lowering=False)
    with ExitStack() as es:
        kb = KB(nc, es)
        build(nc, kb)
        kb.finish()
    import os
    if os.environ.get("K_TRACE"):
        res = run_bass_kernel_spmd(nc, in_maps, core_ids=list(range(len(in_maps))), trace=True)
        print("K_TRACE", build.__name__, "exec_time_ns", res.exec_time_ns, flush=True)
    else:
        res = run_bass_kernel_spmd(nc, in_maps, core_ids=list(range(len(in_maps))))
    N_LAUNCH[0] += 1
    return res.results


def c32(a):
    return np.ascontiguousarray(a, dtype=np.float32)


def build_k0(nc, kb):
    cT = kb.dram_in("cT", [128, 8, 5])
    w = kb.dram_in("w", [1024, 1536])
    bias = kb.dram_in("bias", [5, 1536])
    out = kb.dram_out("m", [5, 1536])
    ct = kb.sb("ct", [128, 8, 5], F32, dma=True)
    cs = kb.sb("cs", [128, 8, 5], F32)
    wt = kb.sb("wt", [128, 8, 1536], F32, dma=True)
    bt = kb.sb("bt", [5, 1536], F32, dma=True)
    ot = kb.sb("ot", [5, 1536], F32, dma=True)
    kb.load(ct, ct[:], cT)
    kb.load(bt, bt[:], bias)
    wv = w.rearrange("(k p) n -> p k n", p=128)
    kb.op("sp", lambda: [nc.sync.dma_start(out=wt[:, k, :], in_=wv[:, k, :]) for k in range(8)], writes=[wt], dma=wt)
    kb.op("act", lambda: nc.scalar.activation(out=cs[:], in_=ct[:], func=AF.Silu), reads=[ct], writes=[cs])
    pss = [kb.ps("ps%d" % i, [128, 512]) for i in range(3)]
    for j in range(3):
        kb.op("pe", lambda: [nc.tensor.matmul(pss[j][0:5, :], lhsT=cs[:, k, :], rhs=wt[:, k, j * 512:(j + 1) * 512],
                                               start=(k == 0), stop=(k == 7)) for k in range(8)],
              reads=[cs, wt], writes=[pss[j]])
        kb.op("dve", lambda: nc.vector.tensor_tensor(out=ot[:, j * 512:(j + 1) * 512], in0=pss[j][0:5, :],
                                                     in1=bt[:, j * 512:(j + 1) * 512], op=ALU.add),
              reads=[pss[j], bt], writes=[ot])
    kb.store(out, out[:], ot, ot[:])


def run_k0(c, c_ctx, w_ada, b_ada):
    cond = np.concatenate([c, c_ctx[None]], axis=0)
    cT = c32(cond.T.reshape(8, 128, 5).transpose(1, 0, 2))
    wflat = [w_ada[0], w_ada[1]]
    maps = []
    for r in range(8):
        l, j = r // 4, r % 4
        maps.append({"cT": cT, "w": c32(wflat[l][:, j * 1536:(j + 1) * 1536]),
                     "bias": c32(np.broadcast_to(b_ada[l][j * 1536:(j + 1) * 1536], (5, 1536)))})
    res = launch(build_k0, maps)
    m = np.zeros((2, 5, 6144), np.float32)
    for r in range(8):
        l, j = r // 4, r % 4
        m[l, :, j * 1536:(j + 1) * 1536] = res[r]["m"]
    return m


def fm(x_tok):
    n, f = x_tok.shape
    return c32(x_tok.T.reshape(f // 128, 128, n).transpose(1, 0, 2))


def unfm(x_fm):
    p, kc, n = x_fm.shape
    return x_fm.transpose(1, 0, 2).reshape(kc * 128, n).T


def vec_fm(v):
    return c32(v.reshape(-1, 128).T)


class FM:
    def __init__(self, nc, kb, pfx=""):
        self.nc = nc
        self.kb = kb
        self.ones = kb.sb(pfx + "ones_bf", [128, 128], BF16)
        kb.op("dve", lambda: nc.vector.memset(self.ones[:], 1.0), writes=[self.ones])
        self.epst = kb.sb(pfx + "epst", [128, 1], F32)
        kb.op("dve", lambda: nc.vector.memset(self.epst[:], EPS), writes=[self.epst])

    def rstd_bc(self, xt, kc, n, ps, sq, out, nfeat, k0=0):
        nc, kb = self.nc, self.kb
        kb.op("act", lambda: nc.scalar.activation(out=sq[:, 0:kc, 0:n], in_=xt[:, k0:k0 + kc, 0:n], func=AF.Square),
              reads=[xt], writes=[sq])
        kb.op("pe", lambda: [nc.tensor.matmul(ps[:, 0:n], lhsT=self.ones[:], rhs=sq[:, k, 0:n], start=(k == 0),
                                               stop=(k == kc - 1)) for k in range(kc)],
              reads=[sq, self.ones], writes=[ps])
        kb.op("act", lambda: nc.scalar.activation(out=out[:, 0:n], in_=ps[:, 0:n], func=AF.Sqrt, bias=self.epst[:, 0:1],
                                                  scale=1.0 / nfeat), reads=[ps, self.epst], writes=[out])
        kb.op("dve", lambda: nc.vector.reciprocal(out=out[:, 0:n], in_=out[:, 0:n]), reads=[out], writes=[out])

    def norm_mod(self, xt, n, ps, sq, rs, tmp, at, s1p, sh, col):
        nc, kb = self.nc, self.kb
        self.rstd_bc(xt, 8, n, ps, sq, rs, D)
        for k in range(8):
            kb.op("dve", lambda: nc.vector.scalar_tensor_tensor(out=tmp[:, k, 0:n], in0=xt[:, k, 0:n],
                                                                scalar=s1p[:, col * 8 + k:col * 8 + k + 1],
                                                                in1=rs[:, 0:n], op0=ALU.mult, op1=ALU.mult),
                  reads=[xt, rs, s1p], writes=[tmp])
            kb.op("act", lambda: nc.scalar.activation(out=at[:, k, 0:n], in_=tmp[:, k, 0:n], func=AF.Identity,
                                                      bias=sh[:, col * 8 + k:col * 8 + k + 1], scale=1.0),
                  reads=[tmp, sh], writes=[at])


def load_weight_bf16(nc, kb, wt, w_dram, kc, ncols, c0=0):
    wv = w_dram.rearrange("(k p) n -> p k n", p=128)
    step = 2048
    def f():
        ins = []
        for k in range(kc):
            for c in range(0, ncols, step):
                ce = min(ncols, c + step)
                ins.append(nc.gpsimd.dma_start(out=wt[:, k, c0 + c:c0 + ce], in_=wv[:, k, c:ce]))
        return ins
    kb.op("pool", f, writes=[wt], dma=wt)


GROUPS = [(0, 512, 0), (512, 512, 0), (1024, 512, 0), (1536, 512, 0), (2048, 128, 1)]
NTOK = 2176


def mod_pack(m_l, b):
    out = np.zeros((128, 6, 2, 8), np.float32)
    for which in range(6):
        for col, row in enumerate((b, 4)):
            v = m_l[row, which * 1024:(which + 1) * 1024]
            out[:, which, col, :] = v.reshape(8, 128).T
    return c32(out.reshape(128, 96))


def build_k1(nc, kb):
    xT = kb.dram_in("xT", [128, 8, NTOK])
    mod = kb.dram_in("mod", [128, 96])
    w = kb.dram_in("w", [1024, IN0])
    out = kb.dram_out("pT", [29, 128, NTOK])
    fmh = FM(nc, kb)
    modt = kb.sb("modt", [128, 96], F32, dma=True)
    s1p = kb.sb("s1p", [128, 96], F32)
    kb.load(modt, modt[:], mod)
    kb.op("dve", lambda: nc.vector.tensor_scalar(out=s1p[:], in0=modt[:], scalar1=1.0, scalar2=None, op0=ALU.add),
          reads=[modt], writes=[s1p])
    wt = kb.sb("wt", [128, 8, 29 * 128], BF16, dma=True)
    kb.op("dve", lambda: nc.vector.memset(wt[:, :, IN0:29 * 128], 0.0), writes=[wt])
    load_weight_bf16(nc, kb, wt, w, 8, IN0)
    xts = [kb.sb("xt%d" % i, [128, 8, 512], F32, dma=True) for i in range(2)]
    sq = kb.sb("sq", [128, 8, 512], BF16)
    rs = kb.sb("rs", [128, 512], F32)
    tmp = kb.sb("tmp", [128, 8, 512], F32)
    ats = [kb.sb("at%d" % i, [128, 8, 512], BF16) for i in range(2)]
    psn = kb.ps("psn", [128, 512])
    pss = [kb.ps("ps%d" % i, [128, 512]) for i in range(4)]
    ots = [kb.sb("ot%d" % i, [128, 512], F32, dma=True) for i in range(4)]
    cnt = 0
    def prep(gi):
        t0, n, col = GROUPS[gi]
        xt = xts[gi % 2]
        kb.load(xt, xt[:, :, 0:n], xT[:, :, t0:t0 + n], q="pool")
        return _norm_mod_thunks(fmh, xt, n, psn, sq, rs, tmp, ats[gi % 2], s1p, modt, 1, 0, col)

    for t in prep(0):
        t()
    for gi, (t0, n, col) in enumerate(GROUPS):
        at = ats[gi % 2]
        nxt = prep(gi + 1) if gi + 1 < len(GROUPS) else []
        for m in range(29):
            ps = pss[cnt % 4]
            ot = ots[cnt % 4]
            cnt += 1
            kb.op("pe", lambda: [nc.tensor.matmul(ps[:, 0:n], lhsT=wt[:, k, m * 128:(m + 1) * 128], rhs=at[:, k, 0:n],
                                                   start=(k == 0), stop=(k == 7)) for k in range(8)],
                  reads=[wt, at], writes=[ps])
            eng = "act" if m % 2 == 0 else "dve"
            if eng == "act":
                kb.op("act", lambda: nc.scalar.copy(out=ot[:, 0:n], in_=ps[:, 0:n]), reads=[ps], writes=[ot])
            else:
                kb.op("dve", lambda: nc.vector.tensor_copy(out=ot[:, 0:n], in_=ps[:, 0:n]), reads=[ps], writes=[ot])
            kb.store(out, out[m, :, t0:t0 + n], ot, ot[:, 0:n])
            if nxt and m >= 2:
                nxt.pop(0)()
        for t in nxt:
            t()


def _norm_mod_thunks(fmh, xt, n, ps, sq, rs, tmp, at, s1p, modt, which_scale, which_shift, col):
    nc, kb = fmh.nc, fmh.kb
    if not hasattr(tmp, "views"):
        tmp.views = [T(tmp.ap[:, k, :]) for k in range(8)]
    th = []
    th.append(lambda: kb.op("act", lambda: nc.scalar.activation(out=sq[:, 0:8, 0:n], in_=xt[:, 0:8, 0:n], func=AF.Square),
                            reads=[xt], writes=[sq]))
    th.append(lambda: kb.op("pe", lambda: [nc.tensor.matmul(ps[:, 0:n], lhsT=fmh.ones[:], rhs=sq[:, k, 0:n], start=(k == 0), stop=(k == 7))
                                           for k in range(8)], reads=[sq, fmh.ones], writes=[ps]))
    th.append(lambda: kb.op("act", lambda: nc.scalar.activation(out=rs[:, 0:n], in_=ps[:, 0:n], func=AF.Sqrt, bias=fmh.epst[:, 0:1],
                                                                scale=1.0 / D), reads=[ps, fmh.epst], writes=[rs]))
    th.append(lambda: kb.op("dve", lambda: nc.vector.reciprocal(out=rs[:, 0:n], in_=rs[:, 0:n]), reads=[rs], writes=[rs]))
    for k in range(8):
        isc = (which_scale * 2 + col) * 8 + k
        ish = (which_shift * 2 + col) * 8 + k
        tv = tmp.views[k]
        th.append(lambda k=k, isc=isc, tv=tv: kb.op("dve", lambda: nc.vector.scalar_tensor_tensor(
            out=tv[:, 0:n], in0=xt[:, k, 0:n], scalar=s1p[:, isc:isc + 1], in1=rs[:, 0:n], op0=ALU.mult, op1=ALU.mult),
            reads=[xt, rs, s1p], writes=[tv]))
        th.append(lambda k=k, ish=ish, tv=tv: kb.op("act", lambda: nc.scalar.activation(
            out=at[:, k, 0:n], in_=tv[:, 0:n], func=AF.Identity, bias=modt[:, ish:ish + 1], scale=1.0),
            reads=[tv, modt], writes=[at]))
    return th


def _norm_mod(fmh, xt, n, ps, sq, rs, tmp, at, s1p, modt, which_scale, which_shift, col):
    for t in _norm_mod_thunks(fmh, xt, n, ps, sq, rs, tmp, at, s1p, modt, which_scale, which_shift, col):
        t()


def core_tokens(xb, ctxb, s):
    return np.concatenate([xb[s * 2048:(s + 1) * 2048], ctxb[s * 128:(s + 1) * 128]], axis=0)


def run_k1(h_lat, h_ctx, m0, w_in):
    maps = []
    for r in range(8):
        b, s = r // 2, r % 2
        maps.append({"xT": fm(core_tokens(h_lat[b], h_ctx[b], s)), "mod": mod_pack(m0, b), "w": c32(w_in)})
    res = launch(build_k1, maps)
    p_lat = np.zeros((B, L, IN0), np.float32)
    p_ctx = np.zeros((B, CTX, IN0), np.float32)
    for r in range(8):
        b, s = r // 2, r % 2
        pt = res[r]["pT"].reshape(29 * 128, NTOK)[:IN0].T
        p_lat[b, s * 2048:(s + 1) * 2048] = pt[:2048]
        p_ctx[b, s * 128:(s + 1) * 128] = pt[2048:]
    return p_lat, p_ctx


def make_k3a(finish):
    def build(nc, kb):
        hT = kb.dram_in("hT", [128, 8, NTOK])
        mod = kb.dram_in("mod", [128, 96])
        w = kb.dram_in("w", [1024, 1024])
        if finish:
            yr = kb.dram_in("yr", [128, 4, NTOK])
            ys = kb.dram_in("ys", [128, 4, NTOK])
            gT = kb.dram_in("gT", [128, 4, NTOK])
            zT = kb.dram_in("zT", [128, 4, NTOK])
            nw = kb.dram_in("nw", [128, 8])
            bo = kb.dram_in("bo", [128, 128])
        else:
            mixin = kb.dram_in("mix", [128, 8, NTOK])
        out_h = kb.dram_out("h1T", [128, 8, NTOK])
        out_a = kb.dram_out("a2T", [128, 8, NTOK])
        fmh = FM(nc, kb)
        modt = kb.sb("modt", [128, 96], F32, dma=True)
        s1p = kb.sb("s1p", [128, 96], F32)
        kb.load(modt, modt[:], mod)
        kb.op("dve", lambda: nc.vector.tensor_scalar(out=s1p[:], in0=modt[:], scalar1=1.0, scalar2=None, op0=ALU.add),
              reads=[modt], writes=[s1p])
        wt = kb.sb("wt", [128, 8, 1024], BF16, dma=True)
        load_weight_bf16(nc, kb, wt, w, 8, 1024)
        hts = [kb.sb("ht%d" % i, [128, 8, 256], F32, dma=True) for i in range(2)]
        mixs = [kb.sb("mixb%d" % i, [128, 8, 256], BF16, dma=True) for i in range(2)]
        sq = kb.sb("sq", [128, 8, 256], BF16)
        rs = kb.sb("rs", [128, 256], F32)
        tmp = kb.sb("tmp", [128, 8, 256], F32)
        ats = [kb.sb("at%d" % i, [128, 8, 256], F32, dma=True) for i in range(2)]
        psn = kb.ps("psn", [128, 512])
        pss = [kb.ps("ps%d" % i, [128, 512]) for i in range(2)]
        if finish:
            yrts = [kb.sb("yrt%d" % i, [128, 4, 256], F32, dma=True) for i in range(2)]
            ysts = [kb.sb("yst%d" % i, [128, 4, 256], F32, dma=True) for i in range(2)]
            gts = [kb.sb("gt%d" % i, [128, 4, 256], F32, dma=True) for i in range(2)]
            zts = [kb.sb("zt%d" % i, [128, 4, 256], F32, dma=True) for i in range(2)]
            nwt = kb.sb("nwt", [128, 8], F32, dma=True)
            bot = kb.sb("bot", [128, 128], F32, dma=True)
            onesf = kb.sb("onesf", [128, 128], F32)
            dd = kb.sb("dd", [128, 4, 256], F32)
            psA = kb.ps("psA", [128, 4, 256])
            psB = kb.ps("psB", [128, 4, 256])
            sqd = kb.sb("sqd", [128, 4, 256], F32)
            rr = kb.sb("rr", [128, 4, 256], F32)
            kb.load(nwt, nwt[:], nw)
            kb.load(bot, bot[:], bo)
            kb.op("dve", lambda: nc.vector.memset(onesf[:], 1.0 / 512.0), writes=[onesf])
        for gi, (t0, n, col) in enumerate(GROUPS_B):
            ht, mix, at = hts[gi % 2], mixs[gi % 2], ats[gi % 2]
            kb.load(ht, ht[:, :, 0:n], hT[:, :, t0:t0 + n], q="pool")
            if finish:
                yrt, yst, gt, zt = yrts[gi % 2], ysts[gi % 2], gts[gi % 2], zts[gi % 2]
                kb.load(yrt, yrt[:, :, 0:n], yr[:, :, t0:t0 + n], q="pool")
                kb.load(yst, yst[:, :, 0:n], ys[:, :, t0:t0 + n], q="pool")
                kb.load(gt, gt[:, :, 0:n], gT[:, :, t0:t0 + n], q="pool")
                kb.load(zt, zt[:, :, 0:n], zT[:, :, t0:t0 + n], q="pool")
                kb.op("act", lambda: nc.scalar.activation(out=gt[:, :, 0:n], in_=gt[:, :, 0:n], func=AF.Silu), reads=[gt], writes=[gt])
                kb.op("act", lambda: nc.scalar.activation(out=zt[:, :, 0:n], in_=zt[:, :, 0:n], func=AF.Silu), reads=[zt], writes=[zt])
                kb.op("pe", lambda: [nc.tensor.matmul(psA[:, c, 0:n], lhsT=bot[:], rhs=yrt[:, c, 0:n], start=True, stop=True) for c in range(4)],
                      reads=[bot, yrt], writes=[psA])
                kb.op("dve", lambda: nc.vector.tensor_tensor(out=dd[:, :, 0:n], in0=yrt[:, :, 0:n], in1=psA[:, :, 0:n], op=ALU.subtract),
                      reads=[yrt, psA], writes=[dd])
                kb.op("act", lambda: nc.scalar.activation(out=sqd[:, :, 0:n], in_=dd[:, :, 0:n], func=AF.Square), reads=[dd], writes=[sqd])
                kb.op("pe", lambda: [nc.tensor.matmul(psB[:, c, 0:n], lhsT=bot[:], rhs=sqd[:, c, 0:n], start=True, stop=True) for c in range(4)],
                      reads=[bot, sqd], writes=[psB])
                kb.op("act", lambda: nc.scalar.activation(out=rr[:, :, 0:n], in_=psB[:, :, 0:n], func=AF.Sqrt, bias=fmh.epst[:, 0:1], scale=1.0),
                      reads=[psB, fmh.epst], writes=[rr])
                kb.op("dve", lambda: nc.vector.reciprocal(out=rr[:, :, 0:n], in_=rr[:, :, 0:n]), reads=[rr], writes=[rr])
                kb.op("dve", lambda: nc.vector.tensor_tensor(out=dd[:, :, 0:n], in0=dd[:, :, 0:n], in1=rr[:, :, 0:n], op=ALU.mult),
                      reads=[dd, rr], writes=[dd])
                for cch in range(4):
                    kb.op("dve", lambda: nc.vector.scalar_tensor_tensor(out=mix[:, cch, 0:n], in0=dd[:, cch, 0:n], scalar=nwt[:, cch:cch + 1],
                                                                        in1=gt[:, cch, 0:n], op0=ALU.mult, op1=ALU.mult),
                          reads=[dd, gt, nwt], writes=[mix])
                kb.op("dve", lambda: nc.vector.tensor_tensor(out=yst[:, :, 0:n], in0=yst[:, :, 0:n], in1=zt[:, :, 0:n], op=ALU.mult),
                      reads=[yst, zt], writes=[yst])
                kb.op("act", lambda: nc.scalar.activation(out=sqd[:, :, 0:n], in_=yst[:, :, 0:n], func=AF.Square), reads=[yst], writes=[sqd])
                kb.op("pe", lambda: [nc.tensor.matmul(psA[:, 0, 0:n], lhsT=onesf[:], rhs=sqd[:, k, 0:n], start=(k == 0), stop=(k == 3))
                                     for k in range(4)], reads=[onesf, sqd], writes=[psA])
                kb.op("act", lambda: nc.scalar.activation(out=rr[:, 0, 0:n], in_=psA[:, 0, 0:n], func=AF.Sqrt, bias=fmh.epst[:, 0:1], scale=1.0),
                      reads=[psA, fmh.epst], writes=[rr])
                kb.op("dve", lambda: nc.vector.reciprocal(out=rr[:, 0, 0:n], in_=rr[:, 0, 0:n]), reads=[rr], writes=[rr])
                for cch in range(4):
                    kb.op("dve", lambda: nc.vector.scalar_tensor_tensor(out=mix[:, 4 + cch, 0:n], in0=yst[:, cch, 0:n],
                                                                        scalar=nwt[:, 4 + cch:5 + cch], in1=rr[:, 0, 0:n],
                                                                        op0=ALU.mult, op1=ALU.mult),
                          reads=[yst, rr, nwt], writes=[mix])
            else:
                kb.load(mix, mix[:, :, 0:n], mixin[:, :, t0:t0 + n], q="pool")
            for m in range(8):
                ps = pss[m % 2]
                kb.op("pe", lambda: [nc.tensor.matmul(ps[:, 0:n], lhsT=wt[:, k, m * 128:(m + 1) * 128], rhs=mix[:, k, 0:n],
                                                       start=(k == 0), stop=(k == 7)) for k in range(8)],
                      reads=[wt, mix], writes=[ps])
                ig = (2 * 2 + col) * 8 + m
                kb.op("dve", lambda: nc.vector.scalar_tensor_tensor(out=ht[:, m, 0:n], in0=ps[:, 0:n], scalar=modt[:, ig:ig + 1],
                                                                    in1=ht[:, m, 0:n], op0=ALU.mult, op1=ALU.add),
                      reads=[ps, ht, modt], writes=[ht])
            kb.store(out_h, out_h[:, :, t0:t0 + n], ht, ht[:, :, 0:n])
            _norm_mod(fmh, ht, n, psn, sq, rs, tmp, at, s1p, modt, 4, 3, col)
            kb.store(out_a, out_a[:, :, t0:t0 + n], at, at[:, :, 0:n])
    return build


GROUPS_B = [(i * 256, 256, 0) for i in range(8)] + [(2048, 128, 1)]


def make_k3b(final, pfx="", src=None):
    def build(nc, kb):
        P = pfx
        if src is None:
            aT = kb.dram_in("aT", [128, 8, NTOK])
            hT = kb.dram_in("hT", [128, 8, NTOK])
        else:
            aT, hT = src["a2T"].ap, src["h1T"].ap
        mod = kb.dram_in(P + "mod", [128, 96])
        wg = kb.dram_in("wg", [1024, DFF])
        wu = kb.dram_in("wu", [1024, DFF])
        wd = kb.dram_in("wd", [DFF, 1024])
        out = kb.dram_out("oT", [128, 8, NTOK])
        modt = kb.sb(P + "modt", [128, 96], F32, dma=True)
        kb.load(modt, modt[:], mod)
        wgt = kb.sb(P + "wgt", [128, 8, DFF], BF16, dma=True)
        wut = kb.sb(P + "wut", [128, 8, DFF], BF16, dma=True)
        wdt = kb.sb(P + "wdt", [128, 22, 1024], BF16, dma=True)
        load_weight_bf16(nc, kb, wgt, wg, 8, DFF)
        load_weight_bf16(nc, kb, wut, wu, 8, DFF)
        load_weight_bf16(nc, kb, wdt, wd, 22, 1024)
        ats = [kb.sb(P + "at%d" % i, [128, 8, 256], BF16, dma=True) for i in range(2)]
        hts = [kb.sb(P + "ht%d" % i, [128, 8, 256], F32, dma=True) for i in range(2 if final else 1)]
        h1 = kb.sb(P + "h1", [128, 22, 256], BF16)
        sgs = [kb.sb(P + "sg%d" % i, [128, 256], F32) for i in range(2)]
        psg = [kb.ps(P + "psg%d" % i, [128, 512]) for i in range(2)]
        psu = [kb.ps(P + "psu%d" % i, [128, 512]) for i in range(2)]
        psd = [kb.ps(P + "psd%d" % i, [128, 512]) for i in range(3)]
        if final:
            fmh = FM(nc, kb, P)
            fnw = kb.dram_in("fnw", [128, 8])
            fnt = kb.sb(P + "fnt", [128, 8], F32, dma=True)
            kb.load(fnt, fnt[:], fnw)
            sq = kb.sb(P + "sq", [128, 8, 256], BF16)
            rs = kb.sb(P + "rs", [128, 256], F32)
            psn = kb.ps(P + "psn", [128, 512])
        pend = []
        for gi, (t0, n, col) in enumerate(GROUPS_B):
            at = ats[gi % 2]
            ht = hts[gi % len(hts)]
            kb.load(at, at[:, :, 0:n], aT[:, :, t0:t0 + n], q="pool")
            kb.load(ht, ht[:, :, 0:n], hT[:, :, t0:t0 + n], q="pool")
            for j in range(22):
                if pend and j >= 1:
                    pend.pop(0)()
                pg, pu, sg = psg[j % 2], psu[j % 2], sgs[j % 2]
                kb.op("pe", lambda: [nc.tensor.matmul(pg[:, 0:n], lhsT=wgt[:, k, j * 128:(j + 1) * 128], rhs=at[:, k, 0:n],
                                                       start=(k == 0), stop=(k == 7)) for k in range(8)],
                      reads=[wgt, at], writes=[pg])
                kb.op("pe", lambda: [nc.tensor.matmul(pu[:, 0:n], lhsT=wut[:, k, j * 128:(j + 1) * 128], rhs=at[:, k, 0:n],
                                                       start=(k == 0), stop=(k == 7)) for k in range(8)],
                      reads=[wut, at], writes=[pu])
                kb.op("act", lambda: nc.scalar.activation(out=sg[:, 0:n], in_=pg[:, 0:n], func=AF.Silu), reads=[pg], writes=[sg])
                kb.op("dve", lambda: nc.vector.tensor_tensor(out=h1[:, j, 0:n], in0=sg[:, 0:n], in1=pu[:, 0:n], op=ALU.mult),
                      reads=[sg, pu], writes=[h1])
            for m in range(8):
                pd = psd[m % 3]
                kb.op("pe", lambda: [nc.tensor.matmul(pd[:, 0:n], lhsT=wdt[:, j, m * 128:(m + 1) * 128], rhs=h1[:, j, 0:n],
                                                       start=(j == 0), stop=(j == 21)) for j in range(22)],
                      reads=[wdt, h1], writes=[pd])
                ig = (5 * 2 + col) * 8 + m
                kb.op("dve", lambda: nc.vector.scalar_tensor_tensor(out=ht[:, m, 0:n], in0=pd[:, 0:n], scalar=modt[:, ig:ig + 1],
                                                                    in1=ht[:, m, 0:n], op0=ALU.mult, op1=ALU.add),
                      reads=[pd, ht, modt], writes=[ht])
            for t in pend:
                t()
            pend = []
            if final:
                def tail(ht=ht, n=n, t0=t0):
                    th = []
                    th.append(lambda: kb.op("act", lambda: nc.scalar.activation(out=sq[:, 0:8, 0:n], in_=ht[:, 0:8, 0:n], func=AF.Square),
                                            reads=[ht], writes=[sq]))
                    th.append(lambda: kb.op("pe", lambda: [nc.tensor.matmul(psn[:, 0:n], lhsT=fmh.ones[:], rhs=sq[:, kk, 0:n], start=(kk == 0),
                                                                          stop=(kk == 7)) for kk in range(8)], reads=[sq, fmh.ones], writes=[psn]))
                    th.append(lambda: kb.op("act", lambda: nc.scalar.activation(out=rs[:, 0:n], in_=psn[:, 0:n], func=AF.Sqrt, bias=fmh.epst[:, 0:1],
                                                                                scale=1.0 / D), reads=[psn, fmh.epst], writes=[rs]))
                    th.append(lambda: kb.op("dve", lambda: nc.vector.reciprocal(out=rs[:, 0:n], in_=rs[:, 0:n]), reads=[rs], writes=[rs]))
                    for kk in range(8):
                        th.append(lambda kk=kk: kb.op("dve", lambda: nc.vector.scalar_tensor_tensor(
                            out=ht[:, kk, 0:n], in0=ht[:, kk, 0:n], scalar=fnt[:, kk:kk + 1], in1=rs[:, 0:n], op0=ALU.mult, op1=ALU.mult),
                            reads=[ht, rs, fnt], writes=[ht]))
                    th.append(lambda: kb.store(out, out[:, :, t0:t0 + n], ht, ht[:, :, 0:n]))
                    return th
                pend = tail()
            else:
                kb.store(out, out[:, :, t0:t0 + n], ht, ht[:, :, 0:n])
        for t in pend:
            t()
    return build


def make_k3ab(finish, final):
    def build(nc, kb):
        outer = kb.es
        with ExitStack() as es1:
            kb.es = es1
            make_k3a(finish)(nc, kb)
            kb.barrier()
        kb.es = outer
        src = {t.name: t for t in kb.outs}
        make_k3b(final, pfx="b_", src=src)(nc, kb)
    return build


def run_k3ab(h_lat, h_ctx, m_l, w_out, wg, wu, wd, finish=None, mix_lat=None, final_w=None):
    maps = []
    bo = np.zeros((128, 128), np.float32)
    bo[:64, :64] = 1.0 / 64
    bo[64:, 64:] = 1.0 / 64
    for r in range(8):
        b, s = r // 2, r % 2
        mp = mod_pack(m_l, b)
        d = {"hT": fm(core_rows(h_lat[b], h_ctx[b], s)), "mod": mp, "w": c32(w_out), "b_mod": mp,
             "wg": c32(wg), "wu": c32(wu), "wd": c32(wd)}
        if finish is not None:
            f = finish
            d["yr"] = fm(core_rows(f["yr"][b, CTX:], f["yr"][b, :CTX], s))
            d["ys"] = fm(core_rows(f["ys"][b, CTX:], f["ys"][b, :CTX], s))
            d["gT"] = fm(core_rows(f["p_lat"][b][:, 1536:2048], f["p_ctx"][b][:, 1536:2048], s))
            d["zT"] = fm(core_rows(f["p_lat"][b][:, 2048:2560], f["p_ctx"][b][:, 2048:2560], s))
            d["nw"] = c32(np.concatenate([vec_fm(f["gn_w"]), vec_fm(f["ssd_norm_w"])], 1))
            d["bo"] = bo
        else:
            d["mix"] = fm(core_rows(mix_lat[b], np.zeros((CTX, 1024), np.float32), s))
        if final_w is not None:
            d["fnw"] = vec_fm(final_w)
        maps.append(d)
    res = launch(make_k3ab(finish is not None, final_w is not None), maps)
    return [res[r]["oT"] for r in range(8)]


def run_k3b(aT_list, hT_list, m_l, wg, wu, wd, final_w=None):
    maps = []
    for r in range(8):
        b = r // 2
        d = {"aT": aT_list[r], "hT": hT_list[r], "mod": mod_pack(m_l, b), "wg": c32(wg), "wu": c32(wu), "wd": c32(wd)}
        if final_w is not None:
            d["fnw"] = vec_fm(final_w)
        maps.append(d)
    res = launch(make_k3b(final_w is not None), maps)
    return [res[r]["oT"] for r in range(8)]


SEQ = CTX + L
NBLK = SEQ // 256


def build_k1b(nc, kb):
    qk = kb.dram_in("qk", [NBLK, 128, 4, 256])
    cs = kb.dram_in("cs", [NBLK, 128, 2, 256])
    xbc = kb.dram_in("xbc", [NBLK, 128, 4, 260])
    cw = kb.dram_in("cw", [128, 4, 5])
    cb = kb.dram_in("cb", [128, 4])
    dtr = kb.dram_in("dtr", [NBLK, 8, 256])
    dpar = kb.dram_in("dpar", [8, 2])
    qko = kb.dram_out("qko", [NBLK, 128, 4, 256])
    xo = kb.dram_out("xo", [NBLK, 128, 4, 256])
    dto = kb.dram_out("dto", [NBLK, 8, 2, 256])
    cwt = kb.sb("cwt", [128, 4, 5], F32, dma=True)
    cbt = kb.sb("cbt", [128, 4], F32, dma=True)
    dpt = kb.sb("dpt", [8, 2], F32, dma=True)
    nA = kb.sb("nA", [8, 1], F32)
    one8 = kb.sb("one8", [8, 1], F32)
    kb.load(cwt, cwt[:], cw)
    kb.load(cbt, cbt[:], cb)
    kb.load(dpt, dpt[:], dpar)
    kb.op("dve", lambda: nc.vector.memset(one8[:], 1.0), writes=[one8])
    kb.op("act", lambda: nc.scalar.activation(out=nA[:], in_=dpt[:, 1:2], func=AF.Exp), reads=[dpt], writes=[nA])
    kb.op("dve", lambda: nc.vector.tensor_scalar(out=nA[:], in0=nA[:], scalar1=-1.0, scalar2=None, op0=ALU.mult), reads=[nA], writes=[nA])
    qkt = [kb.sb("qkt%d" % i, [128, 4, 256], F32, dma=True) for i in range(2)]
    cst = [kb.sb("cst%d" % i, [128, 2, 256], F32, dma=True) for i in range(2)]
    xt = [kb.sb("xbt%d" % i, [128, 4, 260], F32, dma=True) for i in range(2)]
    dt_ = [kb.sb("dtt%d" % i, [8, 256], F32, dma=True) for i in range(2)]
    qo = [kb.sb("qo%d" % i, [128, 4, 256], F32, dma=True) for i in range(2)]
    xot = [kb.sb("xot%d" % i, [128, 4, 256], F32, dma=True) for i in range(2)]
    dot = [kb.sb("dot%d" % i, [8, 2, 256], F32, dma=True) for i in range(2)]
    ta = kb.sb("ta", [128, 256], F32)
    tb = kb.sb("tb", [128, 256], F32)
    acc = kb.sb("acc", [128, 256], F32)
    for blk in range(NBLK):
        i2 = blk % 2
        a, c_, x_, d_, o_, xo_, do_ = qkt[i2], cst[i2], xt[i2], dt_[i2], qo[i2], xot[i2], dot[i2]
        kb.load(a, a[:], qk[blk], q="pool")
        kb.load(c_, c_[:], cs[blk], q="pool")
        kb.load(x_, x_[:], xbc[blk], q="pool")
        kb.load(d_, d_[:], dtr[blk], q="pool")
        import os
        PARTS = 'rope,conv,dt'
        for pair, scl in (((0, 1.0), (1, 0.125)) if 'rope' in PARTS else ()):
            x1 = a[:, 2 * pair, :]
            x2 = a[:, 2 * pair + 1, :]
            cos, sin = c_[:, 0, :], c_[:, 1, :]
            V = nc.vector
            kb.op("dve", lambda: V.scalar_tensor_tensor(out=ta[:], in0=x1, scalar=scl, in1=cos, op0=ALU.mult, op1=ALU.mult), reads=[a, c_], writes=[ta])
            kb.op("dve", lambda: V.scalar_tensor_tensor(out=tb[:], in0=x2, scalar=scl, in1=sin, op0=ALU.mult, op1=ALU.mult), reads=[a, c_], writes=[tb])
            kb.op("dve", lambda: V.tensor_tensor(out=o_[:, 2 * pair, :], in0=ta[:], in1=tb[:], op=ALU.subtract), reads=[ta, tb], writes=[o_])
            kb.op("dve", lambda: V.scalar_tensor_tensor(out=ta[:], in0=x1, scalar=scl, in1=sin, op0=ALU.mult, op1=ALU.mult), reads=[a, c_], writes=[ta])
            kb.op("dve", lambda: V.scalar_tensor_tensor(out=tb[:], in0=x2, scalar=scl, in1=cos, op0=ALU.mult, op1=ALU.mult), reads=[a, c_], writes=[tb])
            kb.op("dve", lambda: V.tensor_tensor(out=o_[:, 2 * pair + 1, :], in0=ta[:], in1=tb[:], op=ALU.add), reads=[ta, tb], writes=[o_])
        kb.store(qko, qko[blk], o_, o_[:])
        for t in (range(4) if 'conv' in PARTS else ()):
            G = nc.vector
            kb.op("dve", lambda: G.tensor_scalar(out=acc[:], in0=x_[:, t, 0:256], scalar1=cwt[:, t, 0:1], scalar2=None, op0=ALU.mult),
                  reads=[x_, cwt], writes=[acc])
            for k in range(1, 5):
                kb.op("dve", lambda: G.scalar_tensor_tensor(out=acc[:], in0=x_[:, t, k:k + 256], scalar=cwt[:, t, k:k + 1], in1=acc[:],
                                                             op0=ALU.mult, op1=ALU.add), reads=[x_, cwt, acc], writes=[acc])
            kb.op("act", lambda: nc.scalar.activation(out=xo_[:, t, :], in_=acc[:], func=AF.Silu, bias=cbt[:, t:t + 1], scale=1.0),
                  reads=[acc, cbt], writes=[xo_])
        kb.store(xo, xo[blk], xo_, xo_[:])
        if 'dt' not in PARTS:
            continue
        kb.op("dve", lambda: nc.vector.tensor_scalar(out=do_[:, 0, :], in0=d_[:], scalar1=dpt[:, 0:1], scalar2=None, op0=ALU.add), reads=[d_, dpt], writes=[do_])
        kb.op("act", lambda: nc.scalar.activation(out=do_[:, 0, :], in_=do_[:, 0, :], func=AF.Exp), reads=[do_], writes=[do_])
        kb.op("dve", lambda: nc.vector.tensor_scalar(out=do_[:, 0, :], in0=do_[:, 0, :], scalar1=1.0, scalar2=None, op0=ALU.add), reads=[do_], writes=[do_])
        kb.op("act", lambda: nc.scalar.activation(out=do_[:, 0, :], in_=do_[:, 0, :], func=AF.Ln), reads=[do_], writes=[do_])
        kb.op("dve", lambda: nc.vector.tensor_scalar(out=do_[:, 1, :], in0=do_[:, 0, :], scalar1=nA[:, 0:1], scalar2=None, op0=ALU.mult),
              reads=[do_, nA], writes=[do_])
        kb.store(dto, dto[blk], do_, do_[:])


def rope_tables(rot_dim):
    rows = L // 64
    row = np.repeat(np.arange(rows), 64).astype(np.float32)
    col = np.tile(np.arange(64), rows).astype(np.float32)
    n_freq = rot_dim // 4
    inv = (10000.0 ** (-np.arange(n_freq, dtype=np.float32) / n_freq)).astype(np.float32)
    ang = np.concatenate([row[:, None] * inv, col[:, None] * inv], axis=-1)
    cos = np.concatenate([np.ones((CTX, rot_dim // 2), np.float32), np.cos(ang)], 0)
    sin = np.concatenate([np.zeros((CTX, rot_dim // 2), np.float32), np.sin(ang)], 0)
    return cos.astype(np.float32), sin.astype(np.float32)


def blocks(xT, w=256):
    r, n = xT.shape
    return xT.reshape(r, n // w, w).transpose(1, 0, 2)


def run_k1b(p_lat, p_ctx, conv_w, conv_b, dt_bias, a_log):
    cos, sin = rope_tables(64)
    maps = []
    for r in range(8):
        b, s = r // 2, r % 2
        P = np.concatenate([p_ctx[b], p_lat[b]], 0)
        hs = slice(4 * s, 4 * s + 4)
        q = P[:, 0:512].reshape(SEQ, 8, 64)[:, hs]
        k = P[:, 512:1024].reshape(SEQ, 8, 64)[:, hs]
        arrs = [q[:, :, :32], q[:, :, 32:], k[:, :, :32], k[:, :, 32:]]
        qk = np.stack([blocks(a.reshape(SEQ, 128).T) for a in arrs], axis=2)
        ct = np.tile(cos, (1, 4)).T
        st = np.tile(sin, (1, 4)).T
        cs = np.stack([blocks(ct), blocks(st)], axis=2)
        xs = P[:, 2560 + 256 * s:2560 + 256 * s + 256]
        Bm = P[:, 3072 + 128 * s:3072 + 128 * s + 128]
        Cm = P[:, 3328 + 128 * s:3328 + 128 * s + 128]
        ch = np.concatenate([xs, Bm, Cm], 1)
        padc = np.pad(ch[:CTX], ((2, 2), (0, 0)))
        padl = np.pad(ch[CTX:], ((2, 2), (0, 0)))
        xb = np.zeros((NBLK, 128, 4, 260), np.float32)
        xb[0] = padc.T.reshape(4, 128, 260).transpose(1, 0, 2)
        for i in range(16):
            xb[1 + i] = padl[i * 256:i * 256 + 260].T.reshape(4, 128, 260).transpose(1, 0, 2)
        cidx = np.concatenate([np.arange(256 * s, 256 * s + 256), 512 + 128 * s + np.arange(128), 768 + 128 * s + np.arange(128)])
        cw = conv_w[:, cidx].T.reshape(4, 128, 5).transpose(1, 0, 2)
        cbv = conv_b[cidx].reshape(4, 128).T
        dtraw = P[:, 3584:3600].reshape(SEQ, 2, 8)[:, :, hs].reshape(SEQ, 8).T
        dpar = np.stack([dt_bias[:, hs].reshape(8), a_log[:, hs].reshape(8)], 1)
        maps.append({"qk": c32(qk), "cs": c32(cs), "xbc": c32(xb), "cw": c32(cw), "cb": c32(cbv),
                     "dtr": c32(blocks(dtraw)), "dpar": c32(dpar)})
    res = launch(build_k1b, maps)
    outs = []
    for r in range(8):
        qko = res[r]["qko"].transpose(1, 2, 0, 3).reshape(128, 4, SEQ)
        xo = res[r]["xo"].transpose(1, 2, 0, 3).reshape(128, 4, SEQ)
        dto = res[r]["dto"].transpose(1, 2, 0, 3).reshape(8, 2, SEQ)
        outs.append((qko, xo, dto))
    return outs


NCH = SEQ // 128
BWD_ORDER = [1, 0] + list(range(NCH - 1, 1, -1))


def bc(ap, axis, shape):
    return ap.unsqueeze(axis).to_broadcast(list(shape))


def build_k2(nc, kb):
    V, A, PE = nc.vector, nc.scalar, nc.tensor
    qT_d = kb.dram_in("qT", [128, 2, SEQ])
    kT_d = kb.dram_in("kT", [128, 2, SEQ])
    ktm_d = kb.dram_in("ktm", [128, NCH, 256])
    vtm_d = kb.dram_in("vtm", [128, NCH, 256])
    BT_d = kb.dram_in("BT", [128, SEQ])
    CT_d = kb.dram_in("CT", [128, SEQ])
    Btm_d = kb.dram_in("Btm", [128, NCH, 128])
    xtm_d = kb.dram_in("xtm", [128, NCH, 256])
    dtm_d = kb.dram_in("dtm", [128, NCH, 16])
    cst_d = kb.dram_in("cst", [128, 10, 128])
    pc_d = kb.dram_in("pc", [128, 2])
    dlp_d = kb.dram_in("dlp", [128, 4])
    dlt_d = kb.dram_in("dlt", [128, 8])
    dsk_d = kb.dram_in("dsk", [128, 4])
    yret = kb.dram_out("yret", [NCH, 128, 256])
    yssd = kb.dram_out("yssd", [NCH, 128, 256])

    def castload(name, shape, src, dims3):
        t = kb.sb(name, shape, BF16, dma=True)
        def f():
            ins = []
            if dims3:
                for a in range(shape[1]):
                    for c in range(0, shape[2], 2048):
                        ce = min(shape[2], c + 2048)
                        ins.append(nc.gpsimd.dma_start(out=t[:, a, c:ce], in_=src[:, a, c:ce]))
            else:
                for c in range(0, shape[1], 2048):
                    ce = min(shape[1], c + 2048)
                    ins.append(nc.gpsimd.dma_start(out=t[:, c:ce], in_=src[:, c:ce]))
            return ins
        kb.op("pool", f, writes=[t], dma=t)
        return t

    def f32load(name, shape, src):
        t = kb.sb(name + "_s", shape, F32, dma=True)
        kb.load(t, t[:], src)
        return t

    cst = f32load("cst", [128, 10, 128], cst_d)
    pc = f32load("pc", [128, 2], pc_d)
    dlp = f32load("dlp", [128, 4], dlp_d)
    dlt = f32load("dlt", [128, 8], dlt_d)
    dsk = f32load("dsk", [128, 4], dsk_d)
    dm = f32load("dm", [128, NCH, 16], dtm_d)
    xb16 = castload("xb16", [128, NCH * 256], xtm_d.rearrange("p n c -> p (n c)"), False)
    ktm = castload("ktmb", [128, NCH * 256], ktm_d.rearrange("p n c -> p (n c)"), False)
    vtm = castload("vtmb", [128, NCH * 256], vtm_d.rearrange("p n c -> p (n c)"), False)
    Btm = castload("Btmb", [128, NCH * 128], Btm_d.rearrange("p n c -> p (n c)"), False)
    qT = castload("qTb", [128, 2, SEQ], qT_d, True)
    kT = castload("kTb", [128, 2, SEQ], kT_d, True)
    BT = castload("BTb", [128, SEQ], BT_d, False)
    CT = castload("CTb", [128, SEQ], CT_d, False)
    TRI_LE, TRI_GE, TRI_GT, TRI_LT, DPOS, DNEG, IDX1, IDXB, ONESM = 0, 1, 2, 3, 5, 6, 7, 8, 9

    one1 = kb.sb("one1", [128, 1], F32)
    kb.op("dve", lambda: V.memset(one1[:], 1.0), writes=[one1])

    def logsig(t, n):
        kb.op("dve", lambda: V.tensor_scalar(out=t[:, 0:n], in0=t[:, 0:n], scalar1=-1.0, scalar2=None, op0=ALU.mult), reads=[t], writes=[t])
        kb.op("act", lambda: A.activation(out=t[:, 0:n], in_=t[:, 0:n], func=AF.Exp), reads=[t], writes=[t])
        kb.op("dve", lambda: V.tensor_scalar(out=t[:, 0:n], in0=t[:, 0:n], scalar1=1.0, scalar2=None, op0=ALU.add), reads=[t], writes=[t])
        kb.op("act", lambda: A.activation(out=t[:, 0:n], in_=t[:, 0:n], func=AF.Ln), reads=[t], writes=[t])
        kb.op("dve", lambda: V.tensor_scalar(out=t[:, 0:n], in0=t[:, 0:n], scalar1=-1.0, scalar2=None, op0=ALU.mult), reads=[t], writes=[t])

    logsig(dlp, 4)
    logsig(dlt, 8)
    MASK = kb.sb("MASK", [128, 4, 128], F32)
    tmpm = kb.sb("tmpm", [128, 128], F32)
    for h in range(4):
        kb.op("dve", lambda: V.tensor_scalar(out=tmpm[:], in0=cst[:, DPOS, :], scalar1=dlt[:, h:h + 1], scalar2=None, op0=ALU.mult),
              reads=[cst, dlt], writes=[tmpm])
        kb.op("dve", lambda: V.scalar_tensor_tensor(out=tmpm[:], in0=cst[:, DNEG, :], scalar=dlt[:, 4 + h:5 + h], in1=tmpm[:],
                                                    op0=ALU.mult, op1=ALU.add), reads=[cst, dlt, tmpm], writes=[tmpm])
        kb.op("act", lambda: A.activation(out=MASK[:, h, :], in_=tmpm[:], func=AF.Exp), reads=[tmpm], writes=[MASK])
    GF = kb.sb("GF", [128, 2, 128], F32)
    GB = kb.sb("GB", [128, 2, 128], F32)
    for t in range(2):
        kb.op("dve", lambda: V.tensor_scalar(out=GF[:, t, :], in0=cst[:, IDX1, :], scalar1=dlp[:, 2 * t:2 * t + 1], scalar2=None, op0=ALU.mult),
              reads=[cst, dlp], writes=[GF])
        kb.op("dve", lambda: V.tensor_scalar(out=GB[:, t, :], in0=cst[:, IDXB, :], scalar1=dlp[:, 2 * t + 1:2 * t + 2], scalar2=None, op0=ALU.mult),
              reads=[cst, dlp], writes=[GB])
    kb.op("act", lambda: A.activation(out=GF[:], in_=GF[:], func=AF.Exp), reads=[GF], writes=[GF])
    kb.op("act", lambda: A.activation(out=GB[:], in_=GB[:], func=AF.Exp), reads=[GB], writes=[GB])
    WFB = kb.sb("WFB", [128, 8], F32)
    kb.op("dve", lambda: V.tensor_scalar(out=WFB[:, 0:4], in0=dlt[:, 0:4], scalar1=pc[:, 0:1], scalar2=None, op0=ALU.mult), reads=[dlt, pc], writes=[WFB])
    kb.op("dve", lambda: V.tensor_scalar(out=WFB[:, 4:8], in0=dlt[:, 4:8], scalar1=pc[:, 1:2], scalar2=None, op0=ALU.mult), reads=[dlt, pc], writes=[WFB])
    kb.op("act", lambda: A.activation(out=WFB[:], in_=WFB[:], func=AF.Exp), reads=[WFB], writes=[WFB])
    TOTP = kb.sb("TOTP", [128, 4], F32)
    kb.op("dve", lambda: V.tensor_scalar(out=TOTP[:], in0=dlp[:], scalar1=128.0, scalar2=None, op0=ALU.mult), reads=[dlp], writes=[TOTP])
    kb.op("act", lambda: A.activation(out=TOTP[:], in_=TOTP[:], func=AF.Exp), reads=[TOTP], writes=[TOTP])

    SR = [[kb.sb("SR%d%d" % (d, t), [128, 64], F32) for t in range(2)] for d in range(2)]
    SS = [kb.sb("SS%d" % d, [128, 256], F32) for d in range(2)]
    for d in range(2):
        for t in range(2):
            kb.op("dve", lambda: V.memset(SR[d][t][:], 0.0), writes=[SR[d][t]])
        kb.op("dve", lambda: V.memset(SS[d][:], 0.0), writes=[SS[d]])
    SRB = kb.sb("SRB", [128, NCH, 2, 64], BF16)
    SSB = kb.sb("SSB", [128, NCH, 256], BF16)
    SRFb = kb.sb("SRFb", [128, 2, 64], BF16)
    SSFb = kb.sb("SSFb", [128, 256], BF16)

    p_dec = kb.ps("p_dec", [128, 512])
    p_arg = kb.ps("p_arg", [128, 512])
    p_st = kb.ps("p_st", [128, 512])
    p_y = kb.ps("p_y", [128, 512])
    p_yfb = kb.ps("p_yfb", [128, 512])
    p_r = kb.ps("p_r", [128, 512])
    p_yr = kb.ps("p_yr", [128, 512])
    p_kv = kb.ps("p_kv", [128, 512])
    dec = kb.sb("dec", [128, 32], F32)
    vw = kb.sb("vw", [128, 256], BF16)
    xw = kb.sb("xw", [128, 256], BF16)

    decA = kb.sb("decA", [128, NCH, 32], F32)
    kb.op("pe", lambda: [PE.matmul(p_dec[:, 0:NCH * 4], lhsT=cst[:, TRI_LE, :], rhs=dm[:, :, 8:12], start=True, stop=True),
                         PE.matmul(p_dec[:, NCH * 4:NCH * 8], lhsT=cst[:, TRI_GE, :], rhs=dm[:, :, 12:16], start=True, stop=True)],
          reads=[cst, dm], writes=[p_dec])
    kb.op("pe", lambda: PE.matmul(p_arg[:, 0:NCH * 8], lhsT=cst[:, ONESM, :], rhs=dm[:, :, 8:16], start=True, stop=True),
          reads=[cst, dm], writes=[p_arg])
    kb.op("dve", lambda: V.tensor_copy(out=decA[:, :, 0:4], in_=p_dec[:, 0:NCH * 4].rearrange("p (n c) -> p n c", c=4)), reads=[p_dec], writes=[decA])
    kb.op("dve", lambda: V.tensor_copy(out=decA[:, :, 4:8], in_=p_dec[:, NCH * 4:NCH * 8].rearrange("p (n c) -> p n c", c=4)), reads=[p_dec], writes=[decA])
    kb.op("dve", lambda: V.tensor_copy(out=decA[:, :, 8:16], in_=p_arg[:, 0:NCH * 8].rearrange("p (n c) -> p n c", c=8)), reads=[p_arg], writes=[decA])
    kb.op("dve", lambda: V.tensor_tensor(out=decA[:, :, 16:24], in0=decA[:, :, 8:16], in1=decA[:, :, 0:8], op=ALU.subtract), reads=[decA], writes=[decA])
    kb.op("act", lambda: A.activation(out=decA[:, :, 0:24], in_=decA[:, :, 0:24], func=AF.Exp), reads=[decA], writes=[decA])
    kb.op("dve", lambda: V.tensor_tensor(out=decA[:, :, 24:32], in0=decA[:, :, 16:24], in1=dm[:, :, 0:8], op=ALU.mult), reads=[decA, dm], writes=[decA])

    def decays(n):
        pass

    vws = [vw, kb.sb("vw2", [128, 256], BF16)]
    xws = [xw, kb.sb("xw2", [128, 256], BF16)]
    pkvs = [p_kv, p_dec]
    kvcnt = [0]

    def kv_part(n, d):
        i = kvcnt[0] % 2
        kvcnt[0] += 1
        vw_, xw_, pk = vws[i], xws[i], pkvs[i]
        wcol = WFB[:, 4 * d:4 * d + 4]
        kb.op("dve", lambda: V.tensor_tensor(out=vw_[:].rearrange("p (h d) -> p h d", h=4),
                                             in0=vtm[:, n * 256:(n + 1) * 256].rearrange("p (h d) -> p h d", h=4),
                                             in1=bc(wcol, 2, [128, 4, 64]), op=ALU.mult), reads=[vtm, WFB], writes=[vw_])
        wcs = decA[:, n, 24 + 4 * d:28 + 4 * d]
        kb.op("dve", lambda: V.tensor_tensor(out=xw_[:].rearrange("p (h d) -> p h d", h=4),
                                             in0=xb16[:, n * 256:(n + 1) * 256].rearrange("p (h d) -> p h d", h=4),
                                             in1=bc(wcs, 2, [128, 4, 64]), op=ALU.mult), reads=[xb16, decA], writes=[xw_])
        kb.op("pe", lambda: [PE.matmul(pk[:, 0:128], lhsT=ktm[:, n * 256:n * 256 + 128], rhs=vw_[:, 0:128], start=True, stop=True),
                             PE.matmul(pk[:, 128:256], lhsT=ktm[:, n * 256 + 128:n * 256 + 256], rhs=vw_[:, 128:256], start=True, stop=True),
                             PE.matmul(pk[:, 256:512], lhsT=Btm[:, n * 128:(n + 1) * 128], rhs=xw_[:], start=True, stop=True)],
              reads=[ktm, vw_, Btm, xw_], writes=[pk])
        return pk

    def upd_part(n, d, pk):
        for t in range(2):
            for u in range(2):
                rows = slice(u * 64, (u + 1) * 64)
                cc = t * 128 + u * 64
                kb.op("dve", lambda: V.scalar_tensor_tensor(out=SR[d][t][rows, :], in0=SR[d][t][rows, :], scalar=TOTP[rows, 2 * t + d:2 * t + d + 1],
                                                            in1=pk[rows, cc:cc + 64], op0=ALU.mult, op1=ALU.add),
                      reads=[SR[d][t], TOTP, pk], writes=[SR[d][t]])
        tot = decA[:, n, 8 + 4 * d:12 + 4 * d]
        kb.op("dve", lambda: V.tensor_tensor(out=SS[d][:].rearrange("p (h d) -> p h d", h=4), in0=SS[d][:].rearrange("p (h d) -> p h d", h=4),
                                             in1=bc(tot, 2, [128, 4, 64]), op=ALU.mult), reads=[SS[d], decA], writes=[SS[d]])
        kb.op("dve", lambda: V.tensor_tensor(out=SS[d][:], in0=SS[d][:], in1=pk[:, 256:512], op=ALU.add), reads=[SS[d], pk], writes=[SS[d]])

    import os
    STOP = ''
    if STOP == 'setup':
        return
    pk = kv_part(BWD_ORDER[0], 1)
    for idx, n in enumerate(BWD_ORDER):
        for t in range(2):
            kb.op("act", lambda: A.copy(out=SRB[:, n, t, :], in_=SR[1][t][:]), reads=[SR[1][t]], writes=[SRB])
        kb.op("act", lambda: A.copy(out=SSB[:, n, :], in_=SS[1][:]), reads=[SS[1]], writes=[SSB])
        pk_next = kv_part(BWD_ORDER[idx + 1], 1) if idx + 1 < NCH else None
        upd_part(n, 1, pk)
        pk = pk_next

    if STOP == 'sweep1':
        return
    rhsF = kb.sb("rhsF", [128, 4, 128], F32)
    rhsB = kb.sb("rhsB", [128, 4, 128], F32)
    Eex = kb.sb("Eex", [128, 4, 128], F32)
    t1 = kb.sb("t1", [128, 4, 128], F32)
    ddt = kb.sb("ddt", [128, 4], F32)
    Pm = kb.sb("Pm", [128, 4, 128], BF16)
    Pr = kb.sb("Pr", [128, 4, 128], BF16)
    qwF = kb.sb("qwF", [128, 2, 128], BF16)
    qwB = kb.sb("qwB", [128, 2, 128], BF16)
    yo = [kb.sb("yo%d" % i, [128, 256], F32, dma=True) for i in range(2)]
    yr = [kb.sb("yr%d" % i, [128, 256], F32, dma=True) for i in range(2)]
    ya = kb.sb("ya", [128, 256], F32)
    qz = kb.sb("qz", [128, 4, 128], BF16)
    SRFz = kb.sb("SRFz", [128, 4, 64], BF16)
    SRBz = kb.sb("SRBz", [128, 4, 64], BF16)
    rmask = kb.sb("rmask", [128, 2], F32)
    kb.op("dve", lambda: V.memset(SRFz[:], 0.0), writes=[SRFz])
    kb.op("dve", lambda: V.memset(SRBz[:], 0.0), writes=[SRBz])
    kb.op("dve", lambda: V.memset(rmask[:], 0.0), writes=[rmask])
    kb.op("dve", lambda: V.memset(rmask[0:64, 0:1], 1.0), writes=[rmask])
    kb.op("dve", lambda: V.memset(rmask[64:128, 1:2], 1.0), writes=[rmask])
    S2 = 'ssd,ret,upd'
    def sweep2():
        for n in range(NCH):
            cs_ = slice(n * 128, (n + 1) * 128)
            pk = kv_part(n, 0)
            kb.op("act", lambda: A.copy(out=SSFb[:], in_=SS[0][:]), reads=[SS[0]], writes=[SSFb])
            ssdA(n, cs_)
            retA(n, cs_)
            ssdB(n, cs_)
            ssdC(n, cs_)
            retB(n, cs_)
            ssdD(n, cs_)
            retC(n, cs_)
            upd_part(n, 0, pk)

    def ssdA(n, cs_):
        kb.op("dve", lambda: V.tensor_tensor(out=rhsF[:], in0=bc(cst[:, TRI_LE, :], 1, [128, 4, 128]), in1=bc(dm[:, n, 8:12], 2, [128, 4, 128]),
                                             op=ALU.mult), reads=[cst, dm], writes=[rhsF])
        kb.op("dve", lambda: V.tensor_tensor(out=rhsB[:], in0=bc(cst[:, TRI_GE, :], 1, [128, 4, 128]), in1=bc(dm[:, n, 12:16], 2, [128, 4, 128]),
                                             op=ALU.mult), reads=[cst, dm], writes=[rhsB])
        kb.op("pe", lambda: [PE.matmul(p_arg[:], lhsT=cst[:, TRI_GT, :], rhs=rhsF[:].rearrange("p h i -> p (h i)"), start=True, stop=False),
                             PE.matmul(p_arg[:], lhsT=cst[:, TRI_LT, :], rhs=rhsB[:].rearrange("p h i -> p (h i)"), start=False, stop=True)],
              reads=[cst, rhsF, rhsB], writes=[p_arg])

    def ssdB(n, cs_):
        kb.op("act", lambda: A.activation(out=Eex[:].rearrange("p h i -> p (h i)"), in_=p_arg[:], func=AF.Exp), reads=[p_arg], writes=[Eex])
        kb.op("dve", lambda: V.tensor_tensor(out=ddt[:], in0=dm[:, n, 0:4], in1=dm[:, n, 4:8], op=ALU.subtract), reads=[dm], writes=[ddt])
        kb.op("dve", lambda: V.tensor_tensor(out=t1[:], in0=bc(cst[:, TRI_LE, :], 1, [128, 4, 128]), in1=bc(ddt[:], 2, [128, 4, 128]), op=ALU.mult),
              reads=[cst, ddt], writes=[t1])
        kb.op("dve", lambda: V.tensor_tensor(out=t1[:], in0=t1[:], in1=bc(dm[:, n, 4:8], 2, [128, 4, 128]), op=ALU.add), reads=[t1, dm], writes=[t1])
        kb.op("pe", lambda: PE.matmul(p_st[:, 0:128], lhsT=BT[:, cs_], rhs=CT[:, cs_], start=True, stop=True), reads=[BT, CT], writes=[p_st])

    def ssdC(n, cs_):
        kb.op("dve", lambda: V.tensor_tensor(out=t1[:], in0=t1[:], in1=Eex[:], op=ALU.mult), reads=[t1, Eex], writes=[t1])
        kb.op("dve", lambda: V.tensor_tensor(out=Pm[:], in0=t1[:], in1=bc(p_st[:, 0:128], 1, [128, 4, 128]), op=ALU.mult), reads=[t1, p_st], writes=[Pm])
        kb.op("pe", lambda: [PE.matmul(p_y[:, h * 64:(h + 1) * 64], lhsT=Pm[:, h, :], rhs=xb16[:, n * 256 + h * 64:n * 256 + (h + 1) * 64],
                                       start=True, stop=True) for h in range(4)], reads=[Pm, xb16], writes=[p_y])
        kb.op("pe", lambda: [PE.matmul(p_yfb[:, 0:256], lhsT=CT[:, cs_], rhs=SSFb[:], start=True, stop=True),
                             PE.matmul(p_yfb[:, 256:512], lhsT=CT[:, cs_], rhs=SSB[:, n, :], start=True, stop=True)],
              reads=[CT, SSFb, SSB], writes=[p_yfb])

    def ssdD(n, cs_):
        o = yo[n % 2]
        v3 = lambda ap: ap.rearrange("p (h d) -> p h d", h=4)
        kb.op("dve", lambda: V.tensor_tensor(out=v3(o[:]), in0=v3(p_yfb[:, 0:256]), in1=bc(decA[:, n, 0:4], 2, [128, 4, 64]), op=ALU.mult),
              reads=[p_yfb, decA], writes=[o])
        kb.op("dve", lambda: V.tensor_tensor(out=v3(ya[:]), in0=v3(p_yfb[:, 256:512]), in1=bc(decA[:, n, 4:8], 2, [128, 4, 64]), op=ALU.mult),
              reads=[p_yfb, decA], writes=[ya])
        kb.op("dve", lambda: V.tensor_tensor(out=o[:], in0=o[:], in1=ya[:], op=ALU.add), reads=[o, ya], writes=[o])
        kb.op("dve", lambda: V.tensor_tensor(out=o[:], in0=o[:], in1=p_y[:, 0:256], op=ALU.add), reads=[o, p_y], writes=[o])
        kb.op("dve", lambda: V.tensor_tensor(out=v3(ya[:]), in0=v3(xb16[:, n * 256:(n + 1) * 256]), in1=bc(dsk[:], 2, [128, 4, 64]), op=ALU.mult),
              reads=[xb16, dsk], writes=[ya])
        kb.op("dve", lambda: V.tensor_tensor(out=o[:], in0=o[:], in1=ya[:], op=ALU.add), reads=[o, ya], writes=[o])
        kb.store(yssd, yssd[n], o, o[:])

    def retA(n, cs_):
        for h in range(4):
            kb.op("dve", lambda: V.tensor_scalar(out=qz[:, h, :], in0=qT[:, h // 2, cs_], scalar1=rmask[:, h % 2:h % 2 + 1], scalar2=None, op0=ALU.mult),
                  reads=[qT, rmask], writes=[qz])
        kb.op("pe", lambda: [PE.matmul(p_r[:, h * 128:(h + 1) * 128], lhsT=kT[:, h // 2, cs_], rhs=qz[:, h, :], start=True, stop=True)
                             for h in range(4)], reads=[kT, qz], writes=[p_r])
        kb.op("dve", lambda: V.tensor_tensor(out=qwF[:], in0=qT[:, :, cs_], in1=GF[:], op=ALU.mult), reads=[qT, GF], writes=[qwF])
        kb.op("dve", lambda: V.tensor_tensor(out=qwB[:], in0=qT[:, :, cs_], in1=GB[:], op=ALU.mult), reads=[qT, GB], writes=[qwB])
        for h in range(4):
            rows = slice((h % 2) * 64, (h % 2) * 64 + 64)
            kb.op("act", lambda: A.copy(out=SRFz[rows, h, :], in_=SR[0][h // 2][rows, :]), reads=[SR[0][h // 2]], writes=[SRFz])
            kb.op("act", lambda: A.copy(out=SRBz[rows, h, :], in_=SRB[rows, n, h // 2, :]), reads=[SRB], writes=[SRBz])

    def retB(n, cs_):
        kb.op("dve", lambda: V.tensor_tensor(out=Pr[:].rearrange("p h i -> p (h i)"), in0=p_r[:], in1=MASK[:].rearrange("p h i -> p (h i)"),
                                             op=ALU.mult), reads=[p_r, MASK], writes=[Pr])

        def retmm():
            ins = []
            for h in range(4):
                oc = p_yr[:, h * 64:(h + 1) * 64]
                ins.append(PE.matmul(oc, lhsT=Pr[:, h, :], rhs=vtm[:, n * 256 + h * 64:n * 256 + (h + 1) * 64], start=True, stop=False))
                ins.append(PE.matmul(oc, lhsT=qwF[:, h // 2, :], rhs=SRFz[:, h, :], start=False, stop=False))
                ins.append(PE.matmul(oc, lhsT=qwB[:, h // 2, :], rhs=SRBz[:, h, :], start=False, stop=True))
            return ins
        kb.op("pe", retmm, reads=[Pr, vtm, qwF, qwB, SRFz, SRBz], writes=[p_yr])

    def retC(n, cs_):
        r_ = yr[n % 2]
        kb.op("act", lambda: A.copy(out=r_[:], in_=p_yr[:, 0:256]), reads=[p_yr], writes=[r_])
        kb.store(yret, yret[n], r_, r_[:])

    sweep2()


def k2_consts():
    j = np.arange(128)[:, None].astype(np.float32)
    i = np.arange(128)[None, :].astype(np.float32)
    c = np.zeros((128, 10, 128), np.float32)
    c[:, 0] = (j <= i)
    c[:, 1] = (j >= i)
    c[:, 2] = (j > i)
    c[:, 3] = (j < i)
    c[:, 5] = np.maximum(i - j, 0)
    c[:, 6] = np.maximum(j - i, 0)
    c[:, 7] = np.broadcast_to(i + 1, (128, 128))
    c[:, 8] = np.broadcast_to(128 - i, (128, 128))
    c[:, 9] = 1.0
    pc = np.stack([127 - np.arange(128), np.arange(128)], 1).astype(np.float32)
    return c, pc


def tm_chunks(x_tok):
    n, cdim = x_tok.shape
    return c32(x_tok.reshape(n // 128, 128, cdim).transpose(1, 0, 2))


def run_k2(p_lat, p_ctx, prep, decay_logit, d_skip):
    cst, pc = k2_consts()
    maps = []
    for r in range(8):
        b, s = r // 2, r % 2
        qko, xo, dto = prep[r]
        hs = slice(4 * s, 4 * s + 4)
        P = np.concatenate([p_ctx[b], p_lat[b]], 0)
        q = np.concatenate([qko[:, 0].reshape(4, 32, SEQ), qko[:, 1].reshape(4, 32, SEQ)], 1)
        k = np.concatenate([qko[:, 2].reshape(4, 32, SEQ), qko[:, 3].reshape(4, 32, SEQ)], 1)
        qT = q.reshape(2, 128, SEQ).transpose(1, 0, 2)
        kT = k.reshape(2, 128, SEQ).transpose(1, 0, 2)
        ktm = tm_chunks(k.reshape(256, SEQ).T)
        v = P[:, 1024:1536].reshape(SEQ, 8, 64)[:, hs].reshape(SEQ, 256)
        xsT = xo[:, 0:2].transpose(1, 0, 2).reshape(256, SEQ)
        dtm = np.concatenate([dto[:, 0], dto[:, 1]], 0).T
        dl = decay_logit[:, hs]
        dlp = np.zeros((128, 4), np.float32)
        for t in range(2):
            for d in range(2):
                dlp[:64, 2 * t + d] = dl[d, 2 * t]
                dlp[64:, 2 * t + d] = dl[d, 2 * t + 1]
        dlt = np.broadcast_to(dl.reshape(8), (128, 8))
        dsk = np.broadcast_to(d_skip[hs], (128, 4))
        maps.append({"qT": c32(qT), "kT": c32(kT), "ktm": ktm, "vtm": tm_chunks(v), "BT": c32(xo[:, 2]), "CT": c32(xo[:, 3]),
                     "Btm": tm_chunks(xo[:, 2].T), "xtm": tm_chunks(xsT.T), "dtm": tm_chunks(dtm), "cst": cst, "pc": pc,
                     "dlp": dlp, "dlt": c32(dlt), "dsk": c32(dsk)})
    res = launch(build_k2, maps)
    yr = np.zeros((B, SEQ, 512), np.float32)
    ys = np.zeros((B, SEQ, 512), np.float32)
    for r in range(8):
        b, s = r // 2, r % 2
        yr[b, :, 256 * s:256 * s + 256] = res[r]["yret"].reshape(SEQ, 256)
        ys[b, :, 256 * s:256 * s + 256] = res[r]["yssd"].reshape(SEQ, 256)
    return yr, ys


def build_k4(nc, kb):
    xT = kb.dram_in("xT", [128, 8, NTOK])
    mod = kb.dram_in("mod", [128, 96])
    w_in = kb.dram_in("w_in", [1024, IN1])
    w_uq = kb.dram_in("w_uq", [768, 1536])
    w_ukv = kb.dram_in("w_ukv", [256, 2048])
    nw = kb.dram_in("nw", [128, 8])
    out_q = kb.dram_out("qT", [12, 128, NTOK])
    out_kv = kb.dram_out("kvT", [16, 128, NTOK])
    out_pe = kb.dram_out("peT", [32, NTOK])
    fmh = FM(nc, kb)
    modt = kb.sb("modt", [128, 96], F32, dma=True)
    s1p = kb.sb("s1p", [128, 96], F32)
    nwt = kb.sb("nwt", [128, 8], F32, dma=True)
    kb.load(modt, modt[:], mod)
    kb.load(nwt, nwt[:], nw)
    kb.op("dve", lambda: nc.vector.tensor_scalar(out=s1p[:], in0=modt[:], scalar1=1.0, scalar2=None, op0=ALU.add),
          reads=[modt], writes=[s1p])
    wt = kb.sb("wt", [128, 8, 9 * 128], BF16, dma=True)
    kb.op("dve", lambda: nc.vector.memset(wt[:, :, IN1:9 * 128], 0.0), writes=[wt])
    load_weight_bf16(nc, kb, wt, w_in, 8, IN1)
    wq = kb.sb("wq", [128, 6, 1536], BF16, dma=True)
    load_weight_bf16(nc, kb, wq, w_uq, 6, 1536)
    wkv = kb.sb("wkv", [128, 2, 2048], BF16, dma=True)
    load_weight_bf16(nc, kb, wkv, w_ukv, 2, 2048)
    xts = [kb.sb("xt%d" % i, [128, 8, 512], F32, dma=True) for i in range(2)]
    sq = kb.sb("sq", [128, 8, 512], BF16)
    rs = kb.sb("rs", [128, 512], F32)
    tmp = kb.sb("tmp", [128, 8, 512], F32)
    at = kb.sb("at", [128, 8, 512], BF16)
    cq = kb.sb("cq", [128, 9, 512], F32, dma=True)
    cns = [kb.sb("cn%d" % i, [128, 8, 512], BF16) for i in range(2)]
    psn = kb.ps("psn", [128, 512])
    pss = [kb.ps("ps%d" % i, [128, 512]) for i in range(4)]
    ots = [kb.sb("ot%d" % i, [128, 512], F32, dma=True) for i in range(4)]
    cnt = [0]

    def front(gi):
        t0, n, col = GROUPS[gi]
        xt = xts[gi % 2]
        cn = cns[gi % 2]
        th = [lambda: kb.load(xt, xt[:, :, 0:n], xT[:, :, t0:t0 + n], q="pool")]
        th += _norm_mod_thunks(fmh, xt, n, psn, sq, rs, tmp, at, s1p, modt, 1, 0, col)
        for m in range(9):
            def mm(m=m):
                ps = pss[cnt[0] % 4]
                cnt[0] += 1
                kb.op("pe", lambda: [nc.tensor.matmul(ps[:, 0:n], lhsT=wt[:, k, m * 128:(m + 1) * 128], rhs=at[:, k, 0:n],
                                                       start=(k == 0), stop=(k == 7)) for k in range(8)],
                      reads=[wt, at], writes=[ps])
                if m % 2 == 0:
                    kb.op("act", lambda: nc.scalar.copy(out=cq[:, m, 0:n], in_=ps[:, 0:n]), reads=[ps], writes=[cq])
                else:
                    kb.op("dve", lambda: nc.vector.tensor_copy(out=cq[:, m, 0:n], in_=ps[:, 0:n]), reads=[ps], writes=[cq])
            th.append(mm)
        th.append(lambda: kb.store(out_pe, out_pe[:, t0:t0 + n], cq, cq[0:32, 8, 0:n]))
        for (k0, kc, nf) in ((0, 6, 768), (6, 2, 256)):
            th.append(lambda k0=k0, kc=kc, nf=nf: fmh.rstd_bc(cq, kc, n, psn, sq, rs, nf, k0=k0))
            for k in range(k0, k0 + kc):
                th.append(lambda k=k: kb.op("dve", lambda: nc.vector.scalar_tensor_tensor(
                    out=cn[:, k, 0:n], in0=cq[:, k, 0:n], scalar=nwt[:, k:k + 1], in1=rs[:, 0:n], op0=ALU.mult, op1=ALU.mult),
                    reads=[cq, rs, nwt], writes=[cn]))
        return th

    for t in front(0):
        t()
    for gi, (t0, n, col) in enumerate(GROUPS):
        cn = cns[gi % 2]
        nxt = front(gi + 1) if gi + 1 < len(GROUPS) else []
        for m in range(12 + 16):
            ps = pss[cnt[0] % 4]
            ot = ots[cnt[0] % 4]
            cnt[0] += 1
            if m < 12:
                kb.op("pe", lambda: [nc.tensor.matmul(ps[:, 0:n], lhsT=wq[:, k, m * 128:(m + 1) * 128], rhs=cn[:, k, 0:n],
                                                       start=(k == 0), stop=(k == 5)) for k in range(6)],
                      reads=[wq, cn], writes=[ps])
            else:
                mm_ = m - 12
                kb.op("pe", lambda: [nc.tensor.matmul(ps[:, 0:n], lhsT=wkv[:, k, mm_ * 128:(mm_ + 1) * 128], rhs=cn[:, 6 + k, 0:n],
                                                       start=(k == 0), stop=(k == 1)) for k in range(2)],
                      reads=[wkv, cn], writes=[ps])
            if m % 2 == 0:
                kb.op("act", lambda: nc.scalar.copy(out=ot[:, 0:n], in_=ps[:, 0:n]), reads=[ps], writes=[ot])
            else:
                kb.op("dve", lambda: nc.vector.tensor_copy(out=ot[:, 0:n], in_=ps[:, 0:n]), reads=[ps], writes=[ot])
            if m < 12:
                kb.store(out_q, out_q[m, :, t0:t0 + n], ot, ot[:, 0:n])
            else:
                kb.store(out_kv, out_kv[m - 12, :, t0:t0 + n], ot, ot[:, 0:n])
            for _ in range(2):
                if nxt:
                    nxt.pop(0)()
        for t in nxt:
            t()


def run_k4(h_lat, h_ctx, m1, inp):
    maps = []
    nw = c32(np.concatenate([vec_fm(inp["mla_q_norm_w"][0]), vec_fm(inp["mla_kv_norm_w"][0])], 1))
    for r in range(8):
        b, s = r // 2, r % 2
        maps.append({"xT": fm(core_rows(h_lat[b], h_ctx[b], s)), "mod": mod_pack(m1, b), "w_in": c32(inp["mla_w_in"][0]),
                     "w_uq": c32(inp["mla_w_uq"][0]), "w_ukv": c32(inp["mla_w_ukv"][0]), "nw": nw})
    res = launch(build_k4, maps)
    q = np.zeros((B, SEQ, 1536), np.float32)
    kv = np.zeros((B, SEQ, 2048), np.float32)
    pe = np.zeros((B, SEQ, 32), np.float32)
    for r in range(8):
        b, s = r // 2, r % 2
        for arr, name, f in ((q, "qT", 1536), (kv, "kvT", 2048), (pe, "peT", 32)):
            t = res[r][name].reshape(f, NTOK).T
            arr[b, CTX + s * 2048:CTX + (s + 1) * 2048] = t[:2048]
            arr[b, s * 128:(s + 1) * 128] = t[2048:]
    return q, kv, pe


def build_rope(nc, kb):
    qk = kb.dram_in("qk", [NBLK, 128, 4, 256])
    cs = kb.dram_in("cs", [NBLK, 128, 2, 256])
    qko = kb.dram_out("qko", [NBLK, 128, 4, 256])
    qkt = [kb.sb("qkt%d" % i, [128, 4, 256], F32, dma=True) for i in range(2)]
    cst = [kb.sb("cst%d" % i, [128, 2, 256], F32, dma=True) for i in range(2)]
    qo = [kb.sb("qo%d" % i, [128, 4, 256], F32, dma=True) for i in range(2)]
    ta = kb.sb("ta", [128, 256], F32)
    tb = kb.sb("tb", [128, 256], F32)
    V = nc.vector
    for blk in range(NBLK):
        i2 = blk % 2
        a, c_, o_ = qkt[i2], cst[i2], qo[i2]
        kb.load(a, a[:], qk[blk], q="pool")
        kb.load(c_, c_[:], cs[blk], q="pool")
        for pair in range(2):
            x1 = a[:, 2 * pair, :]
            x2 = a[:, 2 * pair + 1, :]
            cos, sin = c_[:, 0, :], c_[:, 1, :]
            kb.op("dve", lambda: V.tensor_tensor(out=ta[:], in0=x1, in1=cos, op=ALU.mult), reads=[a, c_], writes=[ta])
            kb.op("dve", lambda: V.tensor_tensor(out=tb[:], in0=x2, in1=sin, op=ALU.mult), reads=[a, c_], writes=[tb])
            kb.op("dve", lambda: V.tensor_tensor(out=o_[:, 2 * pair, :], in0=ta[:], in1=tb[:], op=ALU.subtract), reads=[ta, tb], writes=[o_])
            kb.op("dve", lambda: V.tensor_tensor(out=ta[:], in0=x1, in1=sin, op=ALU.mult), reads=[a, c_], writes=[ta])
            kb.op("dve", lambda: V.tensor_tensor(out=tb[:], in0=x2, in1=cos, op=ALU.mult), reads=[a, c_], writes=[tb])
            kb.op("dve", lambda: V.tensor_tensor(out=o_[:, 2 * pair + 1, :], in0=ta[:], in1=tb[:], op=ALU.add), reads=[ta, tb], writes=[o_])
        kb.store(qko, qko[blk], o_, o_[:])


def run_rope_mla(q, pe):
    cos, sin = rope_tables(32)
    maps = []
    for r in range(8):
        b, s = r // 2, r % 2
        qpe = q[b].reshape(SEQ, 16, 96)[:, 8 * s:8 * s + 8, 64:96]
        q1 = qpe[:, :, :16].reshape(SEQ, 128).T
        q2 = qpe[:, :, 16:].reshape(SEQ, 128).T
        k1 = np.tile(pe[b][:, :16], (1, 8)).T
        k2 = np.tile(pe[b][:, 16:], (1, 8)).T
        qk = np.stack([blocks(a) for a in (q1, q2, k1, k2)], axis=2)
        ct = np.tile(cos, (1, 8)).T
        st = np.tile(sin, (1, 8)).T
        cs = np.stack([blocks(ct), blocks(st)], axis=2)
        maps.append({"qk": c32(qk), "cs": c32(cs)})
    res = launch(build_rope, maps)
    q_pe = np.zeros((B, SEQ, 16, 32), np.float32)
    k_pe = np.zeros((B, SEQ, 32), np.float32)
    for r in range(8):
        b, s = r // 2, r % 2
        o = res[r]["qko"].transpose(1, 2, 0, 3).reshape(128, 4, SEQ)
        q_pe[b, :, 8 * s:8 * s + 8, :16] = o[:, 0].reshape(8, 16, SEQ).transpose(2, 0, 1)
        q_pe[b, :, 8 * s:8 * s + 8, 16:] = o[:, 1].reshape(8, 16, SEQ).transpose(2, 0, 1)
        if s == 0:
            k_pe[b, :, :16] = o[0:16, 2].T
            k_pe[b, :, 16:] = o[0:16, 3].T
    return q_pe, k_pe


NKT = SEQ // 128
NQG = L // 512
ATT_SCALE = 96.0 ** -0.5


def build_k5(nc, kb):
    V, A, PE = nc.vector, nc.scalar, nc.tensor
    Qd = kb.dram_in("Q", [8, 128, L])
    Kd = kb.dram_in("K", [8, 128, SEQ])
    Vd = kb.dram_in("V", [8, 128, NKT, 128])
    out = kb.dram_out("O", [8, 64, L])
    qf = kb.sb("qf", [128, L], F32, dma=True)
    qb = [kb.sb("qb%d" % i, [128, L], BF16) for i in range(2)]
    kbt = [kb.sb("kb%d" % i, [128, SEQ], BF16, dma=True) for i in range(2)]
    vbt = [kb.sb("vb%d" % i, [128, NKT, 128], BF16, dma=True) for i in range(2)]
    pst = [kb.ps("pst%d" % i, [128, 3, 512]) for i in range(2)]
    pso = [kb.ps("pso%d" % i, [128, 512]) for i in range(2)]
    pts = [kb.sb("pt%d" % i, [128, 3, 512], BF16) for i in range(2)]
    rec = kb.sb("rec", [64, 512], F32)
    oss = [kb.sb("os%d" % i, [64, 512], F32, dma=True) for i in range(2)]
    KG = [(g0, min(3, NKT - g0)) for g0 in range(0, NKT, 3)]
    NG = len(KG)

    def prep_head(h):
        kt_, vt_, qb_ = kbt[h % 2], vbt[h % 2], qb[h % 2]
        kb.load(qf, qf[:], Qd[h])
        kb.op("pool", lambda: [nc.gpsimd.dma_start(out=kt_[:, c:min(SEQ, c + 2048)], in_=Kd[h, :, c:min(SEQ, c + 2048)])
                               for c in range(0, SEQ, 2048)], writes=[kt_], dma=kt_)
        kb.op("pool", lambda: [nc.gpsimd.dma_start(out=vt_[:, t0:min(NKT, t0 + 16), :], in_=Vd[h, :, t0:min(NKT, t0 + 16), :])
                               for t0 in range(0, NKT, 16)], writes=[vt_], dma=vt_)
        kb.op("dve", lambda: V.tensor_scalar(out=qb_[:], in0=qf[:], scalar1=ATT_SCALE, scalar2=None, op0=ALU.mult), reads=[qf], writes=[qb_])

    it = 0
    og = 0
    prep_head(0)
    for h in range(8):
        kt_, vt_, qb_ = kbt[h % 2], vbt[h % 2], qb[h % 2]
        if h + 1 < 8:
            prep_head(h + 1)
        for qg in range(NQG):
            qs = slice(qg * 512, (qg + 1) * 512)
            po = pso[og % 2]
            osb = oss[og % 2]
            og += 1

            def smm(g, ps):
                g0, gc = KG[g]
                kb.op("pe", lambda: [PE.matmul(ps[:, j, :], lhsT=kt_[:, (g0 + j) * 128:(g0 + j + 1) * 128], rhs=qb_[:, qs],
                                               start=True, stop=True) for j in range(gc)], reads=[kt_, qb_], writes=[ps])
            smm(0, pst[it % 2])
            for g in range(NG):
                g0, gc = KG[g]
                ps = pst[it % 2]
                pt = pts[it % 2]
                it += 1
                if g + 1 < NG:
                    smm(g + 1, pst[it % 2])
                kb.op("act", lambda: A.activation(out=pt[:, 0:gc, :], in_=ps[:, 0:gc, :], func=AF.Exp), reads=[ps], writes=[pt])
                kb.op("pe", lambda: [PE.matmul(po[:], lhsT=vt_[:, g0 + j, :], rhs=pt[:, j, :], start=(g == 0 and j == 0),
                                               stop=(g == NG - 1 and j == gc - 1)) for j in range(gc)], reads=[vt_, pt], writes=[po])
            kb.op("dve", lambda: V.reciprocal(out=rec[:], in_=po[64:128, :]), reads=[po], writes=[rec])
            kb.op("dve", lambda: V.tensor_tensor(out=osb[:], in0=po[0:64, :], in1=rec[:], op=ALU.mult), reads=[po, rec], writes=[osb])
            kb.store(out, out[h, :, qs], osb, osb[:])


def run_k5(q, kv, q_pe, k_pe):
    maps = []
    for r in range(8):
        b, s = r // 2, r % 2
        hs = slice(8 * s, 8 * s + 8)
        qn = q[b].reshape(SEQ, 16, 96)[CTX:, hs, 0:64]
        Q = np.zeros((8, 128, L), np.float32)
        Q[:, 0:32] = q_pe[b, CTX:, hs].transpose(1, 2, 0)
        Q[:, 32:96] = qn.transpose(1, 2, 0)
        kvh = kv[b].reshape(SEQ, 16, 128)[:, hs]
        K = np.zeros((8, 128, SEQ), np.float32)
        K[:, 0:32] = k_pe[b].T[None]
        K[:, 32:96] = kvh[:, :, 0:64].transpose(1, 2, 0)
        Vv = np.ones((8, 128, NKT, 128), np.float32)
        Vv[:, :, :, 0:64] = kvh[:, :, 64:128].reshape(NKT, 128, 8, 64).transpose(2, 1, 0, 3)
        maps.append({"Q": Q, "K": K, "V": Vv})
    res = launch(build_k5, maps)
    o = np.zeros((B, L, 1024), np.float32)
    for r in range(8):
        b, s = r // 2, r % 2
        o[b, :, 512 * s:512 * s + 512] = res[r]["O"].transpose(2, 0, 1).reshape(L, 512)
    return o


def core_rows(lat_b, ctx_b, s):
    return np.concatenate([lat_b[s * 2048:(s + 1) * 2048], ctx_b[s * 128:(s + 1) * 128]], axis=0)


def run_k3a(h_lat, h_ctx, m_l, w_out, finish=None, mix_lat=None):
    maps = []
    bo = np.zeros((128, 128), np.float32)
    bo[:64, :64] = 1.0 / 64
    bo[64:, 64:] = 1.0 / 64
    for r in range(8):
        b, s = r // 2, r % 2
        d = {"hT": fm(core_rows(h_lat[b], h_ctx[b], s)), "mod": mod_pack(m_l, b), "w": c32(w_out)}
        if finish is not None:
            f = finish
            d["yr"] = fm(core_rows(f["yr"][b, CTX:], f["yr"][b, :CTX], s))
            d["ys"] = fm(core_rows(f["ys"][b, CTX:], f["ys"][b, :CTX], s))
            d["gT"] = fm(core_rows(f["p_lat"][b][:, 1536:2048], f["p_ctx"][b][:, 1536:2048], s))
            d["zT"] = fm(core_rows(f["p_lat"][b][:, 2048:2560], f["p_ctx"][b][:, 2048:2560], s))
            d["nw"] = c32(np.concatenate([vec_fm(f["gn_w"]), vec_fm(f["ssd_norm_w"])], 1))
            d["bo"] = bo
        else:
            d["mix"] = fm(core_rows(mix_lat[b], np.zeros((CTX, 1024), np.float32), s))
        maps.append(d)
    res = launch(make_k3a(finish is not None), maps)
    return [res[r]["h1T"] for r in range(8)], [res[r]["a2T"] for r in range(8)]


def gather_tokens(oT_list):
    h_lat = np.zeros((B, L, D), np.float32)
    h_ctx = np.zeros((B, CTX, D), np.float32)
    for r in range(8):
        b, s = r // 2, r % 2
        t = unfm(oT_list[r])
        h_lat[b, s * 2048:(s + 1) * 2048] = t[:2048]
        h_ctx[b, s * 128:(s + 1) * 128] = t[2048:]
    return h_lat, h_ctx


def layer0(h_lat, h_ctx, m0, inp):
    p_lat, p_ctx = run_k1(h_lat, h_ctx, m0, inp["ret_ssd_w_in"][0])
    prep = run_k1b(p_lat, p_ctx, inp["ssd_conv_w"][0], inp["ssd_conv_b"][0], inp["ssd_dt_bias"][0], inp["ssd_a_log"][0])
    yr, ys = run_k2(p_lat, p_ctx, prep, inp["ret_decay_logit"][0], inp["ssd_d"][0])
    fin = dict(yr=yr, ys=ys, p_lat=p_lat, p_ctx=p_ctx, gn_w=inp["ret_gn_w"][0], ssd_norm_w=inp["ssd_norm_w"][0])
    o = run_k3ab(h_lat, h_ctx, m0, inp["ret_ssd_w_out"][0], inp["w_ffn_gate"][0], inp["w_ffn_up"][0], inp["w_ffn_down"][0], finish=fin)
    return gather_tokens(o)


def layer1(h_lat, h_ctx, m1, inp):
    q, kv, pe = run_k4(h_lat, h_ctx, m1, inp)
    q_pe, k_pe = run_rope_mla(q, pe)
    o = run_k5(q, kv, q_pe, k_pe)
    oT = run_k3ab(h_lat, h_ctx, m1, inp["mla_w_out"][0], inp["w_ffn_gate"][1], inp["w_ffn_up"][1], inp["w_ffn_down"][1],
                  mix_lat=o, final_w=inp["final_norm_w"])
    out_lat, _ = gather_tokens(oT)
    return out_lat


def kernel(**inputs):
    inp = {k: np.asarray(v) for k, v in inputs.items()}
    m = run_k0(inp["c"], inp["c_ctx"], inp["w_ada"], inp["b_ada"])
    h_lat, h_ctx = layer0(inp["x"], inp["ctx"], m[0], inp)
    out = layer1(h_lat, h_ctx, m[1], inp)
    return np.ascontiguousarray(out, dtype=np.float32)
```

```python
import numpy as np
from contextlib import ExitStack
import concourse.bass as bass
import concourse.mybir as mybir
from concourse.bass_utils import run_bass_kernel_spmd

F32 = mybir.dt.float32
BF16 = mybir.dt.bfloat16
AF = mybir.ActivationFunctionType
ALU = mybir.AluOpType
AX = mybir.AxisListType

D = 1024
B = 4
L = 4096
CTX = 256
DFF = 2816
EPS = 1e-6
NCORES = 8
IN0 = 3600
IN1 = 1056


class T:
    def __init__(self, ap, sem=None, name=""):
        self.ap = ap
        self.sem = sem
        self.w = []
        self.r = []
        self.name = name

    def __getitem__(self, idx):
        return self.ap[idx]


class KB:
    def __init__(self, nc, es):
        self.nc = nc
        self.es = es
        self.eng = dict(pe=nc.tensor, act=nc.scalar, dve=nc.vector, pool=nc.gpsimd, sp=nc.sync)
        self.sems = {}
        self.count = {}
        self.seen = {e: {} for e in self.eng}
        for e in ("pe", "act", "dve", "pool"):
            self._newsem("e_" + e)
        self._uid = 0
        self.outs = []
        self.pfx = ""
        self.alias = {}

    def _newsem(self, name):
        self.sems[name] = self.nc.alloc_semaphore(name=name)
        self.count[name] = 0
        return name

    def uid(self, p):
        self._uid += 1
        return "%s_%d" % (p, self._uid)

    def wrap(self, ap, name="", dma=False):
        sem = self._newsem(self.uid("d_" + name)) if dma else None
        return T(ap, sem, name)

    def sb(self, name, shape, dt, dma=False):
        name = self.pfx + name
        t = self.es.enter_context(self.nc.sbuf_tensor(name, list(shape), dt))
        return self.wrap(t, name, dma)

    def ps(self, name, shape, dt=F32):
        name = self.pfx + name
        t = self.es.enter_context(self.nc.psum_tensor(name, list(shape), dt))
        return self.wrap(t, name)

    def dram_in(self, name, shape, dt=F32):
        if name in self.alias:
            return self.alias[name]
        return self.nc.dram_tensor(self.pfx + name, list(shape), dt, kind="ExternalInput").ap()

    def dram_out(self, name, shape, dt=F32):
        name = self.pfx + name
        ap = self.nc.dram_tensor(name, list(shape), dt, kind="ExternalOutput").ap()
        t = self.wrap(ap, name)
        self.outs.append(t)
        return t

    def _wait(self, e, tok):
        sem, val = tok
        if self.seen[e].get(sem, 0) >= val:
            return
        self.eng[e].wait_ge(self.sems[sem], val)
        self.seen[e][sem] = val

    limit = None
    nlim = 0

    def op(self, e, fn, reads=(), writes=(), dma=None):
        if self.limit is not None:
            self.nlim += 1
            if self.nlim > self.limit:
                return None
        for b in reads:
            for tok in b.w:
                self._wait(e, tok)
        for b in writes:
            for tok in b.w:
                self._wait(e, tok)
            for tok in b.r:
                self._wait(e, tok)
        ins = fn()
        if dma is not None:
            if not isinstance(ins, (list, tuple)):
                ins = [ins]
            sem = dma.sem
            for i in ins:
                i.then_inc(self.sems[sem], 16)
                self.count[sem] += 16
        else:
            if isinstance(ins, (list, tuple)):
                ins = ins[-1]
            sem = "e_" + e
            ins.then_inc(self.sems[sem], 1)
            self.count[sem] += 1
        tok = (sem, self.count[sem])
        for b in reads:
            b.r.append(tok)
        for b in writes:
            b.w = [tok]
            b.r = []
        return tok

    def load(self, dst, dst_ap, src_ap, q="sp"):
        eng = self.eng[q]
        return self.op(q, lambda: eng.dma_start(out=dst_ap, in_=src_ap), writes=[dst], dma=dst)

    def store(self, dst_t, dst_ap, src, src_ap, q="sp"):
        eng = self.eng[q]
        return self.op(q, lambda: eng.dma_start(out=dst_ap, in_=src_ap), reads=[src], writes=[dst_t], dma=src)

    def barrier(self):
        for e in self.eng:
            for sem, cnt in self.count.items():
                if cnt > 0:
                    self._wait(e, (sem, cnt))

    def finish(self):
        for b in self.outs:
            for tok in b.w + b.r:
                self._wait("sp", tok)


N_LAUNCH = [0]


def launch(build, in_maps):
    nc = bass.Bass("TRN2", target_bir_lowering=False)
    with ExitStack() as es:
        kb = KB(nc, es)
        build(nc, kb)
        kb.finish()
    import os
    if os.environ.get("K_TRACE"):
        res = run_bass_kernel_spmd(nc, in_maps, core_ids=list(range(len(in_maps))), trace=True)
        print("K_TRACE", build.__name__, "exec_time_ns", res.exec_time_ns, flush=True)
    else:
        res = run_bass_kernel_spmd(nc, in_maps, core_ids=list(range(len(in_maps))))
    N_LAUNCH[0] += 1
    return res.results


def c32(a):
    return np.ascontiguousarray(a, dtype=np.float32)


def build_k0(nc, kb):
    cT = kb.dram_in("cT", [128, 8, 5])
    w = kb.dram_in("w", [1024, 1536])
    bias = kb.dram_in("bias", [5, 1536])
    out = kb.dram_out("m", [5, 1536])
    ct = kb.sb("ct", [128, 8, 5], F32, dma=True)
    cs = kb.sb("cs", [128, 8, 5], F32)
    wt = kb.sb("wt", [128, 8, 1536], F32, dma=True)
    bt = kb.sb("bt", [5, 1536], F32, dma=True)
    ot = kb.sb("ot", [5, 1536], F32, dma=True)
    kb.load(ct, ct[:], cT)
    kb.load(bt, bt[:], bias)
    wv = w.rearrange("(k p) n -> p k n", p=128)
    kb.op("sp", lambda: [nc.sync.dma_start(out=wt[:, k, :], in_=wv[:, k, :]) for k in range(8)], writes=[wt], dma=wt)
    kb.op("act", lambda: nc.scalar.activation(out=cs[:], in_=ct[:], func=AF.Silu), reads=[ct], writes=[cs])
    pss = [kb.ps("ps%d" % i, [128, 512]) for i in range(3)]
    for j in range(3):
        kb.op("pe", lambda: [nc.tensor.matmul(pss[j][0:5, :], lhsT=cs[:, k, :], rhs=wt[:, k, j * 512:(j + 1) * 512],
                                               start=(k == 0), stop=(k == 7)) for k in range(8)],
              reads=[cs, wt], writes=[pss[j]])
        kb.op("dve", lambda: nc.vector.tensor_tensor(out=ot[:, j * 512:(j + 1) * 512], in0=pss[j][0:5, :],
                                                     in1=bt[:, j * 512:(j + 1) * 512], op=ALU.add),
              reads=[pss[j], bt], writes=[ot])
    kb.store(out, out[:], ot, ot[:])


def run_k0(c, c_ctx, w_ada, b_ada):
    cond = np.concatenate([c, c_ctx[None]], axis=0)
    cT = c32(cond.T.reshape(8, 128, 5).transpose(1, 0, 2))
    wflat = [w_ada[0], w_ada[1]]
    maps = []
    for r in range(8):
        l, j = r // 4, r % 4
        maps.append({"cT": cT, "w": c32(wflat[l][:, j * 1536:(j + 1) * 1536]),
                     "bias": c32(np.broadcast_to(b_ada[l][j * 1536:(j + 1) * 1536], (5, 1536)))})
    res = launch(build_k0, maps)
    m = np.zeros((2, 5, 6144), np.float32)
    for r in range(8):
        l, j = r // 4, r % 4
        m[l, :, j * 1536:(j + 1) * 1536] = res[r]["m"]
    return m


def fm(x_tok):
    n, f = x_tok.shape
    return c32(x_tok.T.reshape(f // 128, 128, n).transpose(1, 0, 2))


def unfm(x_fm):
    p, kc, n = x_fm.shape
    return x_fm.transpose(1, 0, 2).reshape(kc * 128, n).T


def vec_fm(v):
    return c32(v.reshape(-1, 128).T)


class FM:
    def __init__(self, nc, kb, pfx=""):
        self.nc = nc
        self.kb = kb
        self.ones = kb.sb(pfx + "ones_bf", [128, 128], BF16)
        kb.op("dve", lambda: nc.vector.memset(self.ones[:], 1.0), writes=[self.ones])
        self.epst = kb.sb(pfx + "epst", [128, 1], F32)
        kb.op("dve", lambda: nc.vector.memset(self.epst[:], EPS), writes=[self.epst])

    def rstd_bc(self, xt, kc, n, ps, sq, out, nfeat, k0=0):
        nc, kb = self.nc, self.kb
        kb.op("act", lambda: nc.scalar.activation(out=sq[:, 0:kc, 0:n], in_=xt[:, k0:k0 + kc, 0:n], func=AF.Square),
              reads=[xt], writes=[sq])
        kb.op("pe", lambda: [nc.tensor.matmul(ps[:, 0:n], lhsT=self.ones[:], rhs=sq[:, k, 0:n], start=(k == 0),
                                               stop=(k == kc - 1)) for k in range(kc)],
              reads=[sq, self.ones], writes=[ps])
        kb.op("act", lambda: nc.scalar.activation(out=out[:, 0:n], in_=ps[:, 0:n], func=AF.Sqrt, bias=self.epst[:, 0:1],
                                                  scale=1.0 / nfeat), reads=[ps, self.epst], writes=[out])
        kb.op("dve", lambda: nc.vector.reciprocal(out=out[:, 0:n], in_=out[:, 0:n]), reads=[out], writes=[out])

    def norm_mod(self, xt, n, ps, sq, rs, tmp, at, s1p, sh, col):
        nc, kb = self.nc, self.kb
        self.rstd_bc(xt, 8, n, ps, sq, rs, D)
        for k in range(8):
            kb.op("dve", lambda: nc.vector.scalar_tensor_tensor(out=tmp[:, k, 0:n], in0=xt[:, k, 0:n],
                                                                scalar=s1p[:, col * 8 + k:col * 8 + k + 1],
                                                                in1=rs[:, 0:n], op0=ALU.mult, op1=ALU.mult),
                  reads=[xt, rs, s1p], writes=[tmp])
            kb.op("act", lambda: nc.scalar.activation(out=at[:, k, 0:n], in_=tmp[:, k, 0:n], func=AF.Identity,
                                                      bias=sh[:, col * 8 + k:col * 8 + k + 1], scale=1.0),
                  reads=[tmp, sh], writes=[at])


def load_weight_bf16(nc, kb, wt, w_dram, kc, ncols, c0=0):
    wv = w_dram.rearrange("(k p) n -> p k n", p=128)
    step = 2048
    def f():
        ins = []
        for k in range(kc):
            for c in range(0, ncols, step):
                ce = min(ncols, c + step)
                ins.append(nc.gpsimd.dma_start(out=wt[:, k, c0 + c:c0 + ce], in_=wv[:, k, c:ce]))
        return ins
    kb.op("pool", f, writes=[wt], dma=wt)


GROUPS = [(0, 512, 0), (512, 512, 0), (1024, 512, 0), (1536, 512, 0), (2048, 128, 1)]
NTOK = 2176


def mod_pack(m_l, b):
    out = np.zeros((128, 6, 2, 8), np.float32)
    for which in range(6):
        for col, row in enumerate((b, 4)):
            v = m_l[row, which * 1024:(which + 1) * 1024]
            out[:, which, col, :] = v.reshape(8, 128).T
    return c32(out.reshape(128, 96))


def build_k1(nc, kb):
    xT = kb.dram_in("xT", [128, 8, NTOK])
    mod = kb.dram_in("mod", [128, 96])
    w = kb.dram_in("w", [1024, IN0])
    out = kb.dram_out("pT", [29, 128, NTOK])
    fmh = FM(nc, kb)
    modt = kb.sb("modt", [128, 96], F32, dma=True)
    s1p = kb.sb("s1p", [128, 96], F32)
    kb.load(modt, modt[:], mod)
    kb.op("dve", lambda: nc.vector.tensor_scalar(out=s1p[:], in0=modt[:], scalar1=1.0, scalar2=None, op0=ALU.add),
          reads=[modt], writes=[s1p])
    wt = kb.sb("wt", [128, 8, 29 * 128], BF16, dma=True)
    kb.op("dve", lambda: nc.vector.memset(wt[:, :, IN0:29 * 128], 0.0), writes=[wt])
    load_weight_bf16(nc, kb, wt, w, 8, IN0)
    xts = [kb.sb("xt%d" % i, [128, 8, 512], F32, dma=True) for i in range(2)]
    sq = kb.sb("sq", [128, 8, 512], BF16)
    rs = kb.sb("rs", [128, 512], F32)
    tmp = kb.sb("tmp", [128, 8, 512], F32)
    ats = [kb.sb("at%d" % i, [128, 8, 512], BF16) for i in range(2)]
    psn = kb.ps("psn", [128, 512])
    pss = [kb.ps("ps%d" % i, [128, 512]) for i in range(4)]
    ots = [kb.sb("ot%d" % i, [128, 512], F32, dma=True) for i in range(4)]
    cnt = 0
    def prep(gi):
        t0, n, col = GROUPS[gi]
        xt = xts[gi % 2]
        kb.load(xt, xt[:, :, 0:n], xT[:, :, t0:t0 + n], q="pool")
        return _norm_mod_thunks(fmh, xt, n, psn, sq, rs, tmp, ats[gi % 2], s1p, modt, 1, 0, col)

    for t in prep(0):
        t()
    for gi, (t0, n, col) in enumerate(GROUPS):
        at = ats[gi % 2]
        nxt = prep(gi + 1) if gi + 1 < len(GROUPS) else []
        for m in range(29):
            ps = pss[cnt % 4]
            ot = ots[cnt % 4]
            cnt += 1
            kb.op("pe", lambda: [nc.tensor.matmul(ps[:, 0:n], lhsT=wt[:, k, m * 128:(m + 1) * 128], rhs=at[:, k, 0:n],
                                                   start=(k == 0), stop=(k == 7)) for k in range(8)],
                  reads=[wt, at], writes=[ps])
            eng = "act" if m % 2 == 0 else "dve"
            if eng == "act":
                kb.op("act", lambda: nc.scalar.copy(out=ot[:, 0:n], in_=ps[:, 0:n]), reads=[ps], writes=[ot])
            else:
                kb.op("dve", lambda: nc.vector.tensor_copy(out=ot[:, 0:n], in_=ps[:, 0:n]), reads=[ps], writes=[ot])
            kb.store(out, out[m, :, t0:t0 + n], ot, ot[:, 0:n])
            if nxt and m >= 2:
                nxt.pop(0)()
        for t in nxt:
            t()


def _norm_mod_thunks(fmh, xt, n, ps, sq, rs, tmp, at, s1p, modt, which_scale, which_shift, col):
    nc, kb = fmh.nc, fmh.kb
    if not hasattr(tmp, "views"):
        tmp.views = [T(tmp.ap[:, k, :]) for k in range(8)]
    th = []
    th.append(lambda: kb.op("act", lambda: nc.scalar.activation(out=sq[:, 0:8, 0:n], in_=xt[:, 0:8, 0:n], func=AF.Square),
                            reads=[xt], writes=[sq]))
    th.append(lambda: kb.op("pe", lambda: [nc.tensor.matmul(ps[:, 0:n], lhsT=fmh.ones[:], rhs=sq[:, k, 0:n], start=(k == 0), stop=(k == 7))
                                           for k in range(8)], reads=[sq, fmh.ones], writes=[ps]))
    th.append(lambda: kb.op("act", lambda: nc.scalar.activation(out=rs[:, 0:n], in_=ps[:, 0:n], func=AF.Sqrt, bias=fmh.epst[:, 0:1],
                                                                scale=1.0 / D), reads=[ps, fmh.epst], writes=[rs]))
    th.append(lambda: kb.op("dve", lambda: nc.vector.reciprocal(out=rs[:, 0:n], in_=rs[:, 0:n]), reads=[rs], writes=[rs]))
    for k in range(8):
        isc = (which_scale * 2 + col) * 8 + k
        ish = (which_shift * 2 + col) * 8 + k
        tv = tmp.views[k]
        th.append(lambda k=k, isc=isc, tv=tv: kb.op("dve", lambda: nc.vector.scalar_tensor_tensor(
            out=tv[:, 0:n], in0=xt[:, k, 0:n], scalar=s1p[:, isc:isc + 1], in1=rs[:, 0:n], op0=ALU.mult, op1=ALU.mult),
            reads=[xt, rs, s1p], writes=[tv]))
        th.append(lambda k=k, ish=ish, tv=tv: kb.op("act", lambda: nc.scalar.activation(
            out=at[:, k, 0:n], in_=tv[:, 0:n], func=AF.Identity, bias=modt[:, ish:ish + 1], scale=1.0),
            reads=[tv, modt], writes=[at]))
    return th


def _norm_mod(fmh, xt, n, ps, sq, rs, tmp, at, s1p, modt, which_scale, which_shift, col):
    for t in _norm_mod_thunks(fmh, xt, n, ps, sq, rs, tmp, at, s1p, modt, which_scale, which_shift, col):
        t()


def core_tokens(xb, ctxb, s):
    return np.concatenate([xb[s * 2048:(s + 1) * 2048], ctxb[s * 128:(s + 1) * 128]], axis=0)


def run_k1(h_lat, h_ctx, m0, w_in):
    maps = []
    for r in range(8):
        b, s = r // 2, r % 2
        maps.append({"xT": fm(core_tokens(h_lat[b], h_ctx[b], s)), "mod": mod_pack(m0, b), "w": c32(w_in)})
    res = launch(build_k1, maps)
    p_lat = np.zeros((B, L, IN0), np.float32)
    p_ctx = np.zeros((B, CTX, IN0), np.float32)
    for r in range(8):
        b, s = r // 2, r % 2
        pt = res[r]["pT"].reshape(29 * 128, NTOK)[:IN0].T
        p_lat[b, s * 2048:(s + 1) * 2048] = pt[:2048]
        p_ctx[b, s * 128:(s + 1) * 128] = pt[2048:]
    return p_lat, p_ctx


def make_k3a(finish):
    def build(nc, kb):
        hT = kb.dram_in("hT", [128, 8, NTOK])
        mod = kb.dram_in("mod", [128, 96])
        w = kb.dram_in("w", [1024, 1024])
        if finish:
            yr = kb.dram_in("yr", [128, 4, NTOK])
            ys = kb.dram_in("ys", [128, 4, NTOK])
            gT = kb.dram_in("gT", [128, 4, NTOK])
            zT = kb.dram_in("zT", [128, 4, NTOK])
            nw = kb.dram_in("nw", [128, 8])
            bo = kb.dram_in("bo", [128, 128])
        else:
            mixin = kb.dram_in("mix", [128, 8, NTOK])
        out_h = kb.dram_out("h1T", [128, 8, NTOK])
        out_a = kb.dram_out("a2T", [128, 8, NTOK])
        fmh = FM(nc, kb)
        modt = kb.sb("modt", [128, 96], F32, dma=True)
        s1p = kb.sb("s1p", [128, 96], F32)
        kb.load(modt, modt[:], mod)
        kb.op("dve", lambda: nc.vector.tensor_scalar(out=s1p[:], in0=modt[:], scalar1=1.0, scalar2=None, op0=ALU.add),
              reads=[modt], writes=[s1p])
        wt = kb.sb("wt", [128, 8, 1024], BF16, dma=True)
        load_weight_bf16(nc, kb, wt, w, 8, 1024)
        hts = [kb.sb("ht%d" % i, [128, 8, 256], F32, dma=True) for i in range(2)]
        mixs = [kb.sb("mixb%d" % i, [128, 8, 256], BF16, dma=True) for i in range(2)]
        sq = kb.sb("sq", [128, 8, 256], BF16)
        rs = kb.sb("rs", [128, 256], F32)
        tmp = kb.sb("tmp", [128, 8, 256], F32)
        ats = [kb.sb("at%d" % i, [128, 8, 256], F32, dma=True) for i in range(2)]
        psn = kb.ps("psn", [128, 512])
        pss = [kb.ps("ps%d" % i, [128, 512]) for i in range(2)]
        if finish:
            yrts = [kb.sb("yrt%d" % i, [128, 4, 256], F32, dma=True) for i in range(2)]
            ysts = [kb.sb("yst%d" % i, [128, 4, 256], F32, dma=True) for i in range(2)]
            gts = [kb.sb("gt%d" % i, [128, 4, 256], F32, dma=True) for i in range(2)]
            zts = [kb.sb("zt%d" % i, [128, 4, 256], F32, dma=True) for i in range(2)]
            nwt = kb.sb("nwt", [128, 8], F32, dma=True)
            bot = kb.sb("bot", [128, 128], F32, dma=True)
            onesf = kb.sb("onesf", [128, 128], F32)
            dd = kb.sb("dd", [128, 4, 256], F32)
            psA = kb.ps("psA", [128, 4, 256])
            psB = kb.ps("psB", [128, 4, 256])
            sqd = kb.sb("sqd", [128, 4, 256], F32)
            rr = kb.sb("rr", [128, 4, 256], F32)
            kb.load(nwt, nwt[:], nw)
            kb.load(bot, bot[:], bo)
            kb.op("dve", lambda: nc.vector.memset(onesf[:], 1.0 / 512.0), writes=[onesf])
        for gi, (t0, n, col) in enumerate(GROUPS_B):
            ht, mix, at = hts[gi % 2], mixs[gi % 2], ats[gi % 2]
            kb.load(ht, ht[:, :, 0:n], hT[:, :, t0:t0 + n], q="pool")
            if finish:
                yrt, yst, gt, zt = yrts[gi % 2], ysts[gi % 2], gts[gi % 2], zts[gi % 2]
                kb.load(yrt, yrt[:, :, 0:n], yr[:, :, t0:t0 + n], q="pool")
                kb.load(yst, yst[:, :, 0:n], ys[:, :, t0:t0 + n], q="pool")
                kb.load(gt, gt[:, :, 0:n], gT[:, :, t0:t0 + n], q="pool")
                kb.load(zt, zt[:, :, 0:n], zT[:, :, t0:t0 + n], q="pool")
                kb.op("act", lambda: nc.scalar.activation(out=gt[:, :, 0:n], in_=gt[:, :, 0:n], func=AF.Silu), reads=[gt], writes=[gt])
                kb.op("act", lambda: nc.scalar.activation(out=zt[:, :, 0:n], in_=zt[:, :, 0:n], func=AF.Silu), reads=[zt], writes=[zt])
                kb.op("pe", lambda: [nc.tensor.matmul(psA[:, c, 0:n], lhsT=bot[:], rhs=yrt[:, c, 0:n], start=True, stop=True) for c in range(4)],
                      reads=[bot, yrt], writes=[psA])
                kb.op("dve", lambda: nc.vector.tensor_tensor(out=dd[:, :, 0:n], in0=yrt[:, :, 0:n], in1=psA[:, :, 0:n], op=ALU.subtract),
                      reads=[yrt, psA], writes=[dd])
                kb.op("act", lambda: nc.scalar.activation(out=sqd[:, :, 0:n], in_=dd[:, :, 0:n], func=AF.Square), reads=[dd], writes=[sqd])
                kb.op("pe", lambda: [nc.tensor.matmul(psB[:, c, 0:n], lhsT=bot[:], rhs=sqd[:, c, 0:n], start=True, stop=True) for c in range(4)],
                      reads=[bot, sqd], writes=[psB])
                kb.op("act", lambda: nc.scalar.activation(out=rr[:, :, 0:n], in_=psB[:, :, 0:n], func=AF.Sqrt, bias=fmh.epst[:, 0:1], scale=1.0),
                      reads=[psB, fmh.epst], writes=[rr])
                kb.op("dve", lambda: nc.vector.reciprocal(out=rr[:, :, 0:n], in_=rr[:, :, 0:n]), reads=[rr], writes=[rr])
                kb.op("dve", lambda: nc.vector.tensor_tensor(out=dd[:, :, 0:n], in0=dd[:, :, 0:n], in1=rr[:, :, 0:n], op=ALU.mult),
                      reads=[dd, rr], writes=[dd])
                for cch in range(4):
                    kb.op("dve", lambda: nc.vector.scalar_tensor_tensor(out=mix[:, cch, 0:n], in0=dd[:, cch, 0:n], scalar=nwt[:, cch:cch + 1],
                                                                        in1=gt[:, cch, 0:n], op0=ALU.mult, op1=ALU.mult),
                          reads=[dd, gt, nwt], writes=[mix])
                kb.op("dve", lambda: nc.vector.tensor_tensor(out=yst[:, :, 0:n], in0=yst[:, :, 0:n], in1=zt[:, :, 0:n], op=ALU.mult),
                      reads=[yst, zt], writes=[yst])
                kb.op("act", lambda: nc.scalar.activation(out=sqd[:, :, 0:n], in_=yst[:, :, 0:n], func=AF.Square), reads=[yst], writes=[sqd])
                kb.op("pe", lambda: [nc.tensor.matmul(psA[:, 0, 0:n], lhsT=onesf[:], rhs=sqd[:, k, 0:n], start=(k == 0), stop=(k == 3))
                                     for k in range(4)], reads=[onesf, sqd], writes=[psA])
                kb.op("act", lambda: nc.scalar.activation(out=rr[:, 0, 0:n], in_=psA[:, 0, 0:n], func=AF.Sqrt, bias=fmh.epst[:, 0:1], scale=1.0),
                      reads=[psA, fmh.epst], writes=[rr])
                kb.op("dve", lambda: nc.vector.reciprocal(out=rr[:, 0, 0:n], in_=rr[:, 0, 0:n]), reads=[rr], writes=[rr])
                for cch in range(4):
                    kb.op("dve", lambda: nc.vector.scalar_tensor_tensor(out=mix[:, 4 + cch, 0:n], in0=yst[:, cch, 0:n],
                                                                        scalar=nwt[:, 4 + cch:5 + cch], in1=rr[:, 0, 0:n],
                                                                        op0=ALU.mult, op1=ALU.mult),
                          reads=[yst, rr, nwt], writes=[mix])
            else:
                kb.load(mix, mix[:, :, 0:n], mixin[:, :, t0:t0 + n], q="pool")
            for m in range(8):
                ps = pss[m % 2]
                kb.op("pe", lambda: [nc.tensor.matmul(ps[:, 0:n], lhsT=wt[:, k, m * 128:(m + 1) * 128], rhs=mix[:, k, 0:n],
                                                       start=(k == 0), stop=(k == 7)) for k in range(8)],
                      reads=[wt, mix], writes=[ps])
                ig = (2 * 2 + col) * 8 + m
                kb.op("dve", lambda: nc.vector.scalar_tensor_tensor(out=ht[:, m, 0:n], in0=ps[:, 0:n], scalar=modt[:, ig:ig + 1],
                                                                    in1=ht[:, m, 0:n], op0=ALU.mult, op1=ALU.add),
                      reads=[ps, ht, modt], writes=[ht])
            kb.store(out_h, out_h[:, :, t0:t0 + n], ht, ht[:, :, 0:n])
            _norm_mod(fmh, ht, n, psn, sq, rs, tmp, at, s1p, modt, 4, 3, col)
            kb.store(out_a, out_a[:, :, t0:t0 + n], at, at[:, :, 0:n])
    return build


GROUPS_B = [(i * 256, 256, 0) for i in range(8)] + [(2048, 128, 1)]


def make_k3b(final, pfx="", src=None):
    def build(nc, kb):
        P = pfx
        if src is None:
            aT = kb.dram_in("aT", [128, 8, NTOK])
            hT = kb.dram_in("hT", [128, 8, NTOK])
        else:
            aT, hT = src["a2T"].ap, src["h1T"].ap
        mod = kb.dram_in(P + "mod", [128, 96])
        wg = kb.dram_in("wg", [1024, DFF])
        wu = kb.dram_in("wu", [1024, DFF])
        wd = kb.dram_in("wd", [DFF, 1024])
        out = kb.dram_out("oT", [128, 8, NTOK])
        modt = kb.sb(P + "modt", [128, 96], F32, dma=True)
        kb.load(modt, modt[:], mod)
        wgt = kb.sb(P + "wgt", [128, 8, DFF], BF16, dma=True)
        wut = kb.sb(P + "wut", [128, 8, DFF], BF16, dma=True)
        wdt = kb.sb(P + "wdt", [128, 22, 1024], BF16, dma=True)
        load_weight_bf16(nc, kb, wgt, wg, 8, DFF)
        load_weight_bf16(nc, kb, wut, wu, 8, DFF)
        load_weight_bf16(nc, kb, wdt, wd, 22, 1024)
        ats = [kb.sb(P + "at%d" % i, [128, 8, 256], BF16, dma=True) for i in range(2)]
        hts = [kb.sb(P + "ht%d" % i, [128, 8, 256], F32, dma=True) for i in range(2 if final else 1)]
        h1 = kb.sb(P + "h1", [128, 22, 256], BF16)
        sgs = [kb.sb(P + "sg%d" % i, [128, 256], F32) for i in range(2)]
        psg = [kb.ps(P + "psg%d" % i, [128, 512]) for i in range(2)]
        psu = [kb.ps(P + "psu%d" % i, [128, 512]) for i in range(2)]
        psd = [kb.ps(P + "psd%d" % i, [128, 512]) for i in range(3)]
        if final:
            fmh = FM(nc, kb, P)
            fnw = kb.dram_in("fnw", [128, 8])
            fnt = kb.sb(P + "fnt", [128, 8], F32, dma=True)
            kb.load(fnt, fnt[:], fnw)
            sq = kb.sb(P + "sq", [128, 8, 256], BF16)
            rs = kb.sb(P + "rs", [128, 256], F32)
            psn = kb.ps(P + "psn", [128, 512])
        pend = []
        for gi, (t0, n, col) in enumerate(GROUPS_B):
            at = ats[gi % 2]
            ht = hts[gi % len(hts)]
            kb.load(at, at[:, :, 0:n], aT[:, :, t0:t0 + n], q="pool")
            kb.load(ht, ht[:, :, 0:n], hT[:, :, t0:t0 + n], q="pool")
            for j in range(22):
                if pend and j >= 1:
                    pend.pop(0)()
                pg, pu, sg = psg[j % 2], psu[j % 2], sgs[j % 2]
                kb.op("pe", lambda: [nc.tensor.matmul(pg[:, 0:n], lhsT=wgt[:, k, j * 128:(j + 1) * 128], rhs=at[:, k, 0:n],
                                                       start=(k == 0), stop=(k == 7)) for k in range(8)],
                      reads=[wgt, at], writes=[pg])
                kb.op("pe", lambda: [nc.tensor.matmul(pu[:, 0:n], lhsT=wut[:, k, j * 128:(j + 1) * 128], rhs=at[:, k, 0:n],
                                                       start=(k == 0), stop=(k == 7)) for k in range(8)],
                      reads=[wut, at], writes=[pu])
                kb.op("act", lambda: nc.scalar.activation(out=sg[:, 0:n], in_=pg[:, 0:n], func=AF.Silu), reads=[pg], writes=[sg])
                kb.op("dve", lambda: nc.vector.tensor_tensor(out=h1[:, j, 0:n], in0=sg[:, 0:n], in1=pu[:, 0:n], op=ALU.mult),
                      reads=[sg, pu], writes=[h1])
            for m in range(8):
                pd = psd[m % 3]
                kb.op("pe", lambda: [nc.tensor.matmul(pd[:, 0:n], lhsT=wdt[:, j, m * 128:(m + 1) * 128], rhs=h1[:, j, 0:n],
                                                       start=(j == 0), stop=(j == 21)) for j in range(22)],
                      reads=[wdt, h1], writes=[pd])
                ig = (5 * 2 + col) * 8 + m
                kb.op("dve", lambda: nc.vector.scalar_tensor_tensor(out=ht[:, m, 0:n], in0=pd[:, 0:n], scalar=modt[:, ig:ig + 1],
                                                                    in1=ht[:, m, 0:n], op0=ALU.mult, op1=ALU.add),
                      reads=[pd, ht, modt], writes=[ht])
            for t in pend:
                t()
            pend = []
            if final:
                def tail(ht=ht, n=n, t0=t0):
                    th = []
                    th.append(lambda: kb.op("act", lambda: nc.scalar.activation(out=sq[:, 0:8, 0:n], in_=ht[:, 0:8, 0:n], func=AF.Square),
                                            reads=[ht], writes=[sq]))
                    th.append(lambda: kb.op("pe", lambda: [nc.tensor.matmul(psn[:, 0:n], lhsT=fmh.ones[:], rhs=sq[:, kk, 0:n], start=(kk == 0),
                                                                          stop=(kk == 7)) for kk in range(8)], reads=[sq, fmh.ones], writes=[psn]))
                    th.append(lambda: kb.op("act", lambda: nc.scalar.activation(out=rs[:, 0:n], in_=psn[:, 0:n], func=AF.Sqrt, bias=fmh.epst[:, 0:1],
                                                                                scale=1.0 / D), reads=[psn, fmh.epst], writes=[rs]))
                    th.append(lambda: kb.op("dve", lambda: nc.vector.reciprocal(out=rs[:, 0:n], in_=rs[:, 0:n]), reads=[rs], writes=[rs]))
                    for kk in range(8):
                        th.append(lambda kk=kk: kb.op("dve", lambda: nc.vector.scalar_tensor_tensor(
                            out=ht[:, kk, 0:n], in0=ht[:, kk, 0:n], scalar=fnt[:, kk:kk + 1], in1=rs[:, 0:n], op0=ALU.mult, op1=ALU.mult),
                            reads=[ht, rs, fnt], writes=[ht]))
                    th.append(lambda: kb.store(out, out[:, :, t0:t0 + n], ht, ht[:, :, 0:n]))
                    return th
                pend = tail()
            else:
                kb.store(out, out[:, :, t0:t0 + n], ht, ht[:, :, 0:n])
        for t in pend:
            t()
    return build


def make_k3ab(finish, final, scope2=False):
    def build(nc, kb):
        outer = kb.es
        with ExitStack() as es1:
            kb.es = es1
            make_k3a(finish)(nc, kb)
            kb.barrier()
        kb.es = outer
        src = {t.name: t for t in kb.outs}
        if scope2:
            with ExitStack() as es2:
                kb.es = es2
                make_k3b(final, pfx="b_", src=src)(nc, kb)
                kb.barrier()
            kb.es = outer
        else:
            make_k3b(final, pfx="b_", src=src)(nc, kb)
    return build


def build_l0tail_k4(nc, kb):
    make_k3ab(True, False, scope2=True)(nc, kb)
    oT = [t for t in kb.outs if t.name == "oT"][0]
    kb.pfx = "c_"
    kb.alias = {"xT": oT.ap}
    build_k4(nc, kb)
    kb.pfx = ""
    kb.alias = {}


def k3ab_maps(h_lat, h_ctx, m_l, w_out, wg, wu, wd, finish=None, mix_lat=None, final_w=None):
    maps = []
    bo = np.zeros((128, 128), np.float32)
    bo[:64, :64] = 1.0 / 64
    bo[64:, 64:] = 1.0 / 64
    for r in range(8):
        b, s = r // 2, r % 2
        mp = mod_pack(m_l, b)
        d = {"hT": fm(core_rows(h_lat[b], h_ctx[b], s)), "mod": mp, "w": c32(w_out), "b_mod": mp,
             "wg": c32(wg), "wu": c32(wu), "wd": c32(wd)}
        if finish is not None:
            f = finish
            d["yr"] = fm(core_rows(f["yr"][b, CTX:], f["yr"][b, :CTX], s))
            d["ys"] = fm(core_rows(f["ys"][b, CTX:], f["ys"][b, :CTX], s))
            d["gT"] = fm(core_rows(f["p_lat"][b][:, 1536:2048], f["p_ctx"][b][:, 1536:2048], s))
            d["zT"] = fm(core_rows(f["p_lat"][b][:, 2048:2560], f["p_ctx"][b][:, 2048:2560], s))
            d["nw"] = c32(np.concatenate([vec_fm(f["gn_w"]), vec_fm(f["ssd_norm_w"])], 1))
            d["bo"] = bo
        else:
            d["mix"] = fm(core_rows(mix_lat[b], np.zeros((CTX, 1024), np.float32), s))
        if final_w is not None:
            d["fnw"] = vec_fm(final_w)
        maps.append(d)
    return maps


def run_k3ab(h_lat, h_ctx, m_l, w_out, wg, wu, wd, finish=None, mix_lat=None, final_w=None):
    maps = k3ab_maps(h_lat, h_ctx, m_l, w_out, wg, wu, wd, finish=finish, mix_lat=mix_lat, final_w=final_w)
    res = launch(make_k3ab(finish is not None, final_w is not None), maps)
    return [res[r]["oT"] for r in range(8)]


def run_l0tail_k4(h_lat, h_ctx, m0, m1, inp, fin):
    maps = k3ab_maps(h_lat, h_ctx, m0, inp["ret_ssd_w_out"][0], inp["w_ffn_gate"][0], inp["w_ffn_up"][0], inp["w_ffn_down"][0], finish=fin)
    nw = c32(np.concatenate([vec_fm(inp["mla_q_norm_w"][0]), vec_fm(inp["mla_kv_norm_w"][0])], 1))
    for r in range(8):
        b = r // 2
        maps[r].update({"c_mod": mod_pack(m1, b), "c_w_in": c32(inp["mla_w_in"][0]), "c_w_uq": c32(inp["mla_w_uq"][0]),
                        "c_w_ukv": c32(inp["mla_w_ukv"][0]), "c_nw": nw})
    res = launch(build_l0tail_k4, maps)
    h_lat2, h_ctx2 = gather_tokens([res[r]["oT"] for r in range(8)])
    q = np.zeros((B, SEQ, 1536), np.float32)
    kv = np.zeros((B, SEQ, 2048), np.float32)
    pe = np.zeros((B, SEQ, 32), np.float32)
    for r in range(8):
        b, s = r // 2, r % 2
        for arr, name, f in ((q, "c_qT", 1536), (kv, "c_kvT", 2048), (pe, "c_peT", 32)):
            t = res[r][name].reshape(f, NTOK).T
            arr[b, CTX + s * 2048:CTX + (s + 1) * 2048] = t[:2048]
            arr[b, s * 128:(s + 1) * 128] = t[2048:]
    return h_lat2, h_ctx2, (q, kv, pe)


def run_k3b(aT_list, hT_list, m_l, wg, wu, wd, final_w=None):
    maps = []
    for r in range(8):
        b = r // 2
        d = {"aT": aT_list[r], "hT": hT_list[r], "mod": mod_pack(m_l, b), "wg": c32(wg), "wu": c32(wu), "wd": c32(wd)}
        if final_w is not None:
            d["fnw"] = vec_fm(final_w)
        maps.append(d)
    res = launch(make_k3b(final_w is not None), maps)
    return [res[r]["oT"] for r in range(8)]


SEQ = CTX + L
NBLK = SEQ // 256


def build_k1b(nc, kb):
    qk = kb.dram_in("qk", [NBLK, 128, 4, 256])
    cs = kb.dram_in("cs", [NBLK, 128, 2, 256])
    xbc = kb.dram_in("xbc", [NBLK, 128, 4, 260])
    cw = kb.dram_in("cw", [128, 4, 5])
    cb = kb.dram_in("cb", [128, 4])
    dtr = kb.dram_in("dtr", [NBLK, 8, 256])
    dpar = kb.dram_in("dpar", [8, 2])
    qko = kb.dram_out("qko", [NBLK, 128, 4, 256])
    xo = kb.dram_out("xo", [NBLK, 128, 4, 256])
    dto = kb.dram_out("dto", [NBLK, 8, 2, 256])
    cwt = kb.sb("cwt", [128, 4, 5], F32, dma=True)
    cbt = kb.sb("cbt", [128, 4], F32, dma=True)
    dpt = kb.sb("dpt", [8, 2], F32, dma=True)
    nA = kb.sb("nA", [8, 1], F32)
    one8 = kb.sb("one8", [8, 1], F32)
    kb.load(cwt, cwt[:], cw)
    kb.load(cbt, cbt[:], cb)
    kb.load(dpt, dpt[:], dpar)
    kb.op("dve", lambda: nc.vector.memset(one8[:], 1.0), writes=[one8])
    kb.op("act", lambda: nc.scalar.activation(out=nA[:], in_=dpt[:, 1:2], func=AF.Exp), reads=[dpt], writes=[nA])
    kb.op("dve", lambda: nc.vector.tensor_scalar(out=nA[:], in0=nA[:], scalar1=-1.0, scalar2=None, op0=ALU.mult), reads=[nA], writes=[nA])
    qkt = [kb.sb("qkt%d" % i, [128, 4, 256], F32, dma=True) for i in range(2)]
    cst = [kb.sb("cst%d" % i, [128, 2, 256], F32, dma=True) for i in range(2)]
    xt = [kb.sb("xbt%d" % i, [128, 4, 260], F32, dma=True) for i in range(2)]
    dt_ = [kb.sb("dtt%d" % i, [8, 256], F32, dma=True) for i in range(2)]
    qo = [kb.sb("qo%d" % i, [128, 4, 256], F32, dma=True) for i in range(2)]
    xot = [kb.sb("xot%d" % i, [128, 4, 256], F32, dma=True) for i in range(2)]
    dot = [kb.sb("dot%d" % i, [8, 2, 256], F32, dma=True) for i in range(2)]
    ta = kb.sb("ta", [128, 256], F32)
    tb = kb.sb("tb", [128, 256], F32)
    acc = kb.sb("acc", [128, 256], F32)
    for blk in range(NBLK):
        i2 = blk % 2
        a, c_, x_, d_, o_, xo_, do_ = qkt[i2], cst[i2], xt[i2], dt_[i2], qo[i2], xot[i2], dot[i2]
        kb.load(a, a[:], qk[blk], q="pool")
        kb.load(c_, c_[:], cs[blk], q="pool")
        kb.load(x_, x_[:], xbc[blk], q="pool")
        kb.load(d_, d_[:], dtr[blk], q="pool")
        import os
        PARTS = 'rope,conv,dt'
        for pair, scl in (((0, 1.0), (1, 0.125)) if 'rope' in PARTS else ()):
            x1 = a[:, 2 * pair, :]
            x2 = a[:, 2 * pair + 1, :]
            cos, sin = c_[:, 0, :], c_[:, 1, :]
            V = nc.vector
            kb.op("dve", lambda: V.scalar_tensor_tensor(out=ta[:], in0=x1, scalar=scl, in1=cos, op0=ALU.mult, op1=ALU.mult), reads=[a, c_], writes=[ta])
            kb.op("dve", lambda: V.scalar_tensor_tensor(out=tb[:], in0=x2, scalar=scl, in1=sin, op0=ALU.mult, op1=ALU.mult), reads=[a, c_], writes=[tb])
            kb.op("dve", lambda: V.tensor_tensor(out=o_[:, 2 * pair, :], in0=ta[:], in1=tb[:], op=ALU.subtract), reads=[ta, tb], writes=[o_])
            kb.op("dve", lambda: V.scalar_tensor_tensor(out=ta[:], in0=x1, scalar=scl, in1=sin, op0=ALU.mult, op1=ALU.mult), reads=[a, c_], writes=[ta])
            kb.op("dve", lambda: V.scalar_tensor_tensor(out=tb[:], in0=x2, scalar=scl, in1=cos, op0=ALU.mult, op1=ALU.mult), reads=[a, c_], writes=[tb])
            kb.op("dve", lambda: V.tensor_tensor(out=o_[:, 2 * pair + 1, :], in0=ta[:], in1=tb[:], op=ALU.add), reads=[ta, tb], writes=[o_])
        kb.store(qko, qko[blk], o_, o_[:])
        for t in (range(4) if 'conv' in PARTS else ()):
            G = nc.vector
            kb.op("dve", lambda: G.tensor_scalar(out=acc[:], in0=x_[:, t, 0:256], scalar1=cwt[:, t, 0:1], scalar2=None, op0=ALU.mult),
                  reads=[x_, cwt], writes=[acc])
            for k in range(1, 5):
                kb.op("dve", lambda: G.scalar_tensor_tensor(out=acc[:], in0=x_[:, t, k:k + 256], scalar=cwt[:, t, k:k + 1], in1=acc[:],
                                                             op0=ALU.mult, op1=ALU.add), reads=[x_, cwt, acc], writes=[acc])
            kb.op("act", lambda: nc.scalar.activation(out=xo_[:, t, :], in_=acc[:], func=AF.Silu, bias=cbt[:, t:t + 1], scale=1.0),
                  reads=[acc, cbt], writes=[xo_])
        kb.store(xo, xo[blk], xo_, xo_[:])
        if 'dt' not in PARTS:
            continue
        kb.op("dve", lambda: nc.vector.tensor_scalar(out=do_[:, 0, :], in0=d_[:], scalar1=dpt[:, 0:1], scalar2=None, op0=ALU.add), reads=[d_, dpt], writes=[do_])
        kb.op("act", lambda: nc.scalar.activation(out=do_[:, 0, :], in_=do_[:, 0, :], func=AF.Exp), reads=[do_], writes=[do_])
        kb.op("dve", lambda: nc.vector.tensor_scalar(out=do_[:, 0, :], in0=do_[:, 0, :], scalar1=1.0, scalar2=None, op0=ALU.add), reads=[do_], writes=[do_])
        kb.op("act", lambda: nc.scalar.activation(out=do_[:, 0, :], in_=do_[:, 0, :], func=AF.Ln), reads=[do_], writes=[do_])
        kb.op("dve", lambda: nc.vector.tensor_scalar(out=do_[:, 1, :], in0=do_[:, 0, :], scalar1=nA[:, 0:1], scalar2=None, op0=ALU.mult),
              reads=[do_, nA], writes=[do_])
        kb.store(dto, dto[blk], do_, do_[:])


def rope_tables(rot_dim):
    rows = L // 64
    row = np.repeat(np.arange(rows), 64).astype(np.float32)
    col = np.tile(np.arange(64), rows).astype(np.float32)
    n_freq = rot_dim // 4
    inv = (10000.0 ** (-np.arange(n_freq, dtype=np.float32) / n_freq)).astype(np.float32)
    ang = np.concatenate([row[:, None] * inv, col[:, None] * inv], axis=-1)
    cos = np.concatenate([np.ones((CTX, rot_dim // 2), np.float32), np.cos(ang)], 0)
    sin = np.concatenate([np.zeros((CTX, rot_dim // 2), np.float32), np.sin(ang)], 0)
    return cos.astype(np.float32), sin.astype(np.float32)


def blocks(xT, w=256):
    r, n = xT.shape
    return xT.reshape(r, n // w, w).transpose(1, 0, 2)


def run_k1b(p_lat, p_ctx, conv_w, conv_b, dt_bias, a_log):
    cos, sin = rope_tables(64)
    maps = []
    for r in range(8):
        b, s = r // 2, r % 2
        P = np.concatenate([p_ctx[b], p_lat[b]], 0)
        hs = slice(4 * s, 4 * s + 4)
        q = P[:, 0:512].reshape(SEQ, 8, 64)[:, hs]
        k = P[:, 512:1024].reshape(SEQ, 8, 64)[:, hs]
        arrs = [q[:, :, :32], q[:, :, 32:], k[:, :, :32], k[:, :, 32:]]
        qk = np.stack([blocks(a.reshape(SEQ, 128).T) for a in arrs], axis=2)
        ct = np.tile(cos, (1, 4)).T
        st = np.tile(sin, (1, 4)).T
        cs = np.stack([blocks(ct), blocks(st)], axis=2)
        xs = P[:, 2560 + 256 * s:2560 + 256 * s + 256]
        Bm = P[:, 3072 + 128 * s:3072 + 128 * s + 128]
        Cm = P[:, 3328 + 128 * s:3328 + 128 * s + 128]
        ch = np.concatenate([xs, Bm, Cm], 1)
        padc = np.pad(ch[:CTX], ((2, 2), (0, 0)))
        padl = np.pad(ch[CTX:], ((2, 2), (0, 0)))
        xb = np.zeros((NBLK, 128, 4, 260), np.float32)
        xb[0] = padc.T.reshape(4, 128, 260).transpose(1, 0, 2)
        for i in range(16):
            xb[1 + i] = padl[i * 256:i * 256 + 260].T.reshape(4, 128, 260).transpose(1, 0, 2)
        cidx = np.concatenate([np.arange(256 * s, 256 * s + 256), 512 + 128 * s + np.arange(128), 768 + 128 * s + np.arange(128)])
        cw = conv_w[:, cidx].T.reshape(4, 128, 5).transpose(1, 0, 2)
        cbv = conv_b[cidx].reshape(4, 128).T
        dtraw = P[:, 3584:3600].reshape(SEQ, 2, 8)[:, :, hs].reshape(SEQ, 8).T
        dpar = np.stack([dt_bias[:, hs].reshape(8), a_log[:, hs].reshape(8)], 1)
        maps.append({"qk": c32(qk), "cs": c32(cs), "xbc": c32(xb), "cw": c32(cw), "cb": c32(cbv),
                     "dtr": c32(blocks(dtraw)), "dpar": c32(dpar)})
    res = launch(build_k1b, maps)
    outs = []
    for r in range(8):
        qko = res[r]["qko"].transpose(1, 2, 0, 3).reshape(128, 4, SEQ)
        xo = res[r]["xo"].transpose(1, 2, 0, 3).reshape(128, 4, SEQ)
        dto = res[r]["dto"].transpose(1, 2, 0, 3).reshape(8, 2, SEQ)
        outs.append((qko, xo, dto))
    return outs


NCH = SEQ // 128
BWD_ORDER = [1, 0] + list(range(NCH - 1, 1, -1))


def bc(ap, axis, shape):
    return ap.unsqueeze(axis).to_broadcast(list(shape))


def build_k2(nc, kb):
    V, A, PE = nc.vector, nc.scalar, nc.tensor
    qT_d = kb.dram_in("qT", [128, 2, SEQ])
    kT_d = kb.dram_in("kT", [128, 2, SEQ])
    ktm_d = kb.dram_in("ktm", [128, NCH, 256])
    vtm_d = kb.dram_in("vtm", [128, NCH, 256])
    BT_d = kb.dram_in("BT", [128, SEQ])
    CT_d = kb.dram_in("CT", [128, SEQ])
    Btm_d = kb.dram_in("Btm", [128, NCH, 128])
    xtm_d = kb.dram_in("xtm", [128, NCH, 256])
    dtm_d = kb.dram_in("dtm", [128, NCH, 16])
    cst_d = kb.dram_in("cst", [128, 10, 128])
    pc_d = kb.dram_in("pc", [128, 2])
    dlp_d = kb.dram_in("dlp", [128, 4])
    dlt_d = kb.dram_in("dlt", [128, 8])
    dsk_d = kb.dram_in("dsk", [128, 4])
    yret = kb.dram_out("yret", [NCH, 128, 256])
    yssd = kb.dram_out("yssd", [NCH, 128, 256])

    def castload(name, shape, src, dims3):
        t = kb.sb(name, shape, BF16, dma=True)
        def f():
            ins = []
            if dims3:
                for a in range(shape[1]):
                    for c in range(0, shape[2], 2048):
                        ce = min(shape[2], c + 2048)
                        ins.append(nc.gpsimd.dma_start(out=t[:, a, c:ce], in_=src[:, a, c:ce]))
            else:
                for c in range(0, shape[1], 2048):
                    ce = min(shape[1], c + 2048)
                    ins.append(nc.gpsimd.dma_start(out=t[:, c:ce], in_=src[:, c:ce]))
            return ins
        kb.op("pool", f, writes=[t], dma=t)
        return t

    def f32load(name, shape, src):
        t = kb.sb(name + "_s", shape, F32, dma=True)
        kb.load(t, t[:], src)
        return t

    cst = f32load("cst", [128, 10, 128], cst_d)
    pc = f32load("pc", [128, 2], pc_d)
    dlp = f32load("dlp", [128, 4], dlp_d)
    dlt = f32load("dlt", [128, 8], dlt_d)
    dsk = f32load("dsk", [128, 4], dsk_d)
    dm = f32load("dm", [128, NCH, 16], dtm_d)
    xb16 = castload("xb16", [128, NCH * 256], xtm_d.rearrange("p n c -> p (n c)"), False)
    ktm = castload("ktmb", [128, NCH * 256], ktm_d.rearrange("p n c -> p (n c)"), False)
    vtm = castload("vtmb", [128, NCH * 256], vtm_d.rearrange("p n c -> p (n c)"), False)
    Btm = castload("Btmb", [128, NCH * 128], Btm_d.rearrange("p n c -> p (n c)"), False)
    qT = castload("qTb", [128, 2, SEQ], qT_d, True)
    kT = castload("kTb", [128, 2, SEQ], kT_d, True)
    BT = castload("BTb", [128, SEQ], BT_d, False)
    CT = castload("CTb", [128, SEQ], CT_d, False)
    TRI_LE, TRI_GE, TRI_GT, TRI_LT, DPOS, DNEG, IDX1, IDXB, ONESM = 0, 1, 2, 3, 5, 6, 7, 8, 9

    one1 = kb.sb("one1", [128, 1], F32)
    kb.op("dve", lambda: V.memset(one1[:], 1.0), writes=[one1])

    def logsig(t, n):
        kb.op("dve", lambda: V.tensor_scalar(out=t[:, 0:n], in0=t[:, 0:n], scalar1=-1.0, scalar2=None, op0=ALU.mult), reads=[t], writes=[t])
        kb.op("act", lambda: A.activation(out=t[:, 0:n], in_=t[:, 0:n], func=AF.Exp), reads=[t], writes=[t])
        kb.op("dve", lambda: V.tensor_scalar(out=t[:, 0:n], in0=t[:, 0:n], scalar1=1.0, scalar2=None, op0=ALU.add), reads=[t], writes=[t])
        kb.op("act", lambda: A.activation(out=t[:, 0:n], in_=t[:, 0:n], func=AF.Ln), reads=[t], writes=[t])
        kb.op("dve", lambda: V.tensor_scalar(out=t[:, 0:n], in0=t[:, 0:n], scalar1=-1.0, scalar2=None, op0=ALU.mult), reads=[t], writes=[t])

    logsig(dlp, 4)
    logsig(dlt, 8)
    MASK = kb.sb("MASK", [128, 4, 128], F32)
    tmpm = kb.sb("tmpm", [128, 128], F32)
    for h in range(4):
        kb.op("dve", lambda: V.tensor_scalar(out=tmpm[:], in0=cst[:, DPOS, :], scalar1=dlt[:, h:h + 1], scalar2=None, op0=ALU.mult),
              reads=[cst, dlt], writes=[tmpm])
        kb.op("dve", lambda: V.scalar_tensor_tensor(out=tmpm[:], in0=cst[:, DNEG, :], scalar=dlt[:, 4 + h:5 + h], in1=tmpm[:],
                                                    op0=ALU.mult, op1=ALU.add), reads=[cst, dlt, tmpm], writes=[tmpm])
        kb.op("act", lambda: A.activation(out=MASK[:, h, :], in_=tmpm[:], func=AF.Exp), reads=[tmpm], writes=[MASK])
    GF = kb.sb("GF", [128, 2, 128], F32)
    GB = kb.sb("GB", [128, 2, 128], F32)
    for t in range(2):
        kb.op("dve", lambda: V.tensor_scalar(out=GF[:, t, :], in0=cst[:, IDX1, :], scalar1=dlp[:, 2 * t:2 * t + 1], scalar2=None, op0=ALU.mult),
              reads=[cst, dlp], writes=[GF])
        kb.op("dve", lambda: V.tensor_scalar(out=GB[:, t, :], in0=cst[:, IDXB, :], scalar1=dlp[:, 2 * t + 1:2 * t + 2], scalar2=None, op0=ALU.mult),
              reads=[cst, dlp], writes=[GB])
    kb.op("act", lambda: A.activation(out=GF[:], in_=GF[:], func=AF.Exp), reads=[GF], writes=[GF])
    kb.op("act", lambda: A.activation(out=GB[:], in_=GB[:], func=AF.Exp), reads=[GB], writes=[GB])
    WFB = kb.sb("WFB", [128, 8], F32)
    kb.op("dve", lambda: V.tensor_scalar(out=WFB[:, 0:4], in0=dlt[:, 0:4], scalar1=pc[:, 0:1], scalar2=None, op0=ALU.mult), reads=[dlt, pc], writes=[WFB])
    kb.op("dve", lambda: V.tensor_scalar(out=WFB[:, 4:8], in0=dlt[:, 4:8], scalar1=pc[:, 1:2], scalar2=None, op0=ALU.mult), reads=[dlt, pc], writes=[WFB])
    kb.op("act", lambda: A.activation(out=WFB[:], in_=WFB[:], func=AF.Exp), reads=[WFB], writes=[WFB])
    TOTP = kb.sb("TOTP", [128, 4], F32)
    kb.op("dve", lambda: V.tensor_scalar(out=TOTP[:], in0=dlp[:], scalar1=128.0, scalar2=None, op0=ALU.mult), reads=[dlp], writes=[TOTP])
    kb.op("act", lambda: A.activation(out=TOTP[:], in_=TOTP[:], func=AF.Exp), reads=[TOTP], writes=[TOTP])

    SR = [[kb.sb("SR%d%d" % (d, t), [128, 64], F32) for t in range(2)] for d in range(2)]
    SS = [kb.sb("SS%d" % d, [128, 256], F32) for d in range(2)]
    for d in range(2):
        for t in range(2):
            kb.op("dve", lambda: V.memset(SR[d][t][:], 0.0), writes=[SR[d][t]])
        kb.op("dve", lambda: V.memset(SS[d][:], 0.0), writes=[SS[d]])
    SRB = kb.sb("SRB", [128, NCH, 2, 64], BF16)
    SSB = kb.sb("SSB", [128, NCH, 256], BF16)
    SRFb = kb.sb("SRFb", [128, 2, 64], BF16)
    SSFb = kb.sb("SSFb", [128, 256], BF16)

    p_dec = kb.ps("p_dec", [128, 512])
    p_arg = kb.ps("p_arg", [128, 512])
    p_st = kb.ps("p_st", [128, 512])
    p_y = kb.ps("p_y", [128, 512])
    p_yfb = kb.ps("p_yfb", [128, 512])
    p_r = kb.ps("p_r", [128, 512])
    p_yr = kb.ps("p_yr", [128, 512])
    p_kv = kb.ps("p_kv", [128, 512])
    dec = kb.sb("dec", [128, 32], F32)
    vw = kb.sb("vw", [128, 256], BF16)
    xw = kb.sb("xw", [128, 256], BF16)

    decA = kb.sb("decA", [128, NCH, 32], F32)
    kb.op("pe", lambda: [PE.matmul(p_dec[:, 0:NCH * 4], lhsT=cst[:, TRI_LE, :], rhs=dm[:, :, 8:12], start=True, stop=True),
                         PE.matmul(p_dec[:, NCH * 4:NCH * 8], lhsT=cst[:, TRI_GE, :], rhs=dm[:, :, 12:16], start=True, stop=True)],
          reads=[cst, dm], writes=[p_dec])
    kb.op("pe", lambda: PE.matmul(p_arg[:, 0:NCH * 8], lhsT=cst[:, ONESM, :], rhs=dm[:, :, 8:16], start=True, stop=True),
          reads=[cst, dm], writes=[p_arg])
    kb.op("dve", lambda: V.tensor_copy(out=decA[:, :, 0:4], in_=p_dec[:, 0:NCH * 4].rearrange("p (n c) -> p n c", c=4)), reads=[p_dec], writes=[decA])
    kb.op("dve", lambda: V.tensor_copy(out=decA[:, :, 4:8], in_=p_dec[:, NCH * 4:NCH * 8].rearrange("p (n c) -> p n c", c=4)), reads=[p_dec], writes=[decA])
    kb.op("dve", lambda: V.tensor_copy(out=decA[:, :, 8:16], in_=p_arg[:, 0:NCH * 8].rearrange("p (n c) -> p n c", c=8)), reads=[p_arg], writes=[decA])
    kb.op("dve", lambda: V.tensor_tensor(out=decA[:, :, 16:24], in0=decA[:, :, 8:16], in1=decA[:, :, 0:8], op=ALU.subtract), reads=[decA], writes=[decA])
    kb.op("act", lambda: A.activation(out=decA[:, :, 0:24], in_=decA[:, :, 0:24], func=AF.Exp), reads=[decA], writes=[decA])
    kb.op("dve", lambda: V.tensor_tensor(out=decA[:, :, 24:32], in0=decA[:, :, 16:24], in1=dm[:, :, 0:8], op=ALU.mult), reads=[decA, dm], writes=[decA])

    def decays(n):
        pass

    vws = [vw, kb.sb("vw2", [128, 256], BF16)]
    xws = [xw, kb.sb("xw2", [128, 256], BF16)]
    pkvs = [p_kv, p_dec]
    kvcnt = [0]

    def kv_part(n, d):
        i = kvcnt[0] % 2
        kvcnt[0] += 1
        vw_, xw_, pk = vws[i], xws[i], pkvs[i]
        wcol = WFB[:, 4 * d:4 * d + 4]
        kb.op("dve", lambda: V.tensor_tensor(out=vw_[:].rearrange("p (h d) -> p h d", h=4),
                                             in0=vtm[:, n * 256:(n + 1) * 256].rearrange("p (h d) -> p h d", h=4),
                                             in1=bc(wcol, 2, [128, 4, 64]), op=ALU.mult), reads=[vtm, WFB], writes=[vw_])
        wcs = decA[:, n, 24 + 4 * d:28 + 4 * d]
        kb.op("dve", lambda: V.tensor_tensor(out=xw_[:].rearrange("p (h d) -> p h d", h=4),
                                             in0=xb16[:, n * 256:(n + 1) * 256].rearrange("p (h d) -> p h d", h=4),
                                             in1=bc(wcs, 2, [128, 4, 64]), op=ALU.mult), reads=[xb16, decA], writes=[xw_])
        kb.op("pe", lambda: [PE.matmul(pk[:, 0:128], lhsT=ktm[:, n * 256:n * 256 + 128], rhs=vw_[:, 0:128], start=True, stop=True),
                             PE.matmul(pk[:, 128:256], lhsT=ktm[:, n * 256 + 128:n * 256 + 256], rhs=vw_[:, 128:256], start=True, stop=True),
                             PE.matmul(pk[:, 256:512], lhsT=Btm[:, n * 128:(n + 1) * 128], rhs=xw_[:], start=True, stop=True)],
              reads=[ktm, vw_, Btm, xw_], writes=[pk])
        return pk

    def upd_part(n, d, pk):
        for t in range(2):
            for u in range(2):
                rows = slice(u * 64, (u + 1) * 64)
                cc = t * 128 + u * 64
                kb.op("dve", lambda: V.scalar_tensor_tensor(out=SR[d][t][rows, :], in0=SR[d][t][rows, :], scalar=TOTP[rows, 2 * t + d:2 * t + d + 1],
                                                            in1=pk[rows, cc:cc + 64], op0=ALU.mult, op1=ALU.add),
                      reads=[SR[d][t], TOTP, pk], writes=[SR[d][t]])
        tot = decA[:, n, 8 + 4 * d:12 + 4 * d]
        kb.op("dve", lambda: V.tensor_tensor(out=SS[d][:].rearrange("p (h d) -> p h d", h=4), in0=SS[d][:].rearrange("p (h d) -> p h d", h=4),
                                             in1=bc(tot, 2, [128, 4, 64]), op=ALU.mult), reads=[SS[d], decA], writes=[SS[d]])
        kb.op("dve", lambda: V.tensor_tensor(out=SS[d][:], in0=SS[d][:], in1=pk[:, 256:512], op=ALU.add), reads=[SS[d], pk], writes=[SS[d]])

    import os
    STOP = ''
    if STOP == 'setup':
        return
    pk = kv_part(BWD_ORDER[0], 1)
    for idx, n in enumerate(BWD_ORDER):
        for t in range(2):
            kb.op("act", lambda: A.copy(out=SRB[:, n, t, :], in_=SR[1][t][:]), reads=[SR[1][t]], writes=[SRB])
        kb.op("act", lambda: A.copy(out=SSB[:, n, :], in_=SS[1][:]), reads=[SS[1]], writes=[SSB])
        pk_next = kv_part(BWD_ORDER[idx + 1], 1) if idx + 1 < NCH else None
        upd_part(n, 1, pk)
        pk = pk_next

    if STOP == 'sweep1':
        return
    rhsF = kb.sb("rhsF", [128, 4, 128], F32)
    rhsB = kb.sb("rhsB", [128, 4, 128], F32)
    Eex = kb.sb("Eex", [128, 4, 128], F32)
    t1 = kb.sb("t1", [128, 4, 128], F32)
    ddt = kb.sb("ddt", [128, 4], F32)
    Pm = kb.sb("Pm", [128, 4, 128], BF16)
    Pr = kb.sb("Pr", [128, 4, 128], BF16)
    qwF = kb.sb("qwF", [128, 2, 128], BF16)
    qwB = kb.sb("qwB", [128, 2, 128], BF16)
    yo = [kb.sb("yo%d" % i, [128, 256], F32, dma=True) for i in range(2)]
    yr = [kb.sb("yr%d" % i, [128, 256], F32, dma=True) for i in range(2)]
    ya = kb.sb("ya", [128, 256], F32)
    qz = kb.sb("qz", [128, 4, 128], BF16)
    SRFz = kb.sb("SRFz", [128, 4, 64], BF16)
    SRBz = kb.sb("SRBz", [128, 4, 64], BF16)
    rmask = kb.sb("rmask", [128, 2], F32)
    kb.op("dve", lambda: V.memset(SRFz[:], 0.0), writes=[SRFz])
    kb.op("dve", lambda: V.memset(SRBz[:], 0.0), writes=[SRBz])
    kb.op("dve", lambda: V.memset(rmask[:], 0.0), writes=[rmask])
    kb.op("dve", lambda: V.memset(rmask[0:64, 0:1], 1.0), writes=[rmask])
    kb.op("dve", lambda: V.memset(rmask[64:128, 1:2], 1.0), writes=[rmask])
    S2 = 'ssd,ret,upd'
    def sweep2():
        for n in range(NCH):
            cs_ = slice(n * 128, (n + 1) * 128)
            pk = kv_part(n, 0)
            kb.op("act", lambda: A.copy(out=SSFb[:], in_=SS[0][:]), reads=[SS[0]], writes=[SSFb])
            ssdA(n, cs_)
            retA(n, cs_)
            ssdB(n, cs_)
            ssdC(n, cs_)
            retB(n, cs_)
            ssdD(n, cs_)
            retC(n, cs_)
            upd_part(n, 0, pk)

    def ssdA(n, cs_):
        kb.op("dve", lambda: V.tensor_tensor(out=rhsF[:], in0=bc(cst[:, TRI_LE, :], 1, [128, 4, 128]), in1=bc(dm[:, n, 8:12], 2, [128, 4, 128]),
                                             op=ALU.mult), reads=[cst, dm], writes=[rhsF])
        kb.op("dve", lambda: V.tensor_tensor(out=rhsB[:], in0=bc(cst[:, TRI_GE, :], 1, [128, 4, 128]), in1=bc(dm[:, n, 12:16], 2, [128, 4, 128]),
                                             op=ALU.mult), reads=[cst, dm], writes=[rhsB])
        kb.op("pe", lambda: [PE.matmul(p_arg[:], lhsT=cst[:, TRI_GT, :], rhs=rhsF[:].rearrange("p h i -> p (h i)"), start=True, stop=False),
                             PE.matmul(p_arg[:], lhsT=cst[:, TRI_LT, :], rhs=rhsB[:].rearrange("p h i -> p (h i)"), start=False, stop=True)],
              reads=[cst, rhsF, rhsB], writes=[p_arg])

    def ssdB(n, cs_):
        kb.op("act", lambda: A.activation(out=Eex[:].rearrange("p h i -> p (h i)"), in_=p_arg[:], func=AF.Exp), reads=[p_arg], writes=[Eex])
        kb.op("dve", lambda: V.tensor_tensor(out=ddt[:], in0=dm[:, n, 0:4], in1=dm[:, n, 4:8], op=ALU.subtract), reads=[dm], writes=[ddt])
        kb.op("dve", lambda: V.tensor_tensor(out=t1[:], in0=bc(cst[:, TRI_LE, :], 1, [128, 4, 128]), in1=bc(ddt[:], 2, [128, 4, 128]), op=ALU.mult),
              reads=[cst, ddt], writes=[t1])
        kb.op("dve", lambda: V.tensor_tensor(out=t1[:], in0=t1[:], in1=bc(dm[:, n, 4:8], 2, [128, 4, 128]), op=ALU.add), reads=[t1, dm], writes=[t1])
        kb.op("pe", lambda: PE.matmul(p_st[:, 0:128], lhsT=BT[:, cs_], rhs=CT[:, cs_], start=True, stop=True), reads=[BT, CT], writes=[p_st])

    def ssdC(n, cs_):
        kb.op("dve", lambda: V.tensor_tensor(out=t1[:], in0=t1[:], in1=Eex[:], op=ALU.mult), reads=[t1, Eex], writes=[t1])
        kb.op("dve", lambda: V.tensor_tensor(out=Pm[:], in0=t1[:], in1=bc(p_st[:, 0:128], 1, [128, 4, 128]), op=ALU.mult), reads=[t1, p_st], writes=[Pm])
        kb.op("pe", lambda: [PE.matmul(p_y[:, h * 64:(h + 1) * 64], lhsT=Pm[:, h, :], rhs=xb16[:, n * 256 + h * 64:n * 256 + (h + 1) * 64],
                                       start=True, stop=True) for h in range(4)], reads=[Pm, xb16], writes=[p_y])
        kb.op("pe", lambda: [PE.matmul(p_yfb[:, 0:256], lhsT=CT[:, cs_], rhs=SSFb[:], start=True, stop=True),
                             PE.matmul(p_yfb[:, 256:512], lhsT=CT[:, cs_], rhs=SSB[:, n, :], start=True, stop=True)],
              reads=[CT, SSFb, SSB], writes=[p_yfb])

    def ssdD(n, cs_):
        o = yo[n % 2]
        v3 = lambda ap: ap.rearrange("p (h d) -> p h d", h=4)
        kb.op("dve", lambda: V.tensor_tensor(out=v3(o[:]), in0=v3(p_yfb[:, 0:256]), in1=bc(decA[:, n, 0:4], 2, [128, 4, 64]), op=ALU.mult),
              reads=[p_yfb, decA], writes=[o])
        kb.op("dve", lambda: V.tensor_tensor(out=v3(ya[:]), in0=v3(p_yfb[:, 256:512]), in1=bc(decA[:, n, 4:8], 2, [128, 4, 64]), op=ALU.mult),
              reads=[p_yfb, decA], writes=[ya])
        kb.op("dve", lambda: V.tensor_tensor(out=o[:], in0=o[:], in1=ya[:], op=ALU.add), reads=[o, ya], writes=[o])
        kb.op("dve", lambda: V.tensor_tensor(out=o[:], in0=o[:], in1=p_y[:, 0:256], op=ALU.add), reads=[o, p_y], writes=[o])
        kb.op("dve", lambda: V.tensor_tensor(out=v3(ya[:]), in0=v3(xb16[:, n * 256:(n + 1) * 256]), in1=bc(dsk[:], 2, [128, 4, 64]), op=ALU.mult),
              reads=[xb16, dsk], writes=[ya])
        kb.op("dve", lambda: V.tensor_tensor(out=o[:], in0=o[:], in1=ya[:], op=ALU.add), reads=[o, ya], writes=[o])
        kb.store(yssd, yssd[n], o, o[:])

    def retA(n, cs_):
        for h in range(4):
            kb.op("dve", lambda: V.tensor_scalar(out=qz[:, h, :], in0=qT[:, h // 2, cs_], scalar1=rmask[:, h % 2:h % 2 + 1], scalar2=None, op0=ALU.mult),
                  reads=[qT, rmask], writes=[qz])
        kb.op("pe", lambda: [PE.matmul(p_r[:, h * 128:(h + 1) * 128], lhsT=kT[:, h // 2, cs_], rhs=qz[:, h, :], start=True, stop=True)
                             for h in range(4)], reads=[kT, qz], writes=[p_r])
        kb.op("dve", lambda: V.tensor_tensor(out=qwF[:], in0=qT[:, :, cs_], in1=GF[:], op=ALU.mult), reads=[qT, GF], writes=[qwF])
        kb.op("dve", lambda: V.tensor_tensor(out=qwB[:], in0=qT[:, :, cs_], in1=GB[:], op=ALU.mult), reads=[qT, GB], writes=[qwB])
        for h in range(4):
            rows = slice((h % 2) * 64, (h % 2) * 64 + 64)
            kb.op("act", lambda: A.copy(out=SRFz[rows, h, :], in_=SR[0][h // 2][rows, :]), reads=[SR[0][h // 2]], writes=[SRFz])
            kb.op("act", lambda: A.copy(out=SRBz[rows, h, :], in_=SRB[rows, n, h // 2, :]), reads=[SRB], writes=[SRBz])

    def retB(n, cs_):
        kb.op("dve", lambda: V.tensor_tensor(out=Pr[:].rearrange("p h i -> p (h i)"), in0=p_r[:], in1=MASK[:].rearrange("p h i -> p (h i)"),
                                             op=ALU.mult), reads=[p_r, MASK], writes=[Pr])

        def retmm():
            ins = []
            for h in range(4):
                oc = p_yr[:, h * 64:(h + 1) * 64]
                ins.append(PE.matmul(oc, lhsT=Pr[:, h, :], rhs=vtm[:, n * 256 + h * 64:n * 256 + (h + 1) * 64], start=True, stop=False))
                ins.append(PE.matmul(oc, lhsT=qwF[:, h // 2, :], rhs=SRFz[:, h, :], start=False, stop=False))
                ins.append(PE.matmul(oc, lhsT=qwB[:, h // 2, :], rhs=SRBz[:, h, :], start=False, stop=True))
            return ins
        kb.op("pe", retmm, reads=[Pr, vtm, qwF, qwB, SRFz, SRBz], writes=[p_yr])

    def retC(n, cs_):
        r_ = yr[n % 2]
        kb.op("act", lambda: A.copy(out=r_[:], in_=p_yr[:, 0:256]), reads=[p_yr], writes=[r_])
        kb.store(yret, yret[n], r_, r_[:])

    sweep2()


def k2_consts():
    j = np.arange(128)[:, None].astype(np.float32)
    i = np.arange(128)[None, :].astype(np.float32)
    c = np.zeros((128, 10, 128), np.float32)
    c[:, 0] = (j <= i)
    c[:, 1] = (j >= i)
    c[:, 2] = (j > i)
    c[:, 3] = (j < i)
    c[:, 5] = np.maximum(i - j, 0)
    c[:, 6] = np.maximum(j - i, 0)
    c[:, 7] = np.broadcast_to(i + 1, (128, 128))
    c[:, 8] = np.broadcast_to(128 - i, (128, 128))
    c[:, 9] = 1.0
    pc = np.stack([127 - np.arange(128), np.arange(128)], 1).astype(np.float32)
    return c, pc


def tm_chunks(x_tok):
    n, cdim = x_tok.shape
    return c32(x_tok.reshape(n // 128, 128, cdim).transpose(1, 0, 2))


def run_k2(p_lat, p_ctx, prep, decay_logit, d_skip):
    cst, pc = k2_consts()
    maps = []
    for r in range(8):
        b, s = r // 2, r % 2
        qko, xo, dto = prep[r]
        hs = slice(4 * s, 4 * s + 4)
        P = np.concatenate([p_ctx[b], p_lat[b]], 0)
        q = np.concatenate([qko[:, 0].reshape(4, 32, SEQ), qko[:, 1].reshape(4, 32, SEQ)], 1)
        k = np.concatenate([qko[:, 2].reshape(4, 32, SEQ), qko[:, 3].reshape(4, 32, SEQ)], 1)
        qT = q.reshape(2, 128, SEQ).transpose(1, 0, 2)
        kT = k.reshape(2, 128, SEQ).transpose(1, 0, 2)
        ktm = tm_chunks(k.reshape(256, SEQ).T)
        v = P[:, 1024:1536].reshape(SEQ, 8, 64)[:, hs].reshape(SEQ, 256)
        xsT = xo[:, 0:2].transpose(1, 0, 2).reshape(256, SEQ)
        dtm = np.concatenate([dto[:, 0], dto[:, 1]], 0).T
        dl = decay_logit[:, hs]
        dlp = np.zeros((128, 4), np.float32)
        for t in range(2):
            for d in range(2):
                dlp[:64, 2 * t + d] = dl[d, 2 * t]
                dlp[64:, 2 * t + d] = dl[d, 2 * t + 1]
        dlt = np.broadcast_to(dl.reshape(8), (128, 8))
        dsk = np.broadcast_to(d_skip[hs], (128, 4))
        maps.append({"qT": c32(qT), "kT": c32(kT), "ktm": ktm, "vtm": tm_chunks(v), "BT": c32(xo[:, 2]), "CT": c32(xo[:, 3]),
                     "Btm": tm_chunks(xo[:, 2].T), "xtm": tm_chunks(xsT.T), "dtm": tm_chunks(dtm), "cst": cst, "pc": pc,
                     "dlp": dlp, "dlt": c32(dlt), "dsk": c32(dsk)})
    res = launch(build_k2, maps)
    yr = np.zeros((B, SEQ, 512), np.float32)
    ys = np.zeros((B, SEQ, 512), np.float32)
    for r in range(8):
        b, s = r // 2, r % 2
        yr[b, :, 256 * s:256 * s + 256] = res[r]["yret"].reshape(SEQ, 256)
        ys[b, :, 256 * s:256 * s + 256] = res[r]["yssd"].reshape(SEQ, 256)
    return yr, ys


def build_k4(nc, kb):
    xT = kb.dram_in("xT", [128, 8, NTOK])
    mod = kb.dram_in("mod", [128, 96])
    w_in = kb.dram_in("w_in", [1024, IN1])
    w_uq = kb.dram_in("w_uq", [768, 1536])
    w_ukv = kb.dram_in("w_ukv", [256, 2048])
    nw = kb.dram_in("nw", [128, 8])
    out_q = kb.dram_out("qT", [12, 128, NTOK])
    out_kv = kb.dram_out("kvT", [16, 128, NTOK])
    out_pe = kb.dram_out("peT", [32, NTOK])
    fmh = FM(nc, kb)
    modt = kb.sb("modt", [128, 96], F32, dma=True)
    s1p = kb.sb("s1p", [128, 96], F32)
    nwt = kb.sb("nwt", [128, 8], F32, dma=True)
    kb.load(modt, modt[:], mod)
    kb.load(nwt, nwt[:], nw)
    kb.op("dve", lambda: nc.vector.tensor_scalar(out=s1p[:], in0=modt[:], scalar1=1.0, scalar2=None, op0=ALU.add),
          reads=[modt], writes=[s1p])
    wt = kb.sb("wt", [128, 8, 9 * 128], BF16, dma=True)
    kb.op("dve", lambda: nc.vector.memset(wt[:, :, IN1:9 * 128], 0.0), writes=[wt])
    load_weight_bf16(nc, kb, wt, w_in, 8, IN1)
    wq = kb.sb("wq", [128, 6, 1536], BF16, dma=True)
    load_weight_bf16(nc, kb, wq, w_uq, 6, 1536)
    wkv = kb.sb("wkv", [128, 2, 2048], BF16, dma=True)
    load_weight_bf16(nc, kb, wkv, w_ukv, 2, 2048)
    xts = [kb.sb("xt%d" % i, [128, 8, 512], F32, dma=True) for i in range(2)]
    sq = kb.sb("sq", [128, 8, 512], BF16)
    rs = kb.sb("rs", [128, 512], F32)
    tmp = kb.sb("tmp", [128, 8, 512], F32)
    at = kb.sb("at", [128, 8, 512], BF16)
    cq = kb.sb("cq", [128, 9, 512], F32, dma=True)
    cns = [kb.sb("cn%d" % i, [128, 8, 512], BF16) for i in range(2)]
    psn = kb.ps("psn", [128, 512])
    pss = [kb.ps("ps%d" % i, [128, 512]) for i in range(4)]
    ots = [kb.sb("ot%d" % i, [128, 512], F32, dma=True) for i in range(4)]
    cnt = [0]

    def front(gi):
        t0, n, col = GROUPS[gi]
        xt = xts[gi % 2]
        cn = cns[gi % 2]
        th = [lambda: kb.load(xt, xt[:, :, 0:n], xT[:, :, t0:t0 + n], q="pool")]
        th += _norm_mod_thunks(fmh, xt, n, psn, sq, rs, tmp, at, s1p, modt, 1, 0, col)
        for m in range(9):
            def mm(m=m):
                ps = pss[cnt[0] % 4]
                cnt[0] += 1
                kb.op("pe", lambda: [nc.tensor.matmul(ps[:, 0:n], lhsT=wt[:, k, m * 128:(m + 1) * 128], rhs=at[:, k, 0:n],
                                                       start=(k == 0), stop=(k == 7)) for k in range(8)],
                      reads=[wt, at], writes=[ps])
                if m % 2 == 0:
                    kb.op("act", lambda: nc.scalar.copy(out=cq[:, m, 0:n], in_=ps[:, 0:n]), reads=[ps], writes=[cq])
                else:
                    kb.op("dve", lambda: nc.vector.tensor_copy(out=cq[:, m, 0:n], in_=ps[:, 0:n]), reads=[ps], writes=[cq])
            th.append(mm)
        th.append(lambda: kb.store(out_pe, out_pe[:, t0:t0 + n], cq, cq[0:32, 8, 0:n]))
        for (k0, kc, nf) in ((0, 6, 768), (6, 2, 256)):
            th.append(lambda k0=k0, kc=kc, nf=nf: fmh.rstd_bc(cq, kc, n, psn, sq, rs, nf, k0=k0))
            for k in range(k0, k0 + kc):
                th.append(lambda k=k: kb.op("dve", lambda: nc.vector.scalar_tensor_tensor(
                    out=cn[:, k, 0:n], in0=cq[:, k, 0:n], scalar=nwt[:, k:k + 1], in1=rs[:, 0:n], op0=ALU.mult, op1=ALU.mult),
                    reads=[cq, rs, nwt], writes=[cn]))
        return th

    for t in front(0):
        t()
    for gi, (t0, n, col) in enumerate(GROUPS):
        cn = cns[gi % 2]
        nxt = front(gi + 1) if gi + 1 < len(GROUPS) else []
        for m in range(12 + 16):
            ps = pss[cnt[0] % 4]
            ot = ots[cnt[0] % 4]
            cnt[0] += 1
            if m < 12:
                kb.op("pe", lambda: [nc.tensor.matmul(ps[:, 0:n], lhsT=wq[:, k, m * 128:(m + 1) * 128], rhs=cn[:, k, 0:n],
                                                       start=(k == 0), stop=(k == 5)) for k in range(6)],
                      reads=[wq, cn], writes=[ps])
            else:
                mm_ = m - 12
                kb.op("pe", lambda: [nc.tensor.matmul(ps[:, 0:n], lhsT=wkv[:, k, mm_ * 128:(mm_ + 1) * 128], rhs=cn[:, 6 + k, 0:n],
                                                       start=(k == 0), stop=(k == 1)) for k in range(2)],
                      reads=[wkv, cn], writes=[ps])
            if m % 2 == 0:
                kb.op("act", lambda: nc.scalar.copy(out=ot[:, 0:n], in_=ps[:, 0:n]), reads=[ps], writes=[ot])
            else:
                kb.op("dve", lambda: nc.vector.tensor_copy(out=ot[:, 0:n], in_=ps[:, 0:n]), reads=[ps], writes=[ot])
            if m < 12:
                kb.store(out_q, out_q[m, :, t0:t0 + n], ot, ot[:, 0:n])
            else:
                kb.store(out_kv, out_kv[m - 12, :, t0:t0 + n], ot, ot[:, 0:n])
            for _ in range(2):
                if nxt:
                    nxt.pop(0)()
        for t in nxt:
            t()


def run_k4(h_lat, h_ctx, m1, inp):
    maps = []
    nw = c32(np.concatenate([vec_fm(inp["mla_q_norm_w"][0]), vec_fm(inp["mla_kv_norm_w"][0])], 1))
    for r in range(8):
        b, s = r // 2, r % 2
        maps.append({"xT": fm(core_rows(h_lat[b], h_ctx[b], s)), "mod": mod_pack(m1, b), "w_in": c32(inp["mla_w_in"][0]),
                     "w_uq": c32(inp["mla_w_uq"][0]), "w_ukv": c32(inp["mla_w_ukv"][0]), "nw": nw})
    res = launch(build_k4, maps)
    q = np.zeros((B, SEQ, 1536), np.float32)
    kv = np.zeros((B, SEQ, 2048), np.float32)
    pe = np.zeros((B, SEQ, 32), np.float32)
    for r in range(8):
        b, s = r // 2, r % 2
        for arr, name, f in ((q, "qT", 1536), (kv, "kvT", 2048), (pe, "peT", 32)):
            t = res[r][name].reshape(f, NTOK).T
            arr[b, CTX + s * 2048:CTX + (s + 1) * 2048] = t[:2048]
            arr[b, s * 128:(s + 1) * 128] = t[2048:]
    return q, kv, pe


def build_rope(nc, kb):
    qk = kb.dram_in("qk", [NBLK, 128, 4, 256])
    cs = kb.dram_in("cs", [NBLK, 128, 2, 256])
    qko = kb.dram_out("qko", [NBLK, 128, 4, 256])
    qkt = [kb.sb("qkt%d" % i, [128, 4, 256], F32, dma=True) for i in range(2)]
    cst = [kb.sb("cst%d" % i, [128, 2, 256], F32, dma=True) for i in range(2)]
    qo = [kb.sb("qo%d" % i, [128, 4, 256], F32, dma=True) for i in range(2)]
    ta = kb.sb("ta", [128, 256], F32)
    tb = kb.sb("tb", [128, 256], F32)
    V = nc.vector
    for blk in range(NBLK):
        i2 = blk % 2
        a, c_, o_ = qkt[i2], cst[i2], qo[i2]
        kb.load(a, a[:], qk[blk], q="pool")
        kb.load(c_, c_[:], cs[blk], q="pool")
        for pair in range(2):
            x1 = a[:, 2 * pair, :]
            x2 = a[:, 2 * pair + 1, :]
            cos, sin = c_[:, 0, :], c_[:, 1, :]
            kb.op("dve", lambda: V.tensor_tensor(out=ta[:], in0=x1, in1=cos, op=ALU.mult), reads=[a, c_], writes=[ta])
            kb.op("dve", lambda: V.tensor_tensor(out=tb[:], in0=x2, in1=sin, op=ALU.mult), reads=[a, c_], writes=[tb])
            kb.op("dve", lambda: V.tensor_tensor(out=o_[:, 2 * pair, :], in0=ta[:], in1=tb[:], op=ALU.subtract), reads=[ta, tb], writes=[o_])
            kb.op("dve", lambda: V.tensor_tensor(out=ta[:], in0=x1, in1=sin, op=ALU.mult), reads=[a, c_], writes=[ta])
            kb.op("dve", lambda: V.tensor_tensor(out=tb[:], in0=x2, in1=cos, op=ALU.mult), reads=[a, c_], writes=[tb])
            kb.op("dve", lambda: V.tensor_tensor(out=o_[:, 2 * pair + 1, :], in0=ta[:], in1=tb[:], op=ALU.add), reads=[ta, tb], writes=[o_])
        kb.store(qko, qko[blk], o_, o_[:])


def run_rope_mla(q, pe):
    cos, sin = rope_tables(32)
    maps = []
    for r in range(8):
        b, s = r // 2, r % 2
        qpe = q[b].reshape(SEQ, 16, 96)[:, 8 * s:8 * s + 8, 64:96]
        q1 = qpe[:, :, :16].reshape(SEQ, 128).T
        q2 = qpe[:, :, 16:].reshape(SEQ, 128).T
        k1 = np.tile(pe[b][:, :16], (1, 8)).T
        k2 = np.tile(pe[b][:, 16:], (1, 8)).T
        qk = np.stack([blocks(a) for a in (q1, q2, k1, k2)], axis=2)
        ct = np.tile(cos, (1, 8)).T
        st = np.tile(sin, (1, 8)).T
        cs = np.stack([blocks(ct), blocks(st)], axis=2)
        maps.append({"qk": c32(qk), "cs": c32(cs)})
    res = launch(build_rope, maps)
    q_pe = np.zeros((B, SEQ, 16, 32), np.float32)
    k_pe = np.zeros((B, SEQ, 32), np.float32)
    for r in range(8):
        b, s = r // 2, r % 2
        o = res[r]["qko"].transpose(1, 2, 0, 3).reshape(128, 4, SEQ)
        q_pe[b, :, 8 * s:8 * s + 8, :16] = o[:, 0].reshape(8, 16, SEQ).transpose(2, 0, 1)
        q_pe[b, :, 8 * s:8 * s + 8, 16:] = o[:, 1].reshape(8, 16, SEQ).transpose(2, 0, 1)
        if s == 0:
            k_pe[b, :, :16] = o[0:16, 2].T
            k_pe[b, :, 16:] = o[0:16, 3].T
    return q_pe, k_pe


NKT = SEQ // 128
NQG = L // 512
ATT_SCALE = 96.0 ** -0.5


def build_k5(nc, kb):
    V, A, PE = nc.vector, nc.scalar, nc.tensor
    Qd = kb.dram_in("Q", [8, 128, L])
    Kd = kb.dram_in("K", [8, 128, SEQ])
    Vd = kb.dram_in("V", [8, 128, NKT, 128])
    out = kb.dram_out("O", [8, 64, L])
    qf = kb.sb("qf", [128, L], F32, dma=True)
    qb = [kb.sb("qb%d" % i, [128, L], BF16) for i in range(2)]
    kbt = [kb.sb("kb%d" % i, [128, SEQ], BF16, dma=True) for i in range(2)]
    vbt = [kb.sb("vb%d" % i, [128, NKT, 128], BF16, dma=True) for i in range(2)]
    pst = [kb.ps("pst%d" % i, [128, 3, 512]) for i in range(2)]
    pso = [kb.ps("pso%d" % i, [128, 512]) for i in range(2)]
    pts = [kb.sb("pt%d" % i, [128, 3, 512], BF16) for i in range(2)]
    rec = kb.sb("rec", [64, 512], F32)
    oss = [kb.sb("os%d" % i, [64, 512], F32, dma=True) for i in range(2)]
    KG = [(g0, min(3, NKT - g0)) for g0 in range(0, NKT, 3)]
    NG = len(KG)

    def prep_head(h):
        kt_, vt_, qb_ = kbt[h % 2], vbt[h % 2], qb[h % 2]
        kb.load(qf, qf[:], Qd[h])
        kb.op("pool", lambda: [nc.gpsimd.dma_start(out=kt_[:, c:min(SEQ, c + 2048)], in_=Kd[h, :, c:min(SEQ, c + 2048)])
                               for c in range(0, SEQ, 2048)], writes=[kt_], dma=kt_)
        kb.op("pool", lambda: [nc.gpsimd.dma_start(out=vt_[:, t0:min(NKT, t0 + 16), :], in_=Vd[h, :, t0:min(NKT, t0 + 16), :])
                               for t0 in range(0, NKT, 16)], writes=[vt_], dma=vt_)
        kb.op("dve", lambda: V.tensor_scalar(out=qb_[:], in0=qf[:], scalar1=ATT_SCALE, scalar2=None, op0=ALU.mult), reads=[qf], writes=[qb_])

    it = 0
    og = 0
    prep_head(0)
    for h in range(8):
        kt_, vt_, qb_ = kbt[h % 2], vbt[h % 2], qb[h % 2]
        if h + 1 < 8:
            prep_head(h + 1)
        for qg in range(NQG):
            qs = slice(qg * 512, (qg + 1) * 512)
            po = pso[og % 2]
            osb = oss[og % 2]
            og += 1

            def smm(g, ps):
                g0, gc = KG[g]
                kb.op("pe", lambda: [PE.matmul(ps[:, j, :], lhsT=kt_[:, (g0 + j) * 128:(g0 + j + 1) * 128], rhs=qb_[:, qs],
                                               start=True, stop=True) for j in range(gc)], reads=[kt_, qb_], writes=[ps])
            smm(0, pst[it % 2])
            for g in range(NG):
                g0, gc = KG[g]
                ps = pst[it % 2]
                pt = pts[it % 2]
                it += 1
                if g + 1 < NG:
                    smm(g + 1, pst[it % 2])
                kb.op("act", lambda: A.activation(out=pt[:, 0:gc, :], in_=ps[:, 0:gc, :], func=AF.Exp), reads=[ps], writes=[pt])
                kb.op("pe", lambda: [PE.matmul(po[:], lhsT=vt_[:, g0 + j, :], rhs=pt[:, j, :], start=(g == 0 and j == 0),
                                               stop=(g == NG - 1 and j == gc - 1)) for j in range(gc)], reads=[vt_, pt], writes=[po])
            kb.op("dve", lambda: V.reciprocal(out=rec[:], in_=po[64:128, :]), reads=[po], writes=[rec])
            kb.op("dve", lambda: V.tensor_tensor(out=osb[:], in0=po[0:64, :], in1=rec[:], op=ALU.mult), reads=[po, rec], writes=[osb])
            kb.store(out, out[h, :, qs], osb, osb[:])


def run_k5(q, kv, q_pe, k_pe):
    maps = []
    for r in range(8):
        b, s = r // 2, r % 2
        hs = slice(8 * s, 8 * s + 8)
        qn = q[b].reshape(SEQ, 16, 96)[CTX:, hs, 0:64]
        Q = np.zeros((8, 128, L), np.float32)
        Q[:, 0:32] = q_pe[b, CTX:, hs].transpose(1, 2, 0)
        Q[:, 32:96] = qn.transpose(1, 2, 0)
        kvh = kv[b].reshape(SEQ, 16, 128)[:, hs]
        K = np.zeros((8, 128, SEQ), np.float32)
        K[:, 0:32] = k_pe[b].T[None]
        K[:, 32:96] = kvh[:, :, 0:64].transpose(1, 2, 0)
        Vv = np.ones((8, 128, NKT, 128), np.float32)
        Vv[:, :, :, 0:64] = kvh[:, :, 64:128].reshape(NKT, 128, 8, 64).transpose(2, 1, 0, 3)
        maps.append({"Q": Q, "K": K, "V": Vv})
    res = launch(build_k5, maps)
    o = np.zeros((B, L, 1024), np.float32)
    for r in range(8):
        b, s = r // 2, r % 2
        o[b, :, 512 * s:512 * s + 512] = res[r]["O"].transpose(2, 0, 1).reshape(L, 512)
    return o


def core_rows(lat_b, ctx_b, s):
    return np.concatenate([lat_b[s * 2048:(s + 1) * 2048], ctx_b[s * 128:(s + 1) * 128]], axis=0)


def run_k3a(h_lat, h_ctx, m_l, w_out, finish=None, mix_lat=None):
    maps = []
    bo = np.zeros((128, 128), np.float32)
    bo[:64, :64] = 1.0 / 64
    bo[64:, 64:] = 1.0 / 64
    for r in range(8):
        b, s = r // 2, r % 2
        d = {"hT": fm(core_rows(h_lat[b], h_ctx[b], s)), "mod": mod_pack(m_l, b), "w": c32(w_out)}
        if finish is not None:
            f = finish
            d["yr"] = fm(core_rows(f["yr"][b, CTX:], f["yr"][b, :CTX], s))
            d["ys"] = fm(core_rows(f["ys"][b, CTX:], f["ys"][b, :CTX], s))
            d["gT"] = fm(core_rows(f["p_lat"][b][:, 1536:2048], f["p_ctx"][b][:, 1536:2048], s))
            d["zT"] = fm(core_rows(f["p_lat"][b][:, 2048:2560], f["p_ctx"][b][:, 2048:2560], s))
            d["nw"] = c32(np.concatenate([vec_fm(f["gn_w"]), vec_fm(f["ssd_norm_w"])], 1))
            d["bo"] = bo
        else:
            d["mix"] = fm(core_rows(mix_lat[b], np.zeros((CTX, 1024), np.float32), s))
        maps.append(d)
    res = launch(make_k3a(finish is not None), maps)
    return [res[r]["h1T"] for r in range(8)], [res[r]["a2T"] for r in range(8)]


def gather_tokens(oT_list):
    h_lat = np.zeros((B, L, D), np.float32)
    h_ctx = np.zeros((B, CTX, D), np.float32)
    for r in range(8):
        b, s = r // 2, r % 2
        t = unfm(oT_list[r])
        h_lat[b, s * 2048:(s + 1) * 2048] = t[:2048]
        h_ctx[b, s * 128:(s + 1) * 128] = t[2048:]
    return h_lat, h_ctx


def layer0(h_lat, h_ctx, m0, inp, m1=None):
    p_lat, p_ctx = run_k1(h_lat, h_ctx, m0, inp["ret_ssd_w_in"][0])
    prep = run_k1b(p_lat, p_ctx, inp["ssd_conv_w"][0], inp["ssd_conv_b"][0], inp["ssd_dt_bias"][0], inp["ssd_a_log"][0])
    yr, ys = run_k2(p_lat, p_ctx, prep, inp["ret_decay_logit"][0], inp["ssd_d"][0])
    fin = dict(yr=yr, ys=ys, p_lat=p_lat, p_ctx=p_ctx, gn_w=inp["ret_gn_w"][0], ssd_norm_w=inp["ssd_norm_w"][0])
    if m1 is not None:
        return run_l0tail_k4(h_lat, h_ctx, m0, m1, inp, fin)
    o = run_k3ab(h_lat, h_ctx, m0, inp["ret_ssd_w_out"][0], inp["w_ffn_gate"][0], inp["w_ffn_up"][0], inp["w_ffn_down"][0], finish=fin)
    return gather_tokens(o)


def layer1(h_lat, h_ctx, m1, inp, front=None):
    q, kv, pe = front if front is not None else run_k4(h_lat, h_ctx, m1, inp)
    q_pe, k_pe = run_rope_mla(q, pe)
    o = run_k5(q, kv, q_pe, k_pe)
    oT = run_k3ab(h_lat, h_ctx, m1, inp["mla_w_out"][0], inp["w_ffn_gate"][1], inp["w_ffn_up"][1], inp["w_ffn_down"][1],
                  mix_lat=o, final_w=inp["final_norm_w"])
    out_lat, _ = gather_tokens(oT)
    return out_lat


def kernel(**inputs):
    inp = {k: np.asarray(v) for k, v in inputs.items()}
    m = run_k0(inp["c"], inp["c_ctx"], inp["w_ada"], inp["b_ada"])
    h_lat, h_ctx, front = layer0(inp["x"], inp["ctx"], m[0], inp, m1=m[1])
    out = layer1(h_lat, h_ctx, m[1], inp, front=front)
    return np.ascontiguousarray(out, dtype=np.float32)
```
